# Optimizing a Trainium2 kernel written in Bass

```python
import jax, jax.numpy as jnp
from jax import lax
import numpy as np

D_MODEL = 1024
BATCH = 4
SEQ = 8192
DEPTH = 2

GRID_W = 64
CTX_LEN = 256

GROUP_W = D_MODEL // 4
MIX_W = 4 * GROUP_W

MLA_HEADS = 4
MLA_NOPE = 64
MLA_ROPE = 32
MLA_V = GROUP_W // MLA_HEADS
MLA_Q_LORA = 256
MLA_KV_LORA = 128
MLA_SCALE = (MLA_NOPE + MLA_ROPE) ** -0.5
ROPE_BASE = 10000.0

FOURIER_GROUPS = 4

POOL_WINDOWS = (2, 4, 8, 16)

NA_HEADS = 4
NA_HEAD_DIM = GROUP_W // NA_HEADS
NA_SCALE = NA_HEAD_DIM ** -0.5
NA_WIN_ROWS = 8
NA_WIN_COLS = 16
NA_QCOLS = 16
NA_KCOLS = NA_QCOLS + NA_WIN_COLS

Q_BLOCK = 128
NEG_INF = -1e30

IN_W = MLA_Q_LORA + MLA_KV_LORA + MLA_ROPE + GROUP_W + GROUP_W + 3 * GROUP_W
IN_SPLITS = (MLA_Q_LORA,
             MLA_Q_LORA + MLA_KV_LORA,
             MLA_Q_LORA + MLA_KV_LORA + MLA_ROPE,
             MLA_Q_LORA + MLA_KV_LORA + MLA_ROPE + GROUP_W,
             MLA_Q_LORA + MLA_KV_LORA + MLA_ROPE + 2 * GROUP_W,
             MLA_Q_LORA + MLA_KV_LORA + MLA_ROPE + 3 * GROUP_W,
             MLA_Q_LORA + MLA_KV_LORA + MLA_ROPE + 4 * GROUP_W)

D_FF = 2816
N_EXPERTS = 8
TOP_K = 2
D_FF_EXPERT = 3584
N_DENSE = (DEPTH + 1) // 2
N_MOE = DEPTH // 2

kernel_name = 'hybrid_mla_fourier_pool_natten_moe_diffusion_block'


def _rmsnorm(x, g, eps=1e-6):
    xf = x.astype(jnp.float32)
    y = xf * lax.rsqrt(jnp.mean(xf * xf, axis=-1, keepdims=True) + eps)
    return (y * g.astype(jnp.float32)).astype(x.dtype)


def _modulate(h, shift, scale):
    return h * (1 + scale) + shift


def _axial_rope_tables(n_tokens):
    t = jnp.arange(n_tokens)
    row = (t // GRID_W).astype(jnp.float32)
    col = (t % GRID_W).astype(jnp.float32)
    half = MLA_ROPE // 2
    inv = 1.0 / (ROPE_BASE ** (jnp.arange(0, half, 2, dtype=jnp.float32) / half))
    ang_r = row[:, None] * inv
    ang_c = col[:, None] * inv
    return (jnp.cos(ang_r), jnp.sin(ang_r), jnp.cos(ang_c), jnp.sin(ang_c))


def _rotate(x, cos, sin):
    x1, x2 = jnp.split(x, 2, axis=-1)
    return jnp.concatenate([x1 * cos - x2 * sin, x1 * sin + x2 * cos], axis=-1)


def _apply_axial_rope(x, tabs):
    cr, sr, cc, sc = tabs
    extra = x.ndim - 3
    shp = lambda a: a.reshape(a.shape[0], *([1] * extra), a.shape[1]).astype(x.dtype)
    half = MLA_ROPE // 2
    return jnp.concatenate([_rotate(x[..., :half], shp(cr), shp(sr)),
                            _rotate(x[..., half:], shp(cc), shp(sc))], axis=-1)


def _mla_queries(u_cq, g_cq, w_uq, g_qn, g_qr):
    cq = _rmsnorm(u_cq, g_cq)
    q = (cq @ w_uq).reshape(*u_cq.shape[:-1], MLA_HEADS, MLA_NOPE + MLA_ROPE)
    return _rmsnorm(q[..., :MLA_NOPE], g_qn), _rmsnorm(q[..., MLA_NOPE:], g_qr)


def _mla_keys_values(u_ckv, u_kr, g_ckv, w_ukv, g_kn, g_kr):
    ckv = _rmsnorm(u_ckv, g_ckv)
    kv = (ckv @ w_ukv).reshape(*u_ckv.shape[:-1], MLA_HEADS, MLA_NOPE + MLA_V)
    return _rmsnorm(kv[..., :MLA_NOPE], g_kn), _rmsnorm(u_kr, g_kr), kv[..., MLA_NOPE:]


def _mla_scores(qn, qr, kn, kr):
    s = jnp.einsum('bqhd,bkhd->bhqk', qn, kn) + jnp.einsum('bqhr,bkr->bhqk', qr, kr)
    return s.astype(jnp.float32) * MLA_SCALE


def _mla_latent(qn, qr, kn, kr, v, kn_c, kr_c, v_c):
    B, L = qn.shape[:2]
    n_lat = kn.shape[1]

    def block(i):
        s0 = i * Q_BLOCK
        qn_b = lax.dynamic_slice_in_dim(qn, s0, Q_BLOCK, axis=1)
        qr_b = lax.dynamic_slice_in_dim(qr, s0, Q_BLOCK, axis=1)
        scores = jnp.concatenate([_mla_scores(qn_b, qr_b, kn, kr),
                                  _mla_scores(qn_b, qr_b, kn_c, kr_c)], axis=-1)
        p = jax.nn.softmax(scores, axis=-1).astype(v.dtype)
        return (jnp.einsum('bhqk,bkhd->bqhd', p[..., :n_lat], v)
                + jnp.einsum('bhqk,bkhd->bqhd', p[..., n_lat:], v_c))

    out = lax.map(block, jnp.arange(L // Q_BLOCK))
    return out.transpose(1, 0, 2, 3, 4).reshape(B, L, MLA_HEADS * MLA_V)


def _mla_context(qn_c, qr_c, kn_c, kr_c, v_c):
    B, L = qn_c.shape[:2]
    p = jax.nn.softmax(_mla_scores(qn_c, qr_c, kn_c, kr_c), axis=-1).astype(v_c.dtype)
    return jnp.einsum('bhqk,bkhd->bqhd', p, v_c).reshape(B, L, MLA_HEADS * MLA_V)


def _fourier_mix(u, w_f):
    B, L, C = u.shape
    ug = u.astype(jnp.float32).reshape(B, L, FOURIER_GROUPS, C // FOURIER_GROUPS)
    f = jnp.fft.fftn(ug, axes=(1, 3), norm='ortho').real.reshape(B, L, C)
    return f.astype(u.dtype) @ w_f


def _multiscale_pool(u, w_pool, p_scale):
    B, L, C = u.shape
    G = len(POOL_WINDOWS)
    cg = C // G
    ug = u.reshape(B, L, G, cg)
    csum = jnp.concatenate([jnp.zeros((B, 1, G, cg), jnp.float32),
                            jnp.cumsum(ug.astype(jnp.float32), axis=1)], axis=1)
    w = jnp.array(POOL_WINDOWS, dtype=jnp.int32)
    t = jnp.arange(L, dtype=jnp.int32)[:, None]
    lo = jnp.clip(t - w // 2, 0, L)
    hi = jnp.clip(t + w - w // 2, 0, L)
    gidx = jnp.arange(G)[None, :]
    win_sum = csum[:, hi, gidx] - csum[:, lo, gidx]
    count = (hi - lo).astype(jnp.float32)[None, :, :, None]
    pooled = (win_sum / count).astype(u.dtype) - ug
    y = jnp.einsum('blgc,gcd->blgd', pooled, w_pool).reshape(B, L, C)
    return y * p_scale


def _heads(u):
    return u.reshape(*u.shape[:-1], NA_HEADS, NA_HEAD_DIM)


def _na_latent(q, k, v, k_c, v_c, rpb):
    B, L = q.shape[:2]
    rows = L // GRID_W
    wr = min(NA_WIN_ROWS, rows)
    ncb = GRID_W // NA_QCOLS
    qg = q.reshape(B, rows, GRID_W, NA_HEADS, NA_HEAD_DIM)
    kg = k.reshape(B, rows, GRID_W, NA_HEADS, NA_HEAD_DIM)
    vg = v.reshape(B, rows, GRID_W, NA_HEADS, NA_HEAD_DIM)
    qcol = jnp.arange(GRID_W).reshape(ncb, NA_QCOLS)
    win_c0 = jnp.clip(qcol - NA_WIN_COLS // 2, 0, GRID_W - NA_WIN_COLS)
    band_c0 = jnp.clip(qcol[:, 0] - NA_WIN_COLS // 2, 0, GRID_W - NA_KCOLS)
    kcol = band_c0[:, None] + jnp.arange(NA_KCOLS)
    col_ok = ((kcol[:, None, :] >= win_c0[..., None])
              & (kcol[:, None, :] < win_c0[..., None] + NA_WIN_COLS))
    col_off = jnp.clip(kcol[:, None, :] - qcol[..., None] + NA_WIN_COLS - 1,
                       0, 2 * NA_WIN_COLS - 2)
    n_loc = wr * NA_KCOLS

    def row_step(r):
        r0 = jnp.clip(r - wr // 2, 0, rows - wr)
        q_blk = lax.dynamic_index_in_dim(qg, r, axis=1, keepdims=False)
        q_blk = q_blk.reshape(B, ncb, NA_QCOLS, NA_HEADS, NA_HEAD_DIM)
        k_blk = lax.dynamic_slice_in_dim(kg, r0, wr, axis=1)[:, :, kcol]
        v_blk = lax.dynamic_slice_in_dim(vg, r0, wr, axis=1)[:, :, kcol]
        row_off = r0 + jnp.arange(wr) - r + NA_WIN_ROWS - 1
        bias = rpb[:, row_off[None, None, :, None], col_off[:, :, None, :]]
        s_loc = (jnp.einsum('bnqhd,brnkhd->bhnqrk', q_blk, k_blk).astype(jnp.float32) * NA_SCALE
                 + bias.astype(jnp.float32))
        s_loc = jnp.where(col_ok[:, :, None, :], s_loc, NEG_INF)
        s_loc = s_loc.reshape(B, NA_HEADS, ncb, NA_QCOLS, n_loc)
        s_ctx = jnp.einsum('bnqhd,bchd->bhnqc', q_blk, k_c).astype(jnp.float32) * NA_SCALE
        p = jax.nn.softmax(jnp.concatenate([s_loc, s_ctx], axis=-1), axis=-1).astype(v.dtype)
        p_loc = p[..., :n_loc].reshape(B, NA_HEADS, ncb, NA_QCOLS, wr, NA_KCOLS)
        o = (jnp.einsum('bhnqrk,brnkhd->bnqhd', p_loc, v_blk)
             + jnp.einsum('bhnqc,bchd->bnqhd', p[..., n_loc:], v_c))
        return o.reshape(B, GRID_W, NA_HEADS * NA_HEAD_DIM)

    out = lax.map(row_step, jnp.arange(rows))
    return out.transpose(1, 0, 2, 3).reshape(B, L, NA_HEADS * NA_HEAD_DIM)


def _dense_attn(q, k, v):
    B, L = q.shape[:2]
    s = jnp.einsum('bqhd,bkhd->bhqk', q, k).astype(jnp.float32) * NA_SCALE
    p = jax.nn.softmax(s, axis=-1).astype(v.dtype)
    return jnp.einsum('bhqk,bkhd->bqhd', p, v).reshape(B, L, NA_HEADS * NA_HEAD_DIM)


def _swiglu(h, w1, w3, w2):
    return (jax.nn.silu(h @ w1) * (h @ w3)) @ w2


def _moe(h, w_router, w1, w3, w2):
    logits = (h @ w_router).astype(jnp.float32)
    top_v, top_i = lax.top_k(logits, TOP_K)
    top_g = jax.nn.softmax(top_v, axis=-1)
    gate = jnp.sum(top_g[..., None] * jax.nn.one_hot(top_i, N_EXPERTS, dtype=jnp.float32),
                   axis=-2).astype(h.dtype)
    y = jnp.zeros_like(h)
    for e in range(N_EXPERTS):
        y = y + gate[..., e:e + 1] * _swiglu(h, w1[e], w3[e], w2[e])
    return y


def setup_inputs(seed: int = 0) -> dict:
    key = jax.random.key(seed)
    ks = iter(jax.random.split(key, 48))
    f32 = jnp.float32

    def nrm(shape, scale):
        return jax.random.normal(next(ks), shape, f32) * scale

    def gain(shape):
        return 1.0 + 0.1 * jax.random.normal(next(ks), shape, f32)

    D = D_MODEL
    cg = GROUP_W // len(POOL_WINDOWS)
    return {
        'x': nrm((BATCH, SEQ, D), 1.0),
        'c': nrm((BATCH, D), 1.0),
        'ctx': nrm((BATCH, CTX_LEN, D), 1.0),
        'c_ctx': nrm((D,), 1.0),
        'w_ada': nrm((DEPTH, D, 6 * D), D ** -0.5),
        'b_ada': nrm((DEPTH, 6 * D), 0.02),
        'g_mix': gain((DEPTH, D)),
        'g_ffn': gain((DEPTH, D)),
        'w_in': nrm((DEPTH, D, IN_W), D ** -0.5),
        'w_out': nrm((DEPTH, MIX_W, D), MIX_W ** -0.5),
        'g_cq': gain((DEPTH, MLA_Q_LORA)),
        'g_ckv': gain((DEPTH, MLA_KV_LORA)),
        'w_uq': nrm((DEPTH, MLA_Q_LORA, MLA_HEADS * (MLA_NOPE + MLA_ROPE)), MLA_Q_LORA ** -0.5),
        'w_ukv': nrm((DEPTH, MLA_KV_LORA, MLA_HEADS * (MLA_NOPE + MLA_V)), MLA_KV_LORA ** -0.5),
        'g_mla_qn': gain((DEPTH, MLA_NOPE)),
        'g_mla_qr': gain((DEPTH, MLA_ROPE)),
        'g_mla_kn': gain((DEPTH, MLA_NOPE)),
        'g_mla_kr': gain((DEPTH, MLA_ROPE)),
        'w_fourier': nrm((DEPTH, GROUP_W, GROUP_W), GROUP_W ** -0.5),
        'w_pool': nrm((DEPTH, len(POOL_WINDOWS), cg, cg), cg ** -0.5),
        'pool_scale': gain((DEPTH, GROUP_W)),
        'g_na_q': gain((DEPTH, NA_HEAD_DIM)),
        'g_na_k': gain((DEPTH, NA_HEAD_DIM)),
        'na_rpb': nrm((DEPTH, NA_HEADS, 2 * NA_WIN_ROWS - 1, 2 * NA_WIN_COLS - 1), 0.1),
        'w1_dense': nrm((N_DENSE, D, D_FF), D ** -0.5),
        'w3_dense': nrm((N_DENSE, D, D_FF), D ** -0.5),
        'w2_dense': nrm((N_DENSE, D_FF, D), D_FF ** -0.5),
        'w_router': nrm((N_MOE, D, N_EXPERTS), D ** -0.5),
        'w1_moe': nrm((N_MOE, N_EXPERTS, D, D_FF_EXPERT), D ** -0.5),
        'w3_moe': nrm((N_MOE, N_EXPERTS, D, D_FF_EXPERT), D ** -0.5),
        'w2_moe': nrm((N_MOE, N_EXPERTS, D_FF_EXPERT, D), D_FF_EXPERT ** -0.5),
    }


def reference(x, c, ctx, c_ctx, w_ada, b_ada, g_mix, g_ffn, w_in, w_out,
              g_cq, g_ckv, w_uq, w_ukv, g_mla_qn, g_mla_qr, g_mla_kn, g_mla_kr,
              w_fourier, w_pool, pool_scale, g_na_q, g_na_k, na_rpb,
              w1_dense, w3_dense, w2_dense, w_router, w1_moe, w3_moe, w2_moe):
    L = x.shape[1]
    tabs = _axial_rope_tables(L)
    silu_c = jax.nn.silu(c)
    silu_cc = jax.nn.silu(c_ctx)
    for l in range(DEPTH):
        last = l == DEPTH - 1
        mx = (silu_c @ w_ada[l] + b_ada[l])[:, None, :]
        mc = silu_cc @ w_ada[l] + b_ada[l]
        sh1, sc1, ga1, sh2, sc2, ga2 = jnp.split(mx, 6, axis=-1)
        csh1, csc1, cga1, csh2, csc2, cga2 = jnp.split(mc, 6, axis=-1)

        ux = _modulate(_rmsnorm(x, g_mix[l]), sh1, sc1) @ w_in[l]
        uc = _modulate(_rmsnorm(ctx, g_mix[l]), csh1, csc1) @ w_in[l]
        x_cq, x_ckv, x_kr, x_f, x_p, x_nq, x_nk, x_nv = jnp.split(ux, IN_SPLITS, axis=-1)
        c_cq, c_ckv, c_kr, c_f, c_p, c_nq, c_nk, c_nv = jnp.split(uc, IN_SPLITS, axis=-1)

        kn_c, kr_c, v_c = _mla_keys_values(c_ckv, c_kr, g_ckv[l], w_ukv[l], g_mla_kn[l], g_mla_kr[l])
        nk_c = _rmsnorm(_heads(c_nk), g_na_k[l])
        nv_c = _heads(c_nv)

        qn, qr = _mla_queries(x_cq, g_cq[l], w_uq[l], g_mla_qn[l], g_mla_qr[l])
        qr = _apply_axial_rope(qr, tabs)
        kn, kr, v = _mla_keys_values(x_ckv, x_kr, g_ckv[l], w_ukv[l], g_mla_kn[l], g_mla_kr[l])
        kr = _apply_axial_rope(kr, tabs)
        nq = _rmsnorm(_heads(x_nq), g_na_q[l])
        nk = _rmsnorm(_heads(x_nk), g_na_k[l])
        o_x = jnp.concatenate([
            _mla_latent(qn, qr, kn, kr, v, kn_c, kr_c, v_c),
            _fourier_mix(x_f, w_fourier[l]),
            _multiscale_pool(x_p, w_pool[l], pool_scale[l]),
            _na_latent(nq, nk, _heads(x_nv), nk_c, nv_c, na_rpb[l]),
        ], axis=-1)
        x = x + ga1 * (o_x @ w_out[l])

        if not last:
            qn_c, qr_c = _mla_queries(c_cq, g_cq[l], w_uq[l], g_mla_qn[l], g_mla_qr[l])
            o_c = jnp.concatenate([
                _mla_context(qn_c, qr_c, kn_c, kr_c, v_c),
                _fourier_mix(c_f, w_fourier[l]),
                _multiscale_pool(c_p, w_pool[l], pool_scale[l]),
                _dense_attn(_rmsnorm(_heads(c_nq), g_na_q[l]), nk_c, nv_c),
            ], axis=-1)
            ctx = ctx + cga1 * (o_c @ w_out[l])

        i = l // 2
        if l % 2 == 0:
            ffn = lambda h, i=i: _swiglu(h, w1_dense[i], w3_dense[i], w2_dense[i])
        else:
            ffn = lambda h, i=i: _moe(h, w_router[i], w1_moe[i], w3_moe[i], w2_moe[i])
        x = x + ga2 * ffn(_modulate(_rmsnorm(x, g_ffn[l]), sh2, sc2))
        if not last:
            ctx = ctx + cga2 * ffn(_modulate(_rmsnorm(ctx, g_ffn[l]), csh2, csc2))
    return x
```

```python
import numpy as np
import ml_dtypes
from contextlib import ExitStack
import concourse.bass as bass
import concourse.mybir as mybir
from concourse.bass_utils import run_bass_kernel_spmd

F32 = mybir.dt.float32
BF16 = mybir.dt.bfloat16
AF = mybir.ActivationFunctionType
ALU = mybir.AluOpType

D = 1024
SEQ = 8192
CTX = 256
TT = SEQ + CTX
DEPTH = 2
GRID_W = 64
IN_W = 1696
D_FF = 2816
NE = 8
D_FFE = 3584
MLA_SCALE = 96 ** -0.5
NA_SCALE = 64 ** -0.5
EPS = 1e-6
NEG = -30000.0
BLOCKS = [(i * 512, 512, False) for i in range(16)] + [(SEQ, 256, True)]
NCORES = 8
HALF = SEQ // 2


class T:
    __slots__ = ("ap", "lw", "rd", "name")

    def __init__(self, ap, name=""):
        self.ap = ap
        self.lw = {}
        self.rd = {}
        self.name = name

    def __getitem__(self, idx):
        return self.ap[idx]


class _Rec:
    def __getattr__(self, name):
        def f(*a, **k):
            self.call = (name, a, k)
            return self
        return f


class KB:
    ENG = ["pe", "act", "dve", "pool", "sp"]
    NDS = 8

    def __init__(self, nc):
        self.nc = nc
        self.es = ExitStack()
        self.semobj = {}
        self.cnt = {}
        self.latest = {}
        for e in self.ENG:
            self.semobj[e] = self.es.enter_context(nc.semaphore("s_" + e))
            self.cnt[e] = 0
        self.dcnt = {}
        for q in ("sp", "pool", "act"):
            self.dcnt[q] = 0
            for i in range(self.NDS):
                self.semobj[f"d_{q}{i}"] = self.es.enter_context(nc.semaphore(f"d_{q}{i}"))
        self.prog = {e: [] for e in self.ENG}
        self.waited = {e: {} for e in self.ENG}
        self.n_alloc = 0

    def sb(self, shape, dtype=F32, name=None):
        self.n_alloc += 1
        name = name or f"sb{self.n_alloc}"
        t = self.es.enter_context(self.nc.sbuf_tensor(name, list(shape), dtype))
        return T(t, name)

    def ps(self, shape, dtype=F32, name=None):
        self.n_alloc += 1
        name = name or f"ps{self.n_alloc}"
        t = self.es.enter_context(self.nc.psum_tensor(name, list(shape), dtype))
        return T(t, name)

    def dram(self, shape, dtype=F32, name=None, kind="Internal"):
        self.n_alloc += 1
        name = name or f"dr{self.n_alloc}"
        t = self.nc.dram_tensor(name, list(shape), dtype, kind=kind)
        return t.ap()

    def _deps(self, E, reads, writes, skip_same=False):
        deps = {}

        def add(k, v):
            if skip_same and k == E:
                return
            if deps.get(k, 0) < v:
                deps[k] = v

        for t in reads:
            for k, v in t.lw.items():
                add(k, v)
        for t in writes:
            for k, v in t.lw.items():
                add(k, v)
            for k, v in t.rd.items():
                add(k, v)
        w = self.waited[E]
        out = []
        for k, v in deps.items():
            if w.get(k, 0) < v:
                w[k] = v
                out.append((k, v))
        return out

    def _commit(self, tok, reads, writes):
        k, v = tok
        self.latest[k] = v
        for t in writes:
            t.lw[k] = v
            t.rd = {}
        for t in reads:
            if t.rd.get(k, 0) < v:
                t.rd[k] = v

    def op(self, E, fn0, reads=(), writes=()):
        rec = _Rec()
        fn0(rec)
        name, a, k = rec.call

        def fn(eng):
            return getattr(eng, name)(*a, **k)

        waits = self._deps(E, reads, writes, skip_same=(E == "pe"))
        self.cnt[E] += 1
        tok = (E, self.cnt[E])
        self.prog[E].append((waits, fn, (E, 1)))
        self._commit(tok, reads, writes)

    def mm(self, out_t, out_ap, pairs, reads, start=True, stop=True):
        waits = self._deps("pe", reads, [out_t], skip_same=True)
        n = len(pairs)
        self.cnt["pe"] += 1
        tok = ("pe", self.cnt["pe"])
        for i, (l, r) in enumerate(pairs):
            st = start and i == 0
            sp = stop and i == n - 1

            def fn(pe, l=l, r=r, st=st, sp=sp):
                return pe.matmul(out_ap, l, r, start=st, stop=sp)

            self.prog["pe"].append((waits if i == 0 else [], fn, ("pe", 1) if i == n - 1 else None))
        self._commit(tok, reads, [out_t])

    def dma(self, q, out_ap, in_ap, reads=(), writes=()):
        i = self.dcnt[q]
        self.dcnt[q] += 1
        s = i % self.NDS
        val = 16 * (i // self.NDS + 1)
        key = f"d_{q}{s}"
        waits = self._deps(q, reads, writes)
        if i >= self.NDS and self.waited[q].get(key, 0) < val - 16:
            self.waited[q][key] = val - 16
            waits.append((key, val - 16))

        def fn(eng):
            src = in_ap() if callable(in_ap) else in_ap
            return eng.dma_start(out=out_ap, in_=src)

        self.prog[q].append((waits, fn, (key, 16)))
        self._commit((key, val), reads, writes)

    def barrier(self):
        for E in self.ENG:
            w = self.waited[E]
            waits = []
            for k, v in self.latest.items():
                if k != E and w.get(k, 0) < v:
                    w[k] = v
                    waits.append((k, v))
            self.prog[E].append((waits, None, None))

    def emit(self):
        nc = self.nc
        with nc.Block() as block:
            def run(eng, E):
                for waits, fn, inc in self.prog[E]:
                    for k, v in waits:
                        eng.wait_ge(self.semobj[k], v)
                    if fn is not None:
                        ins = fn(eng)
                        if inc is not None and ins is not None:
                            ins.then_inc(self.semobj[inc[0]], inc[1])

            @block.tensor
            def _(e):
                run(e, "pe")

            @block.scalar
            def _(e):
                run(e, "act")

            @block.vector
            def _(e):
                run(e, "dve")

            @block.gpsimd
            def _(e):
                run(e, "pool")

            @block.sync
            def _(e):
                run(e, "sp")
        self.es.close()


class RR:
    def __init__(self, items):
        self.items = items
        self.i = 0

    def __call__(self):
        t = self.items[self.i % len(self.items)]
        self.i += 1
        return t


def _const_tables():
    c = {}
    ind96 = np.zeros((96, 96), np.float32)
    ind96[:64, :64] = 1.0 / 64
    ind96[64:, 64:] = 1.0 / 32
    c["ind96"] = ind96
    ind128 = np.zeros((128, 128), np.float32)
    ind128[:64, :64] = 1.0 / 64
    ind128[64:, 64:] = 1.0 / 64
    c["ind128"] = ind128
    R = np.zeros((96, 96), np.float32)
    for base in (64, 80):
        for i in range(8):
            R[base + i, base + 8 + i] = -1.0
            R[base + 8 + i, base + i] = 1.0
    c["r96t"] = np.ascontiguousarray(R.T)
    t = np.arange(SEQ)
    row = (t // GRID_W).astype(np.float32)
    col = (t % GRID_W).astype(np.float32)
    inv = (1.0 / (10000.0 ** (np.arange(0, 16, 2, dtype=np.float32) / 16))).astype(np.float32)
    ang_r = row[:, None] * inv
    ang_c = col[:, None] * inv
    cos96 = np.ones((96, SEQ), np.float32)
    sin96 = np.zeros((96, SEQ), np.float32)
    cos96[64:72] = np.cos(ang_r).T
    cos96[72:80] = np.cos(ang_r).T
    cos96[80:88] = np.cos(ang_c).T
    cos96[88:96] = np.cos(ang_c).T
    sin96[64:72] = np.sin(ang_r).T
    sin96[72:80] = np.sin(ang_r).T
    sin96[80:88] = np.sin(ang_c).T
    sin96[88:96] = np.sin(ang_c).T
    c["cos96"] = cos96
    c["sin96"] = sin96
    m = np.arange(64)
    ang = 2 * np.pi * np.outer(m, m) / 64.0
    wcs = np.zeros((256, 512), np.float32)
    for g in range(4):
        wcs[g * 64:(g + 1) * 64, g * 64:(g + 1) * 64] = np.cos(ang)
        wcs[g * 64:(g + 1) * 64, 256 + g * 64:256 + (g + 1) * 64] = np.sin(ang)
    c["wcs"] = wcs
    c.update(_dft_tables())
    kl = (np.outer(np.arange(CTX), np.arange(CTX)) % CTX).astype(np.float64)
    a = 2 * np.pi * kl / CTX
    c["dftc_c"] = np.cos(a).astype(ml_dtypes.bfloat16)
    c["dftsn_c"] = (-np.sin(a)).astype(ml_dtypes.bfloat16)
    def invcnt(L):
        out = np.zeros((128, 2, L), np.float32)
        tt = np.arange(L)
        for g, w in enumerate((2, 4, 8, 16)):
            lo = np.clip(tt - w // 2, 0, L)
            hi = np.clip(tt + w - w // 2, 0, L)
            ic = 1.0 / (hi - lo).astype(np.float32)
            out[(g % 2) * 64:(g % 2) * 64 + 64, g // 2, :] = ic[None, :]
        return out
    c["invc"] = invcnt(SEQ)
    c["invc_c"] = invcnt(CTX)
    sel = np.zeros((8, 8, 128), np.float32)
    for e in range(8):
        sel[e, e, :] = 1.0
    c["sel"] = sel
    c["ident"] = np.eye(128, dtype=np.float32)
    return c


def _na_classes():
    return None


def _na_bias_tiles(rpb):
    H = 4
    qc = np.arange(64)
    win_c0 = np.clip(qc - 8, 0, 48)
    kc = np.arange(64)
    ok = (kc[:, None] >= win_c0[None, :]) & (kc[:, None] < win_c0[None, :] + 16)
    off = np.clip(kc[:, None] - qc[None, :] + 15, 0, 30)
    tiles = []

    def tile_for(pb, chunk):
        tl = np.full((H, 128, 128), NEG, np.float32)
        for a in range(2):
            for b in range(2):
                krow = 2 * chunk + a
                qrow = 2 * pb + b
                r0 = min(max(qrow - 4, 0), 120)
                if not (r0 <= krow < r0 + 8):
                    continue
                dr = krow - qrow + 7
                blk = np.where(ok[None], rpb[:, dr, :][:, off], NEG)
                tl[:, a * 64:(a + 1) * 64, b * 64:(b + 1) * 64] = blk
        return tl

    for cidx in range(5):
        tiles.append(tile_for(10, 10 - 2 + cidx))
    for pb in (0, 1):
        for ch in range(4):
            tiles.append(tile_for(pb, ch))
    for pb in (62, 63):
        for ch in range(60, 64):
            tiles.append(tile_for(pb, ch))
    arr = np.stack(tiles, 0)
    return np.ascontiguousarray(arr.transpose(2, 0, 1, 3))


def _pack_params(inp, l):
    pk = np.zeros((128, 80), np.float32)
    pk[:, 0:48] = inp["b_ada"][l].reshape(48, 128).T
    pk[:, 48:56] = inp["g_mix"][l].reshape(8, 128).T
    pk[:, 56:64] = inp["g_ffn"][l].reshape(8, 128).T
    pk[:, 64:66] = inp["g_cq"][l].reshape(2, 128).T
    pk[:, 66] = inp["g_ckv"][l]
    pk[:64, 67] = inp["g_mla_qn"][l]
    pk[64:96, 67] = inp["g_mla_qr"][l]
    pk[:64, 68] = inp["g_mla_kn"][l]
    pk[64:96, 68] = inp["g_mla_kr"][l]
    pk[:, 69:71] = inp["pool_scale"][l].reshape(2, 128).T
    pk[:, 71] = np.tile(inp["g_na_q"][l], 2)
    pk[:, 72] = np.tile(inp["g_na_k"][l], 2)
    return pk


def build_program(stop_after=None, debug=False):
    nc = bass.Bass("TRN2", target_bir_lowering=False)
    kb = KB(nc)
    dkind = "ExternalOutput" if debug else "Internal"

    def ein(name, shape, dt=F32):
        return kb.dram(shape, dt, name, kind="ExternalInput")

    x_in = ein("xT", [D, TT])
    cc_in = ein("cc", [128, 16])
    pk_in = ein("pk", [DEPTH, 128, 80])
    w_ada = ein("w_ada", [DEPTH, D, 6 * D])
    w_in = ein("w_in", [DEPTH, D, IN_W])
    w_out = ein("w_out", [DEPTH, D, D])
    w_uq = ein("w_uq", [DEPTH, 256, 384])
    w_ukv = ein("w_ukv", [DEPTH, 128, 512])
    w_fo = ein("w_fourier", [DEPTH, 256, 256])
    w_pl = ein("w_poolbd", [DEPTH, 2, 128, 128])
    nab = ein("na_bias", [DEPTH, 128, 21 * 4 * 128])
    w1d = ein("w1_dense", [1, D, D_FF])
    w3d = ein("w3_dense", [1, D, D_FF])
    w2d = ein("w2_dense", [1, D_FF, D])
    w_rt = ein("w_router", [1, D, NE])
    w1m = ein("w1_moe", [1, NE, D, D_FFE])
    w3m = ein("w3_moe", [1, NE, D, D_FFE])
    w2m = ein("w2_moe", [1, NE, D_FFE, D])
    c_ind96 = ein("ind96", [96, 96])
    c_ind128 = ein("ind128", [128, 128])
    c_r96t = ein("r96t", [96, 96])
    c_cos = ein("cos96", [96, SEQ])
    c_sin = ein("sin96", [96, SEQ])
    c_wcs = ein("wcs", [256, 512])
    c_dftc = ein("dftc", [SEQ, SEQ], BF16)
    c_dfts = ein("dftsn", [SEQ, SEQ], BF16)
    c_dftc_c = ein("dftc_c", [CTX, CTX], BF16)
    c_dfts_c = ein("dftsn_c", [CTX, CTX], BF16)
    c_invc = ein("invc", [128, 2, SEQ])
    c_invc_c = ein("invc_c", [128, 2, CTX])
    c_sel = ein("sel", [8, 8 * 128])
    c_ident = ein("ident", [128, 128])
    flg_in = ein("flg", [128, 2])
    out_d = kb.dram([D, HALF], F32, "outT", kind="ExternalOutput")

    xs = [x_in] + [kb.dram([D, TT], F32, f"xs{i}", kind=dkind) for i in range(1, 4)]
    uT = kb.dram([IN_W, TT], F32, "uT", kind=dkind)
    qT = kb.dram([4, 96, TT], BF16, "qT", kind=dkind)
    kT = kb.dram([4, 96, TT], BF16, "kT", kind=dkind)
    oT = kb.dram([D, TT], BF16, "oT", kind=dkind)
    fT = kb.dram([256, TT], BF16, "fT", kind=dkind)
    NBK = len(BLOCKS)
    xs_t = [[T(None, f"xs{i}_{b}") for b in range(NBK)] for i in range(4)]
    out_t = [T(None) for _ in range(NBK)]
    u_t = [[T(None) for _ in range(NBK)] for _ in range(14)]
    q_t = [[T(None) for _ in range(NBK)] for _ in range(4)]
    k_t = [[T(None) for _ in range(NBK)] for _ in range(4)]
    o_t = [[T(None) for _ in range(NBK)] for _ in range(8)]
    f_t = [T(None) for _ in range(NBK)]

    def u_tiles(r0, r1, bi):
        return [u_t[oc][bi] for oc in range(r0 // 128, (r1 - 1) // 128 + 1)]

    ind96 = kb.sb([96, 96]); ind128 = kb.sb([128, 128]); r96t = kb.sb([96, 96])
    onesD = kb.sb([128, 128]); ones256 = kb.sb([128, 128]); ones128 = kb.sb([128, 128]); onesf = kb.sb([128, 128])
    ident = kb.sb([128, 128]); sel = kb.sb([8, 8 * 128]); identb = kb.sb([128, 128], BF16)
    cc = kb.sb([128, 16]); sc = kb.sb([128, 8, 2]); flg = kb.sb([128, 2])
    pk = kb.sb([128, 80]); mod = kb.sb([128, 48, 2]); gm1 = kb.sb([128, 8, 2]); gm2 = kb.sb([128, 8, 2])
    NAR = 50560
    arena = kb.sb([128, NAR], F32, "arena")
    pst = [kb.ps([128, 512], F32, f"psb{i}") for i in range(8)]
    nps = RR(pst)

    class Carver:
        def __init__(self, lo, hi, dtype):
            self.lo, self.hi, self.dtype = lo, hi, dtype
            self.off = 0

        def reset(self):
            self.off = 0

        def get(self, shape, dtype=None):
            dtype = dtype or self.dtype
            esz = 2 if dtype == BF16 else 4
            osz = 2 if self.dtype == BF16 else 4
            n = int(np.prod(shape))
            byte0 = self.off * osz
            byte0 = (byte0 + 3) // 4 * 4
            nbytes = (n * esz + 3) // 4 * 4
            w0 = self.lo + byte0 // 4
            w1 = w0 + nbytes // 4
            assert w1 <= self.hi, (w1, self.hi)
            self.off = (byte0 + nbytes) // osz
            ap = arena.ap[:, w0:w1]
            if dtype == BF16:
                ap = ap.bitcast(BF16)[:, 0:n]
            if len(shape) == 2:
                ap = ap.rearrange("p (a b) -> p a b", a=shape[0])
            elif len(shape) == 3:
                ap = ap.rearrange("p (a b c) -> p a b c", a=shape[0], b=shape[1])
            elif len(shape) == 4:
                ap = ap.rearrange("p (a b c d) -> p a b c d", a=shape[0], b=shape[1], c=shape[2])
            return T(ap)

    cb = Carver(0, 33792, BF16)
    cf = Carver(33792, NAR, F32)
    cA = Carver(0, NAR, F32)

    for dst, src in ((ind96, c_ind96), (ind128, c_ind128), (r96t, c_r96t), (ident, c_ident), (sel, c_sel), (cc, cc_in), (flg, flg_in)):
        kb.dma("sp", dst[:], src[:], [], [dst])
    kb.op("pool", lambda e: e.memset(onesD[:], 1.0 / D), [], [onesD])
    kb.op("pool", lambda e: e.memset(ones256[:], 1.0 / 256), [], [ones256])
    kb.op("pool", lambda e: e.memset(ones128[:], 1.0 / 128), [], [ones128])
    kb.op("pool", lambda e: e.memset(onesf[:], 1.0), [], [onesf])
    kb.op("act", lambda e: e.activation(sc[:].rearrange("p k j -> p (k j)"), cc[:], AF.Silu), [cc], [sc])
    kb.op("dve", lambda e: e.tensor_copy(identb[:], ident[:]), [ident], [identb])

    evac_i = [0]

    def evac(out_ap, in_ap, reads, writes):
        evac_i[0] += 1
        if evac_i[0] % 2:
            kb.op("act", lambda e: e.copy(out_ap, in_ap), reads, writes)
        else:
            kb.op("dve", lambda e: e.tensor_copy(out_ap, in_ap), reads, writes)

    def adaln(l):
        cf.reset()
        wbs = RR([cf.get([8, 128]) for _ in range(3)])
        kb.dma("sp", pk[:], pk_in[l], [], [pk])
        wv = w_ada[l].rearrange("(k p) n -> p k n", p=128)
        for oc in range(48):
            wb = wbs()
            kb.dma("sp", wb[:], wv[:, :, oc * 128:(oc + 1) * 128], [], [wb])
            p = nps()
            kb.mm(p, p[:, 0:2], [(wb[:, k, :], sc[:, k, :]) for k in range(8)], [wb, sc])
            kb.op("dve", lambda e, p=p, oc=oc: e.tensor_scalar(mod[:, oc, :], p[:, 0:2], pk[:, oc:oc + 1], None, ALU.add),
                  [p, pk], [mod])
        for k in range(8):
            kb.op("dve", lambda e, k=k: e.tensor_scalar(gm1[:, k, :], mod[:, 8 + k, :], 1.0, pk[:, 48 + k:49 + k], ALU.add, ALU.mult),
                  [mod, pk], [gm1])
            kb.op("dve", lambda e, k=k: e.tensor_scalar(gm2[:, k, :], mod[:, 32 + k, :], 1.0, pk[:, 56 + k:57 + k], ALU.add, ALU.mult),
                  [mod, pk], [gm2])
        kb.barrier()

    def norm_mod(xb, sq, rs, w, j, gm, sh0, hb, hf=None):
        for k in range(8):
            kb.op("act", lambda e, k=k: e.activation(sq[:, k, :w], xb[:, k, :w], AF.Square), [xb], [sq])
        p = nps()
        kb.mm(p, p[:, :w], [(onesD[:, :], sq[:, k, :w]) for k in range(8)], [onesD, sq])
        kb.op("act", lambda e: e.activation(rs[:, :w], p[:, :w], AF.Sqrt, bias=EPS, scale=1.0), [p], [rs])
        kb.op("dve", lambda e: e.reciprocal(rs[:, :w], rs[:, :w]), [rs], [rs])
        for k in range(8):
            kb.op("dve", lambda e, k=k: e.tensor_tensor(sq[:, k, :w], xb[:, k, :w], rs[:, :w], ALU.mult), [xb, rs], [sq])
            kb.op("act", lambda e, k=k: e.activation(hb[:, k, :w], sq[:, k, :w], AF.Identity,
                                                      bias=mod[:, sh0 + k, j:j + 1], scale=gm[:, k, j:j + 1]),
                  [sq, mod, gm], [hb])
            if hf is not None:
                kb.op("act", lambda e, k=k: e.activation(hf[:, k, :w], sq[:, k, :w], AF.Identity,
                                                          bias=mod[:, sh0 + k, j:j + 1], scale=gm[:, k, j:j + 1]),
                      [sq, mod, gm], [hf])

    def inproj(l, xi, vna, nblocks):
        cb.off = vna_end
        cf.reset()
        win = cb.get([8, IN_W])
        hbs = RR([cb.get([8, 512]) for _ in range(2)])
        xbs = RR([cf.get([8, 512]) for _ in range(2)])
        sq = cf.get([8, 512])
        rs = cf.get([512])
        ubs = RR([cf.get([512]) for _ in range(2)])
        kb.dma("pool", win[:], w_in[l].rearrange("(k p) n -> p k n", p=128), [], [win])
        for bi in range(nblocks):
            c0, w, isc = BLOCKS[bi]
            j = 1 if isc else 0
            xb = xbs()
            kb.dma("sp", xb[:, :, :w], xs[xi].rearrange("(k p) t -> p k t", p=128)[:, :, c0:c0 + w], [xs_t[xi][bi]], [xb])
            hb = hbs()
            norm_mod(xb, sq, rs, w, j, gm1, 0, hb)
            for oc in range(14):
                r0 = oc * 128
                m = min(128, IN_W - r0)
                p = nps()
                kb.mm(p, p[:m, :w], [(win[:, k, r0:r0 + m], hb[:, k, :w]) for k in range(8)], [win, hb])
                ub = ubs()
                evac(ub[:m, :w], p[:m, :w], [p], [ub])
                kb.dma("sp", uT[r0:r0 + m, c0:c0 + w], ub[:m, :w], [ub], [u_t[oc][bi]])
            for s in range(w // 128):
                p = nps()
                kb.mm(p, p[:, 0:256], [(hb[:, k, s * 128:(s + 1) * 128], win[:, k, 1440:1696]) for k in range(8)], [win, hb])
                ch = c0 // 128 + s
                evac(vna[:, ch, :, 0:64], p[:, 0:256].rearrange("p (h d) -> p h d", h=4), [p], [vna])
        kb.barrier()

    cb.reset()
    vna = cb.get([66, 4, 65])
    vml = cb.get([66, 4, 65])
    ckvn = cb.get([TT])
    vna_end = cb.off
    kb.op("pool", lambda e: e.memset(vna[:], 1.0), [], [vna])
    kb.op("pool", lambda e: e.memset(vml[:], 1.0), [], [vml])

    nqT = kb.dram([256, TT], BF16, "nqT", kind=dkind)
    nkT = kb.dram([256, TT], BF16, "nkT", kind=dkind)
    nq_t = [T(None) for _ in range(NBK)]
    nk_t = [T(None) for _ in range(NBK)]
    psS = RR(pst[0:4])
    psO = RR(pst[4:6])
    psM = RR(pst[6:8])

    def uview(r0, r1):
        return uT[r0:r1, :].rearrange("(k p) t -> p k t", p=128)

    def norm96_rope(raw, wk, w, gcol, c0, rope, dst_ap, dst_t, cosb=None, sinb=None):
        sq, rs, qn, t1 = wk
        kb.op("act", lambda e: e.activation(sq[:96, :w], raw[:96, :w], AF.Square), [raw], [sq])
        p = nps()
        kb.mm(p, p[:96, :w], [(ind96[:, :], sq[:96, :w])], [ind96, sq])
        kb.op("act", lambda e: e.activation(rs[:96, :w], p[:96, :w], AF.Sqrt, bias=EPS, scale=1.0), [p], [rs])
        kb.op("dve", lambda e: e.reciprocal(rs[:96, :w], rs[:96, :w]), [rs], [rs])
        ob = obs()
        if rope:
            kb.op("dve", lambda e: e.scalar_tensor_tensor(qn[:96, :w], raw[:96, :w], gcol, rs[:96, :w], ALU.mult, ALU.mult),
                  [raw, rs, pk], [qn])
            p2 = nps()
            kb.mm(p2, p2[:96, :w], [(r96t[:, :], qn[:96, :w])], [r96t, qn])
            kb.op("pool", lambda e: e.tensor_tensor(t1[:96, :w], qn[:96, :w], cosb[:96, :w], ALU.mult), [qn, cosb], [t1])
            kb.op("dve", lambda e: e.tensor_tensor(sq[:96, :w], p2[:96, :w], sinb[:96, :w], ALU.mult), [p2, sinb], [sq])
            kb.op("dve", lambda e: e.tensor_tensor(ob[:96, :w], t1[:96, :w], sq[:96, :w], ALU.add), [t1, sq], [ob])
        else:
            kb.op("dve", lambda e: e.scalar_tensor_tensor(ob[:96, :w], raw[:96, :w], gcol, rs[:96, :w], ALU.mult, ALU.mult),
                  [raw, rs, pk], [ob])
        kb.dma("pool", dst_ap, ob[:96, :w], [ob], [dst_t])

    obs = None

    def qkprep(l, nblocks):
        nonlocal obs
        cb.off = vna_end
        cf.reset()
        wuq = cb.get([2, 384])
        wukv = cb.get([4, 128])
        cqn = cb.get([2, 512])
        nob = RR([cb.get([2, 512]) for _ in range(2)])
        obs = RR([cb.get([512]) for _ in range(4)])
        wks = RR([[cf.get([512]) for _ in range(4)] for _ in range(2)])
        raws = RR([cf.get([512]) for _ in range(3)])
        xfs = RR([cf.get([2, 512]) for _ in range(2)])
        sq2s = RR([cf.get([2, 512]) for _ in range(2)])
        rs2s = RR([cf.get([512]) for _ in range(2)])
        cbs = RR([cf.get([512]) for _ in range(2)])
        sbs_ = RR([cf.get([512]) for _ in range(2)])
        kb.dma("pool", wuq[:], w_uq[l].rearrange("(k p) n -> p k n", p=128), [], [wuq])
        kb.dma("pool", wukv[:], w_ukv[l].rearrange("p (h n) -> p h n", h=4), [], [wukv])
        for bi in range(nblocks):
            c0, w, isc = BLOCKS[bi]
            cosb = sinb = None
            if not isc:
                cosb, sinb = cbs(), sbs_()
                kb.dma("sp", cosb[:96, :w], c_cos[:, c0:c0 + w], [], [cosb])
                kb.dma("sp", sinb[:96, :w], c_sin[:, c0:c0 + w], [], [sinb])
            xf, sq2, rs2 = xfs(), sq2s(), rs2s()
            kb.dma("sp", xf[:, :, :w], uview(0, 256)[:, :, c0:c0 + w], u_tiles(0, 256, bi), [xf])
            for k in range(2):
                kb.op("act", lambda e, k=k: e.activation(sq2[:, k, :w], xf[:, k, :w], AF.Square), [xf], [sq2])
            p = nps()
            kb.mm(p, p[:, :w], [(ones256[:, :], sq2[:, k, :w]) for k in range(2)], [ones256, sq2])
            kb.op("act", lambda e, p=p: e.activation(rs2[:, :w], p[:, :w], AF.Sqrt, bias=EPS, scale=1.0), [p], [rs2])
            kb.op("dve", lambda e: e.reciprocal(rs2[:, :w], rs2[:, :w]), [rs2], [rs2])
            for k in range(2):
                kb.op("dve", lambda e, k=k: e.scalar_tensor_tensor(cqn[:, k, :w], xf[:, k, :w], pk[:, 64 + k:65 + k], rs2[:, :w],
                                                                    ALU.mult, ALU.mult), [xf, rs2, pk], [cqn])
            for h in range(4):
                p = nps()
                kb.mm(p, p[:96, :w], [(wuq[:, k, h * 96:(h + 1) * 96], cqn[:, k, :w]) for k in range(2)], [wuq, cqn])
                raw = raws()
                kb.op("act", lambda e, p=p, raw=raw: e.copy(raw[:96, :w], p[:96, :w]), [p], [raw])
                norm96_rope(raw, wks(), w, pk[:96, 67:68], c0, not isc, qT[h, :, c0:c0 + w], q_t[h][bi], cosb, sinb)
            xf, sq2, rs2 = xfs(), sq2s(), rs2s()
            kb.dma("sp", xf[:, 0, :w], uT[256:384, c0:c0 + w], u_tiles(256, 384, bi), [xf])
            kb.op("act", lambda e: e.activation(sq2[:, 0, :w], xf[:, 0, :w], AF.Square), [xf], [sq2])
            p = nps()
            kb.mm(p, p[:, :w], [(ones128[:, :], sq2[:, 0, :w])], [ones128, sq2])
            kb.op("act", lambda e, p=p: e.activation(rs2[:, :w], p[:, :w], AF.Sqrt, bias=EPS, scale=1.0), [p], [rs2])
            kb.op("dve", lambda e: e.reciprocal(rs2[:, :w], rs2[:, :w]), [rs2], [rs2])
            kb.op("dve", lambda e: e.scalar_tensor_tensor(ckvn[:, c0:c0 + w], xf[:, 0, :w], pk[:, 66:67], rs2[:, :w],
                                                           ALU.mult, ALU.mult), [xf, rs2, pk], [ckvn])
            for h in range(4):
                p = nps()
                kb.mm(p, p[:64, :w], [(wukv[:, h, 0:64], ckvn[:, c0:c0 + w])], [wukv, ckvn])
                raw = raws()
                kb.op("act", lambda e, p=p, raw=raw: e.copy(raw[:64, :w], p[:64, :w]), [p], [raw])
                kb.dma("sp", raw[64:96, :w], uT[384:416, c0:c0 + w], u_tiles(384, 416, bi), [raw])
                norm96_rope(raw, wks(), w, pk[:96, 68:69], c0, not isc, kT[h, :, c0:c0 + w], k_t[h][bi], cosb, sinb)
            for s in range(w // 128):
                p = nps()
                kb.mm(p, p[:, 0:256].rearrange("p (h d) -> p h d", h=4),
                      [(ckvn[:, c0 + s * 128:c0 + (s + 1) * 128], wukv[:, :, 64:128])], [wukv, ckvn])
                evac(vml[:, c0 // 128 + s, :, 0:64], p[:, 0:256].rearrange("p (h d) -> p h d", h=4), [p], [vml])
            for (r0, gc, dstT, dtl) in ((928, 71, nqT, nq_t), (1184, 72, nkT, nk_t)):
                xf, sq2, rs2 = xfs(), sq2s(), rs2s()
                kb.dma("sp", xf[:, :, :w], uview(r0, r0 + 256)[:, :, c0:c0 + w], u_tiles(r0, r0 + 256, bi), [xf])
                no = nob()
                for k in range(2):
                    kb.op("act", lambda e, k=k: e.activation(sq2[:, k, :w], xf[:, k, :w], AF.Square), [xf], [sq2])
                    p = nps()
                    kb.mm(p, p[:, :w], [(ind128[:, :], sq2[:, k, :w])], [ind128, sq2])
                    kb.op("act", lambda e, p=p: e.activation(rs2[:, :w], p[:, :w], AF.Sqrt, bias=EPS, scale=1.0), [p], [rs2])
                    kb.op("dve", lambda e: e.reciprocal(rs2[:, :w], rs2[:, :w]), [rs2], [rs2])
                    kb.op("dve", lambda e, k=k, no=no, gc=gc: e.scalar_tensor_tensor(no[:, k, :w], xf[:, k, :w], pk[:, gc:gc + 1],
                                                                                      rs2[:, :w], ALU.mult, ALU.mult),
                          [xf, rs2, pk], [no])
                kb.dma("pool", dstT.rearrange("(k p) t -> p k t", p=128)[:, :, c0:c0 + w], no[:, :, :w], [no], [dtl[bi]])
        kb.barrier()

    fin = {}

    def attn_finish(pO, w, dst_ap, src_view, dst_ts):
        osb = fin["osb"]()
        rec = fin["rec"]
        ob = fin["ob"]()
        kb.op("act", lambda e: e.copy(osb[:65, :w], pO[:65, :w]), [pO], [osb])
        kb.op("dve", lambda e: e.reciprocal(rec[64:65, :w], osb[64:65, :w]), [osb], [rec])
        pB = psM()
        kb.mm(pB, pB[:64, :w], [(onesf[64:65, 0:64], rec[64:65, :w])], [onesf, rec])
        kb.op("dve", lambda e: e.tensor_tensor(ob[:64, :w], osb[:64, :w], pB[:64, :w], ALU.mult), [osb, pB], [ob])
        kb.dma("pool", dst_ap, src_view(ob), [ob], dst_ts)

    def mla_attn(l, nblocks):
        cb.off = vna_end
        cf.reset()
        kbuf = cb.get([TT])
        qbs = RR([cb.get([512]) for _ in range(2)])
        pts = RR([cb.get([512]) for _ in range(4)])
        fin["ob"] = RR([cb.get([512]) for _ in range(2)])
        fin["osb"] = RR([cf.get([512]) for _ in range(2)])
        fin["rec"] = cf.get([512])
        for h in range(4):
            kb.dma("sp", kbuf[:96, :], kT[h], [k_t[h][b] for b in range(NBK)], [kbuf])
            for bi in range(nblocks):
                c0, w, isc = BLOCKS[bi]
                qb = qbs()
                kb.dma("sp", qb[:96, :w], qT[h, :, c0:c0 + w], [q_t[h][bi]], [qb])
                chunks = [64, 65] if isc else list(range(66))
                pO = psO()
                n = len(chunks)
                LA = 2
                ptl = {}
                for i in range(n + LA):
                    if i < n:
                        kc = chunks[i]
                        pS = psS()
                        kb.mm(pS, pS[:, :w], [(kbuf[:96, kc * 128:(kc + 1) * 128], qb[:96, :w])], [kbuf, qb])
                        pt = pts()
                        kb.op("act", lambda e: e.activation(pt[:, :w], pS[:, :w], AF.Exp, scale=MLA_SCALE), [pS], [pt])
                        ptl[i] = pt
                    if i >= LA:
                        ii = i - LA
                        pt = ptl.pop(ii)
                        kb.mm(pO, pO[:65, :w], [(vml[:, chunks[ii], h, :], pt[:, :w])], [vml, pt], start=(ii == 0), stop=(ii == n - 1))
                attn_finish(pO, w, oT[h * 64:(h + 1) * 64, c0:c0 + w], lambda ob, w=w: ob[:64, :w], [o_t[h // 2][bi]])
        kb.barrier()

    def na_attn(l, nblocks):
        cb.off = vna_end
        cf.reset()
        fin["osb"] = RR([cf.get([512]) for _ in range(2)])
        fin["rec"] = cf.get([512])
        bias = cb.get([21, 4, 128])
        fin["ob"] = RR([cb.get([512]) for _ in range(2)])
        kctx = cb.get([2, 256])
        qns = RR([cb.get([2, 128]) for _ in range(2)])
        kns = RR([cb.get([2, 640]) for _ in range(2)])
        pts = RR([cb.get([896]) for _ in range(3)])
        nqv = nqT.rearrange("(k p) t -> p k t", p=128)
        nkv = nkT.rearrange("(k p) t -> p k t", p=128)
        kb.dma("pool", bias[:].rearrange("p a h q -> p (a h q)"), nab[l], [], [bias])
        kb.op("dve", lambda e: e.tensor_scalar(bias[:].rearrange("p a h q -> p (a h q)"), bias[:].rearrange("p a h q -> p (a h q)"),
                                               1.0 / NA_SCALE, None, ALU.mult), [bias], [bias])
        kb.dma("sp", kctx[:], nkv[:, :, SEQ:TT], [nk_t[16]], [kctx])
        npb = 64 + (0 if nblocks == 16 else 2)
        pbinfo = {}

        def setup_pb(pb):
            if pb < 64:
                q0 = pb * 128
                if pb < 2:
                    loc, base = [0, 1, 2, 3], 5 + 4 * pb
                elif pb >= 62:
                    loc, base = [60, 61, 62, 63], 13 + 4 * (pb - 62)
                else:
                    loc, base = list(range(pb - 2, pb + 3)), 0
            else:
                q0 = SEQ + (pb - 64) * 128
                loc, base = [], 0
            nl = len(loc)
            qn = qns()
            kb.dma("sp", qn[:], nqv[:, :, q0:q0 + 128], [nq_t[q0 // 512]], [qn])
            kn = kns()
            if nl:
                kc0 = loc[0] * 128
                kb.dma("sp", kn[:, :, 0:nl * 128], nkv[:, :, kc0:kc0 + nl * 128],
                       [nk_t[b_] for b_ in range(kc0 // 512, (kc0 + nl * 128 - 1) // 512 + 1)], [kn])
            pbinfo[pb] = dict(q0=q0, loc=loc, base=base, nl=nl, qn=qn, kn=kn, pO=psO())

        def stage_a(pb, h):
            if h == 0:
                setup_pb(pb)
            I = pbinfo[pb]
            loc, base, nl, qn, kn = I["loc"], I["base"], I["nl"], I["qn"], I["kn"]
            nA = min(nl, 4)
            k_, po = h // 2, (h % 2) * 64
            pA = psS()
            for i in range(nA):
                kb.mm(pA, pA[:, i * 128:(i + 1) * 128], [(kn[po:po + 64, k_, i * 128:(i + 1) * 128], qn[po:po + 64, k_, :]),
                                                         (identb[:, :], bias[:, base + i, h, :])], [kn, qn, identb, bias])
            pB = psS()
            if nl == 5:
                kb.mm(pB, pB[:, 0:128], [(kn[po:po + 64, k_, 512:640], qn[po:po + 64, k_, :]),
                                         (identb[:, :], bias[:, base + 4, h, :])], [kn, qn, identb, bias])
            for c in range(2):
                kb.mm(pB, pB[:, 128 + c * 128:256 + c * 128], [(kctx[po:po + 64, k_, c * 128:(c + 1) * 128], qn[po:po + 64, k_, :])],
                      [kctx, qn])
            pt = pts()
            if nA:
                kb.op("act", lambda e: e.activation(pt[:, 0:nA * 128], pA[:, 0:nA * 128], AF.Exp, scale=NA_SCALE), [pA], [pt])
            if nl == 5:
                kb.op("act", lambda e: e.activation(pt[:, 512:896], pB[:, 0:384], AF.Exp, scale=NA_SCALE), [pB], [pt])
            else:
                kb.op("act", lambda e: e.activation(pt[:, 640:896], pB[:, 128:384], AF.Exp, scale=NA_SCALE), [pB], [pt])
            I[("pt", h)] = pt

        def stage_b(pb, h):
            I = pbinfo[pb]
            loc, nl, pO, q0 = I["loc"], I["nl"], I["pO"], I["q0"]
            pt = I.pop(("pt", h))
            for i in range(nl):
                kb.mm(pO, pO[:65, h * 128:(h + 1) * 128], [(vna[:, loc[i], h, :], pt[:, i * 128:(i + 1) * 128])], [vna, pt],
                      start=(i == 0), stop=False)
            kb.mm(pO, pO[:65, h * 128:(h + 1) * 128], [(vna[:, 64, h, :], pt[:, 640:768])], [vna, pt], start=(nl == 0), stop=False)
            kb.mm(pO, pO[:65, h * 128:(h + 1) * 128], [(vna[:, 65, h, :], pt[:, 768:896])], [vna, pt], start=False, stop=True)
            if h == 3:
                bi = q0 // 512
                attn_finish(pO, 512, oT[768:1024, q0:q0 + 128].rearrange("(h d) q -> d h q", h=4),
                            lambda ob: ob[:64, 0:512].rearrange("d (h q) -> d h q", h=4), [o_t[6][bi], o_t[7][bi]])
                del pbinfo[pb]

        items = [(pb, h) for pb in range(npb) for h in range(4)]
        LA = 1
        for idx in range(len(items) + LA):
            if idx < len(items):
                stage_a(*items[idx])
            if idx >= LA:
                stage_b(*items[idx - LA])
        kb.barrier()

    def fourier(l, nblocks):
        cb.reset()
        cf.reset()
        AB = cb.get([64, 512])
        xf = cb.get([2, SEQ])
        wcs = cb.get([2, 512])
        tcs = RR([cb.get([512]) for _ in range(3)])
        tss = RR([cb.get([512]) for _ in range(3)])
        fbs = RR([cb.get([512]) for _ in range(2)])
        kb.dma("pool", wcs[:], c_wcs.rearrange("(k p) n -> p k n", p=128), [], [wcs])
        kb.dma("pool", xf[:], uview(416, 672)[:, :, 0:SEQ], [t for b in range(16) for t in u_tiles(416, 672, b)], [xf])
        for tc in range(64):
            p = nps()
            kb.mm(p, p[:, :], [(xf[:, k, tc * 128:(tc + 1) * 128], wcs[:, k, :]) for k in range(2)], [xf, wcs])
            evac(AB[:, tc, :], p[:, :], [p], [AB])
        sc_l = float(1.0 / np.sqrt(SEQ * 64.0))
        for kbk in range(16):
            pF = [psO(), psO()]
            for lc in range(64):
                tcn = tcs()
                tsn = tss()
                kb.dma("sp", tcn[:], c_dftc[lc * 128:(lc + 1) * 128, kbk * 512:(kbk + 1) * 512], [], [tcn])
                kb.dma("sp", tsn[:], c_dfts[lc * 128:(lc + 1) * 128, kbk * 512:(kbk + 1) * 512], [], [tsn])
                for fc in range(2):
                    kb.mm(pF[fc], pF[fc][:, :], [(AB[:, lc, fc * 128:(fc + 1) * 128], tcn[:]),
                                                 (AB[:, lc, 256 + fc * 128:256 + (fc + 1) * 128], tsn[:])], [AB, tcn, tsn],
                          start=(lc == 0), stop=(lc == 63))
            for fc in range(2):
                fb = fbs()
                kb.op("act", lambda e, fb=fb, p=pF[fc]: e.activation(fb[:], p[:, :], AF.Copy, scale=sc_l), [pF[fc]], [fb])
                kb.dma("sp", fT[fc * 128:(fc + 1) * 128, kbk * 512:(kbk + 1) * 512], fb[:], [fb], [f_t[kbk]])
        if nblocks == 17:
            xc = cb.get([2, CTX])
            ABc = cb.get([2, 512])
            tcc = cb.get([2, CTX])
            tsc = cb.get([2, CTX])
            kb.dma("pool", xc[:], uview(416, 672)[:, :, SEQ:TT], u_tiles(416, 672, 16), [xc])
            kb.dma("sp", tcc[:], c_dftc_c.rearrange("(k p) n -> p k n", p=128), [], [tcc])
            kb.dma("sp", tsc[:], c_dfts_c.rearrange("(k p) n -> p k n", p=128), [], [tsc])
            for tc in range(2):
                p = nps()
                kb.mm(p, p[:, :], [(xc[:, k, tc * 128:(tc + 1) * 128], wcs[:, k, :]) for k in range(2)], [xc, wcs])
                evac(ABc[:, tc, :], p[:, :], [p], [ABc])
            sc_c = float(1.0 / np.sqrt(CTX * 64.0))
            for fc in range(2):
                p = nps()
                prs = []
                for lc in range(2):
                    prs.append((ABc[:, lc, fc * 128:(fc + 1) * 128], tcc[:, lc, :]))
                    prs.append((ABc[:, lc, 256 + fc * 128:256 + (fc + 1) * 128], tsc[:, lc, :]))
                kb.mm(p, p[:, 0:CTX], prs, [ABc, tcc, tsc])
                fb = fbs()
                kb.op("act", lambda e, fb=fb, p=p: e.activation(fb[:, 0:CTX], p[:, 0:CTX], AF.Copy, scale=sc_c), [p], [fb])
                kb.dma("sp", fT[fc * 128:(fc + 1) * 128, SEQ:TT], fb[:, 0:CTX], [fb], [f_t[16]])
        wf = cb.get([2, 256])
        fls = RR([cb.get([2, 512]) for _ in range(2)])
        kb.dma("pool", wf[:], w_fo[l].rearrange("(k p) n -> p k n", p=128), [], [wf])
        for bi in range(nblocks):
            c0, w, isc = BLOCKS[bi]
            fl = fls()
            kb.dma("sp", fl[:, :, :w], fT.rearrange("(k p) t -> p k t", p=128)[:, :, c0:c0 + w], [f_t[bi]], [fl])
            for oc in range(2):
                p = nps()
                kb.mm(p, p[:, :w], [(wf[:, k, oc * 128:(oc + 1) * 128], fl[:, k, :w]) for k in range(2)], [wf, fl])
                fb = fbs()
                evac(fb[:, :w], p[:, :w], [p], [fb])
                kb.dma("sp", oT[256 + oc * 128:256 + (oc + 1) * 128, c0:c0 + w], fb[:, :w], [fb], [o_t[2 + oc][bi]])
        kb.barrier()

    def pool(l, nblocks):
        cb.reset()
        cf.reset()
        wpl = cb.get([2, 128])
        pds = RR([cb.get([2, 512]) for _ in range(2)])
        pos = RR([cb.get([512]) for _ in range(2)])
        xps = RR([cf.get([2, 528]) for _ in range(2)])
        s2 = cf.get([2, 528]); s4 = cf.get([2, 528]); s8 = cf.get([2, 528]); s16 = cf.get([2, 528])
        ivs = RR([cf.get([2, 512]) for _ in range(2)])
        tmp = cf.get([2, 512])
        kb.dma("pool", wpl[:], w_pl[l].rearrange("c p n -> p c n"), [], [wpl])
        uv = uview(672, 928)
        for bi in range(nblocks):
            c0, w, isc = BLOCKS[bi]
            xp = xps()
            lo = c0 - 8 if (not isc and bi > 0) else c0
            hi = c0 + w + 8 if (not isc and bi < 15) else c0 + w
            kb.op("pool", lambda e, xp=xp: e.memset(xp[:], 0.0), [], [xp])
            deps = []
            for b in range(max(0, bi - 1), min(NBK, bi + 2)):
                deps += u_tiles(672, 928, b)
            kb.dma("sp", xp[:, :, 8 + lo - c0:8 + hi - c0], uv[:, :, lo:hi], deps, [xp])
            n = w + 16
            kb.op("dve", lambda e, xp=xp: e.tensor_tensor(s2[:, :, 1:n], xp[:, :, 0:n - 1], xp[:, :, 1:n], ALU.add), [xp], [s2])
            kb.op("dve", lambda e: e.tensor_tensor(s4[:, :, 2:n - 1], s2[:, :, 1:n - 2], s2[:, :, 3:n], ALU.add), [s2], [s4])
            kb.op("dve", lambda e: e.tensor_tensor(s8[:, :, 4:n - 3], s4[:, :, 2:n - 5], s4[:, :, 6:n - 1], ALU.add), [s4], [s8])
            kb.op("dve", lambda e: e.tensor_tensor(s16[:, :, 8:n - 7], s8[:, :, 4:n - 11], s8[:, :, 12:n - 3], ALU.add), [s8], [s16])
            iv = ivs()
            if isc:
                kb.dma("sp", iv[:, :, :w], c_invc_c[:, :, :], [], [iv])
            else:
                kb.dma("sp", iv[:, :, :w], c_invc[:, :, c0:c0 + w], [], [iv])
            pd = pds()
            for g, sg in enumerate((s2, s4, s8, s16)):
                ch, po = g // 2, (g % 2) * 64
                kb.op("dve", lambda e, sg=sg, ch=ch, po=po, iv=iv: e.tensor_tensor(tmp[po:po + 64, ch, :w], sg[po:po + 64, ch, 8:8 + w],
                                                                                   iv[po:po + 64, ch, :w], ALU.mult), [sg, iv], [tmp])
                kb.op("dve", lambda e, ch=ch, po=po, xp=xp, pd=pd: e.tensor_tensor(pd[po:po + 64, ch, :w], tmp[po:po + 64, ch, :w],
                                                                                   xp[po:po + 64, ch, 8:8 + w], ALU.subtract),
                      [tmp, xp], [pd])
            for ch in range(2):
                p = nps()
                kb.mm(p, p[:, :w], [(wpl[:, ch, :], pd[:, ch, :w])], [wpl, pd])
                po_ = pos()
                kb.op("act", lambda e, p=p, po_=po_, ch=ch: e.activation(po_[:, :w], p[:, :w], AF.Copy, scale=pk[:, 69 + ch:70 + ch]),
                      [p, pk], [po_])
                kb.dma("sp", oT[512 + ch * 128:512 + (ch + 1) * 128, c0:c0 + w], po_[:, :w], [po_], [o_t[4 + ch][bi]])
        kb.barrier()

    def outproj(l, xi, nblocks):
        cb.reset()
        cf.reset()
        wo = cb.get([8, D])
        obs_ = RR([cb.get([8, 512]) for _ in range(2)])
        xbs = RR([cf.get([8, 512]) for _ in range(2)])
        kb.dma("pool", wo[:], w_out[l].rearrange("(k p) n -> p k n", p=128), [], [wo])
        for bi in range(nblocks):
            c0, w, isc = BLOCKS[bi]
            j = 1 if isc else 0
            ob = obs_()
            kb.dma("sp", ob[:, :, :w], oT.rearrange("(k p) t -> p k t", p=128)[:, :, c0:c0 + w], [o_t[r][bi] for r in range(8)], [ob])
            xb = xbs()
            kb.dma("sp", xb[:, :, :w], xs[xi].rearrange("(k p) t -> p k t", p=128)[:, :, c0:c0 + w], [xs_t[xi][bi]], [xb])
            for oc in range(8):
                p = nps()
                kb.mm(p, p[:, :w], [(wo[:, k, oc * 128:(oc + 1) * 128], ob[:, k, :w]) for k in range(8)], [wo, ob])
                kb.op("dve", lambda e, p=p, xb=xb, oc=oc, j=j: e.scalar_tensor_tensor(xb[:, oc, :w], p[:, :w], mod[:, 16 + oc, j:j + 1],
                                                                                       xb[:, oc, :w], ALU.mult, ALU.add),
                      [p, mod, xb], [xb])
            kb.dma("sp", xs[xi + 1].rearrange("(k p) t -> p k t", p=128)[:, :, c0:c0 + w], xb[:, :, :w], [xb], [xs_t[xi + 1][bi]])
        kb.barrier()

    def ffn(l, xi, nblocks, moe, last):
        F = D_FFE if moe else D_FF
        NF = F // 128
        NEX = NE if moe else 1
        groups = [list(range(i, min(i + 7, NF))) for i in range(0, NF, 7)]
        SBW = 2048
        cA.reset()
        R0 = cA.get([8, SBW], F32)
        r0f = R0.ap.rearrange("p k t -> p (k t)")
        xb = r0f[:, 0:4096].rearrange("p (k t) -> p k t", k=8)
        sq = r0f[:, 4096:8192].rearrange("p (k t) -> p k t", k=8)
        yv = R0.ap
        h2 = cA.get([8, SBW], BF16)
        actq = cA.get([7, SBW], BF16)
        WB = RR([(cA.get([4096], BF16), cA.get([4096], BF16)) for _ in range(3)])
        g1s = RR([cA.get([512], BF16) for _ in range(3)])
        gbc = cA.get([SBW], BF16)
        gT = cA.get([SBW], F32)
        rs = cA.get([512], F32)
        xsl = RR([cA.get([512], F32) for _ in range(4 if last else 2)])
        lg = cA.get([8], F32); eq = cA.get([8], F32); l2 = cA.get([8], F32); ex = cA.get([8], F32)
        msk = cA.get([8], F32); gt = cA.get([8], F32); sm = cA.get([8], F32)
        wr = cA.get([8, 8], F32)
        if moe:
            kb.dma("sp", wr[:], w_rt[0].rearrange("(k p) e -> p k e", p=128), [], [wr])
        split = last
        ntok = HALF if split else SEQ
        xBv = r0f[:, 8192:12288].rearrange("p (k t) -> p k t", k=8)
        sbs = [(i * SBW, [(q * 512, 512) for q in range(SBW // 512)], False) for i in range(ntok // SBW)]
        if nblocks == 17:
            sbs.append((SEQ, [(0, 256)], True))
        xin = xs[xi].rearrange("(k p) t -> p k t", p=128)
        for (s0, subs, isc) in sbs:
            j = 1 if isc else 0
            for (so, w) in subs:
                c0 = s0 + so
                bi = c0 // 512
                kb.dma("sp", xb[:, :, :w], xin[:, :, c0:c0 + w], [xs_t[xi][bi]], [R0])
                if split:
                    kb.dma("sp", xBv[:, :, :w], xin[:, :, HALF + c0:HALF + c0 + w], [xs_t[xi][bi + 8]], [R0])
                    kb.op("dve", lambda e: e.tensor_scalar(r0f[:, 0:4096], r0f[:, 0:4096], flg[:, 0:1], None, ALU.mult), [R0, flg], [R0])
                    kb.op("dve", lambda e: e.scalar_tensor_tensor(r0f[:, 0:4096], r0f[:, 8192:12288], flg[:, 1:2], r0f[:, 0:4096],
                                                                  ALU.mult, ALU.add), [R0, flg], [R0])
                for k in range(8):
                    kb.op("act", lambda e: e.activation(sq[:, k, :w], xb[:, k, :w], AF.Square), [R0], [R0])
                p = nps()
                kb.mm(p, p[:, :w], [(onesD[:, :], sq[:, k, :w]) for k in range(8)], [onesD, R0])
                kb.op("act", lambda e: e.activation(rs[:, :w], p[:, :w], AF.Sqrt, bias=EPS, scale=1.0), [p], [rs])
                kb.op("dve", lambda e: e.reciprocal(rs[:, :w], rs[:, :w]), [rs], [rs])
                for k in range(8):
                    kb.op("dve", lambda e: e.tensor_tensor(sq[:, k, :w], xb[:, k, :w], rs[:, :w], ALU.mult), [R0, rs], [R0])
                    kb.op("act", lambda e: e.activation(sq[:, k, :w], sq[:, k, :w], AF.Identity,
                                                        bias=mod[:, 24 + k, j:j + 1], scale=gm2[:, k, j:j + 1]),
                          [R0, mod, gm2], [R0])
                    kb.op("dve", lambda e: e.tensor_copy(h2[:, k, so:so + w], sq[:, k, :w]), [R0], [h2])
                if moe:
                    for t4 in range(w // 128):
                        p = psM()
                        kb.mm(p, p[:, 0:8], [(sq[:, k, t4 * 128:(t4 + 1) * 128], wr[:, k, :]) for k in range(8)], [R0, wr])
                        kb.op("act", lambda e: e.copy(lg[:, :], p[:, 0:8]), [p], [lg])
                        kb.op("dve", lambda e: e.tensor_reduce(sm[:, 0:1], lg[:, :], mybir.AxisListType.X, ALU.max), [lg], [sm])
                        kb.op("dve", lambda e: e.tensor_scalar(eq[:, :], lg[:, :], sm[:, 0:1], None, ALU.is_equal), [lg, sm], [eq])
                        kb.op("dve", lambda e: e.scalar_tensor_tensor(l2[:, :], eq[:, :], -1e30, lg[:, :], ALU.mult, ALU.add), [eq, lg], [l2])
                        kb.op("dve", lambda e: e.tensor_reduce(sm[:, 1:2], l2[:, :], mybir.AxisListType.X, ALU.max), [l2], [sm])
                        kb.op("dve", lambda e: e.tensor_scalar(msk[:, :], lg[:, :], sm[:, 1:2], None, ALU.is_ge), [lg, sm], [msk])
                        kb.op("dve", lambda e: e.tensor_scalar(sm[:, 2:3], sm[:, 0:1], -1.0, None, ALU.mult), [sm], [sm])
                        kb.op("act", lambda e: e.activation(ex[:, :], lg[:, :], AF.Exp, bias=sm[:, 2:3], scale=1.0), [lg, sm], [ex])
                        kb.op("act", lambda e: e.activation(sm[:, 3:4], sm[:, 1:2], AF.Exp, bias=sm[:, 2:3], scale=1.0), [sm], [sm])
                        kb.op("dve", lambda e: e.tensor_scalar(sm[:, 4:5], sm[:, 3:4], 1.0, None, ALU.add), [sm], [sm])
                        kb.op("dve", lambda e: e.reciprocal(sm[:, 5:6], sm[:, 4:5]), [sm], [sm])
                        kb.op("dve", lambda e: e.scalar_tensor_tensor(gt[:, :], ex[:, :], sm[:, 5:6], msk[:, :], ALU.mult, ALU.mult),
                              [ex, sm, msk], [gt])
                        p2 = psM()
                        kb.mm(p2, p2[:8, 0:128], [(gt[:, :], ident[:, :])], [gt, ident])
                        o = so + t4 * 128
                        kb.op("act", lambda e: e.copy(gT[:8, o:o + 128], p2[:8, 0:128]), [p2], [gT])
            kb.barrier()
            first_y = True
            for ex_i in range(NEX):
                if moe:
                    w1v = w1m[0, ex_i].rearrange("(k p) f -> p k f", p=128)
                    w3v = w3m[0, ex_i].rearrange("(k p) f -> p k f", p=128)
                    w2v = w2m[0, ex_i].rearrange("(c p) n -> p c n", p=128)
                    for (so, w) in subs:
                        p = psM()
                        kb.mm(p, p[:, :w], [(sel[:, ex_i * 128:(ex_i + 1) * 128], gT[:8, so:so + w])], [sel, gT])
                        kb.op("act", lambda e: e.copy(gbc[:, so:so + w], p[:, :w]), [p], [gbc])
                else:
                    w1v = w1d[0].rearrange("(k p) f -> p k f", p=128)
                    w3v = w3d[0].rearrange("(k p) f -> p k f", p=128)
                    w2v = w2d[0].rearrange("(c p) n -> p c n", p=128)
                for grp in groups:
                    f0 = grp[0] * 128
                    ncols = len(grp) * 128
                    for cc0 in range(0, ncols, 512):
                        ncc = min(512, ncols - cc0)
                        b1, b3 = WB()
                        w1t = b1.ap.rearrange("p (k f) -> p k f", k=8)
                        w3t = b3.ap.rearrange("p (k f) -> p k f", k=8)
                        kb.dma("pool", w1t[:, :, :ncc], w1v[:, :, f0 + cc0:f0 + cc0 + ncc], [], [b1])
                        kb.dma("pool", w3t[:, :, :ncc], w3v[:, :, f0 + cc0:f0 + cc0 + ncc], [], [b3])
                        for fl in range(ncc // 128):
                            fcl = cc0 // 128 + fl
                            for (so, w) in subs:
                                pa = nps()
                                kb.mm(pa, pa[:, :w], [(w1t[:, k, fl * 128:(fl + 1) * 128], h2[:, k, so:so + w]) for k in range(8)], [b1, h2])
                                pb_ = nps()
                                kb.mm(pb_, pb_[:, :w], [(w3t[:, k, fl * 128:(fl + 1) * 128], h2[:, k, so:so + w]) for k in range(8)], [b3, h2])
                                g1 = g1s()
                                kb.op("act", lambda e: e.activation(g1[:, :w], pa[:, :w], AF.Silu), [pa], [g1])
                                if moe:
                                    kb.op("dve", lambda e: e.tensor_tensor(g1[:, :w], g1[:, :w], gbc[:, so:so + w], ALU.mult), [g1, gbc], [g1])
                                kb.op("dve", lambda e: e.tensor_tensor(actq[:, fcl, so:so + w], g1[:, :w], pb_[:, :w], ALU.mult),
                                      [g1, pb_], [actq])
                    b1, b3 = WB()
                    w2t = b1.ap[:, 0:4096]
                    ng = len(grp)
                    w2pair = T(None)
                    w2a = b1.ap.rearrange("p (c n) -> p c n", n=1024)
                    w2b = b3.ap.rearrange("p (c n) -> p c n", n=1024)
                    na = min(ng, 4)
                    kb.dma("pool", w2a[:, 0:na, :], w2v[:, grp[0]:grp[0] + na, :], [], [b1])
                    if ng > 4:
                        kb.dma("pool", w2b[:, 0:ng - 4, :], w2v[:, grp[0] + 4:grp[0] + ng, :], [], [b3])
                    for oc in range(8):
                        for (so, w) in subs:
                            py = nps()
                            prs = []
                            for c in range(ng):
                                src = w2a if c < 4 else w2b
                                prs.append((src[:, c % 4, oc * 128:(oc + 1) * 128], actq[:, c, so:so + w]))
                            kb.mm(py, py[:, :w], prs, [b1, b3, actq])
                            if first_y:
                                kb.op("act", lambda e: e.copy(yv[:, oc, so:so + w], py[:, :w]), [py], [R0])
                            else:
                                kb.op("dve", lambda e: e.tensor_tensor(yv[:, oc, so:so + w], yv[:, oc, so:so + w], py[:, :w], ALU.add),
                                      [py, R0], [R0])
                    first_y = False
            for (so, w) in subs:
                c0 = s0 + so
                bi = c0 // 512
                for oc in range(8):
                    xl = xsl()
                    kb.dma("sp", xl[:, :w], xs[xi][oc * 128:(oc + 1) * 128, c0:c0 + w], [xs_t[xi][bi]], [xl])
                    if split:
                        xl2 = xsl()
                        kb.dma("sp", xl2[:, :w], xs[xi][oc * 128:(oc + 1) * 128, HALF + c0:HALF + c0 + w], [xs_t[xi][bi + 8]], [xl2])
                        kb.op("dve", lambda e: e.tensor_scalar(xl[:, :w], xl[:, :w], flg[:, 0:1], None, ALU.mult), [xl, flg], [xl])
                        kb.op("dve", lambda e: e.scalar_tensor_tensor(xl[:, :w], xl2[:, :w], flg[:, 1:2], xl[:, :w], ALU.mult, ALU.add),
                              [xl2, xl, flg], [xl])
                    kb.op("dve", lambda e: e.scalar_tensor_tensor(xl[:, :w], yv[:, oc, so:so + w], mod[:, 40 + oc, j:j + 1], xl[:, :w],
                                                                  ALU.mult, ALU.add), [R0, mod, xl], [xl])
                    if last:
                        kb.dma("sp", out_d[oc * 128:(oc + 1) * 128, c0:c0 + w], xl[:, :w], [xl], [out_t[bi]])
                    else:
                        kb.dma("sp", xs[xi + 1][oc * 128:(oc + 1) * 128, c0:c0 + w], xl[:, :w], [xl], [xs_t[xi + 1][bi]])
            kb.barrier()

    def finish():
        kb.barrier()
        kb.emit()
        return nc

    for l in range(DEPTH):
        last = l == DEPTH - 1
        nblocks = 16 if last else 17
        xi = 2 * l
        adaln(l)
        kb.op("pool", lambda e: e.memset(vna[:], 1.0), [], [vna])
        kb.op("pool", lambda e: e.memset(vml[:], 1.0), [], [vml])
        inproj(l, xi, vna, 17)
        if stop_after == f"inproj{l}":
            return finish()
        qkprep(l, 17)
        if stop_after == f"qkprep{l}":
            return finish()
        mla_attn(l, nblocks)
        if stop_after == f"mla{l}":
            return finish()
        na_attn(l, nblocks)
        if stop_after == f"na{l}":
            return finish()
        fourier(l, nblocks)
        pool(l, nblocks)
        if stop_after == f"mix{l}":
            return finish()
        outproj(l, xi, nblocks)
        if stop_after == f"outproj{l}":
            return finish()
        ffn(l, xi + 1, nblocks, moe=(l % 2 == 1), last=last)
        if stop_after == f"ffn{l}":
            return finish()
    return finish()


_CONSTS = None


def _dft_tables():
    c = {}
    k = np.arange(SEQ, dtype=np.int64)
    dc = np.empty((SEQ, SEQ), ml_dtypes.bfloat16)
    ds = np.empty((SEQ, SEQ), ml_dtypes.bfloat16)
    for r0 in range(0, SEQ, 1024):
        kl = (np.outer(k[r0:r0 + 1024], k) % SEQ).astype(np.float32) * np.float32(2 * np.pi / SEQ)
        dc[r0:r0 + 1024] = np.cos(kl).astype(ml_dtypes.bfloat16)
        ds[r0:r0 + 1024] = (-np.sin(kl)).astype(ml_dtypes.bfloat16)
    c["dftc"] = dc
    c["dftsn"] = ds
    return c


def prep_inputs(inp, cores):
    global _CONSTS
    if _CONSTS is None:
        _CONSTS = _const_tables()
    inp = {k: np.asarray(v) for k, v in inp.items()}
    shared = dict(_CONSTS)
    shared["sel"] = shared["sel"].reshape(8, 8 * 128)
    shared["pk"] = np.stack([_pack_params(inp, l) for l in range(DEPTH)], 0)
    for name in ("w_ada", "w_in", "w_out", "w_uq", "w_ukv", "w_fourier", "w1_dense", "w3_dense", "w2_dense",
                 "w_router", "w1_moe", "w3_moe", "w2_moe"):
        shared[name] = np.ascontiguousarray(inp[name], dtype=np.float32)
    bd = np.zeros((DEPTH, 2, 128, 128), np.float32)
    for l in range(DEPTH):
        for g in range(4):
            o = (g % 2) * 64
            bd[l, g // 2, o:o + 64, o:o + 64] = inp["w_pool"][l, g]
    shared["w_poolbd"] = bd
    shared["na_bias"] = np.stack([_na_bias_tiles(inp["na_rpb"][l]).reshape(128, -1) for l in range(DEPTH)], 0)
    maps = []
    for b in cores:
        m = dict(shared)
        m["xT"] = np.ascontiguousarray(np.concatenate([inp["x"][b].T, inp["ctx"][b].T], axis=1), dtype=np.float32)
        cc = np.zeros((128, 16), np.float32)
        cc[:, 0::2] = inp["c"][b].reshape(8, 128).T
        cc[:, 1::2] = inp["c_ctx"].reshape(8, 128).T
        m["cc"] = cc
        maps.append(m)
    return maps


def prep_inputs8(inp):
    base = prep_inputs(inp, list(range(4)))
    maps = []
    for i in range(NCORES):
        m = dict(base[i // 2])
        f = np.zeros((128, 2), np.float32)
        f[:, i % 2] = 1.0
        m["flg"] = f
        maps.append(m)
    return maps


_NC = None


def kernel(**inputs):
    global _NC
    if _NC is None:
        _NC = build_program()
    maps = prep_inputs8(inputs)
    res = run_bass_kernel_spmd(_NC, maps, core_ids=list(range(NCORES)))
    halves = [np.ascontiguousarray(r["outT"].T) for r in res.results]
    out = np.stack([np.concatenate([halves[2 * b], halves[2 * b + 1]], axis=0) for b in range(4)], 0)
    return out.astype(np.float32)
```

```python
import numpy as np
import ml_dtypes
from contextlib import ExitStack
import concourse.bass as bass
import concourse.mybir as mybir
from concourse.bass_utils import run_bass_kernel_spmd

F32 = mybir.dt.float32
BF16 = mybir.dt.bfloat16
AF = mybir.ActivationFunctionType
ALU = mybir.AluOpType

D = 1024
SEQ = 8192
CTX = 256
TT = SEQ + CTX
DEPTH = 2
GRID_W = 64
IN_W = 1696
D_FF = 2816
NE = 8
D_FFE = 3584
MLA_SCALE = 96 ** -0.5
NA_SCALE = 64 ** -0.5
EPS = 1e-6
NEG = -30000.0
BLOCKS = [(i * 512, 512, False) for i in range(16)] + [(SEQ, 256, True)]
NCORES = 8
HALF = SEQ // 2


class T:
    __slots__ = ("ap", "lw", "rd", "name")

    def __init__(self, ap, name=""):
        self.ap = ap
        self.lw = {}
        self.rd = {}
        self.name = name

    def __getitem__(self, idx):
        return self.ap[idx]


class _Rec:
    def __getattr__(self, name):
        def f(*a, **k):
            self.call = (name, a, k)
            return self
        return f


class KB:
    ENG = ["pe", "act", "dve", "pool", "sp"]
    NDS = 8

    def __init__(self, nc):
        self.nc = nc
        self.es = ExitStack()
        self.semobj = {}
        self.cnt = {}
        self.latest = {}
        for e in self.ENG:
            self.semobj[e] = self.es.enter_context(nc.semaphore("s_" + e))
            self.cnt[e] = 0
        self.dcnt = {}
        for q in ("sp", "pool", "act"):
            self.dcnt[q] = 0
            for i in range(self.NDS):
                self.semobj[f"d_{q}{i}"] = self.es.enter_context(nc.semaphore(f"d_{q}{i}"))
        self.prog = {e: [] for e in self.ENG}
        self.waited = {e: {} for e in self.ENG}
        self.n_alloc = 0

    def sb(self, shape, dtype=F32, name=None):
        self.n_alloc += 1
        name = name or f"sb{self.n_alloc}"
        t = self.es.enter_context(self.nc.sbuf_tensor(name, list(shape), dtype))
        return T(t, name)

    def ps(self, shape, dtype=F32, name=None):
        self.n_alloc += 1
        name = name or f"ps{self.n_alloc}"
        t = self.es.enter_context(self.nc.psum_tensor(name, list(shape), dtype))
        return T(t, name)

    def dram(self, shape, dtype=F32, name=None, kind="Internal"):
        self.n_alloc += 1
        name = name or f"dr{self.n_alloc}"
        t = self.nc.dram_tensor(name, list(shape), dtype, kind=kind)
        return t.ap()

    def _deps(self, E, reads, writes, skip_same=False):
        deps = {}

        def add(k, v):
            if skip_same and k == E:
                return
            if deps.get(k, 0) < v:
                deps[k] = v

        for t in reads:
            for k, v in t.lw.items():
                add(k, v)
        for t in writes:
            for k, v in t.lw.items():
                add(k, v)
            for k, v in t.rd.items():
                add(k, v)
        w = self.waited[E]
        out = []
        for k, v in deps.items():
            if w.get(k, 0) < v:
                w[k] = v
                out.append((k, v))
        return out

    def _commit(self, tok, reads, writes):
        k, v = tok
        self.latest[k] = v
        for t in writes:
            t.lw[k] = v
            t.rd = {}
        for t in reads:
            if t.rd.get(k, 0) < v:
                t.rd[k] = v

    def op(self, E, fn0, reads=(), writes=()):
        rec = _Rec()
        fn0(rec)
        name, a, k = rec.call

        def fn(eng):
            return getattr(eng, name)(*a, **k)

        waits = self._deps(E, reads, writes, skip_same=(E == "pe"))
        self.cnt[E] += 1
        tok = (E, self.cnt[E])
        self.prog[E].append((waits, fn, (E, 1)))
        self._commit(tok, reads, writes)

    def mm(self, out_t, out_ap, pairs, reads, start=True, stop=True):
        waits = self._deps("pe", reads, [out_t], skip_same=True)
        n = len(pairs)
        self.cnt["pe"] += 1
        tok = ("pe", self.cnt["pe"])
        for i, (l, r) in enumerate(pairs):
            st = start and i == 0
            sp = stop and i == n - 1

            def fn(pe, l=l, r=r, st=st, sp=sp):
                return pe.matmul(out_ap, l, r, start=st, stop=sp)

            self.prog["pe"].append((waits if i == 0 else [], fn, ("pe", 1) if i == n - 1 else None))
        self._commit(tok, reads, [out_t])

    def dma(self, q, out_ap, in_ap, reads=(), writes=()):
        i = self.dcnt[q]
        self.dcnt[q] += 1
        s = i % self.NDS
        val = 16 * (i // self.NDS + 1)
        key = f"d_{q}{s}"
        waits = self._deps(q, reads, writes)
        if i >= self.NDS and self.waited[q].get(key, 0) < val - 16:
            self.waited[q][key] = val - 16
            waits.append((key, val - 16))

        def fn(eng):
            src = in_ap() if callable(in_ap) else in_ap
            return eng.dma_start(out=out_ap, in_=src)

        self.prog[q].append((waits, fn, (key, 16)))
        self._commit((key, val), reads, writes)

    def barrier(self):
        for E in self.ENG:
            w = self.waited[E]
            waits = []
            for k, v in self.latest.items():
                if k != E and w.get(k, 0) < v:
                    w[k] = v
                    waits.append((k, v))
            self.prog[E].append((waits, None, None))

    def emit(self):
        nc = self.nc
        with nc.Block() as block:
            def run(eng, E):
                for waits, fn, inc in self.prog[E]:
                    for k, v in waits:
                        eng.wait_ge(self.semobj[k], v)
                    if fn is not None:
                        ins = fn(eng)
                        if inc is not None and ins is not None:
                            ins.then_inc(self.semobj[inc[0]], inc[1])

            @block.tensor
            def _(e):
                run(e, "pe")

            @block.scalar
            def _(e):
                run(e, "act")

            @block.vector
            def _(e):
                run(e, "dve")

            @block.gpsimd
            def _(e):
                run(e, "pool")

            @block.sync
            def _(e):
                run(e, "sp")
        self.es.close()


class RR:
    def __init__(self, items):
        self.items = items
        self.i = 0

    def __call__(self):
        t = self.items[self.i % len(self.items)]
        self.i += 1
        return t


def _const_tables():
    c = {}
    ind96 = np.zeros((96, 96), np.float32)
    ind96[:64, :64] = 1.0 / 64
    ind96[64:, 64:] = 1.0 / 32
    c["ind96"] = ind96
    ind128 = np.zeros((128, 128), np.float32)
    ind128[:64, :64] = 1.0 / 64
    ind128[64:, 64:] = 1.0 / 64
    c["ind128"] = ind128
    R = np.zeros((96, 96), np.float32)
    for base in (64, 80):
        for i in range(8):
            R[base + i, base + 8 + i] = -1.0
            R[base + 8 + i, base + i] = 1.0
    c["r96t"] = np.ascontiguousarray(R.T)
    t = np.arange(SEQ)
    row = (t // GRID_W).astype(np.float32)
    col = (t % GRID_W).astype(np.float32)
    inv = (1.0 / (10000.0 ** (np.arange(0, 16, 2, dtype=np.float32) / 16))).astype(np.float32)
    ang_r = row[:, None] * inv
    ang_c = col[:, None] * inv
    cos96 = np.ones((96, SEQ), np.float32)
    sin96 = np.zeros((96, SEQ), np.float32)
    cos96[64:72] = np.cos(ang_r).T
    cos96[72:80] = np.cos(ang_r).T
    cos96[80:88] = np.cos(ang_c).T
    cos96[88:96] = np.cos(ang_c).T
    sin96[64:72] = np.sin(ang_r).T
    sin96[72:80] = np.sin(ang_r).T
    sin96[80:88] = np.sin(ang_c).T
    sin96[88:96] = np.sin(ang_c).T
    c["cos96"] = cos96
    c["sin96"] = sin96
    m = np.arange(64)
    ang = 2 * np.pi * np.outer(m, m) / 64.0
    wcs = np.zeros((256, 512), np.float32)
    for g in range(4):
        wcs[g * 64:(g + 1) * 64, g * 64:(g + 1) * 64] = np.cos(ang)
        wcs[g * 64:(g + 1) * 64, 256 + g * 64:256 + (g + 1) * 64] = np.sin(ang)
    c["wcs"] = wcs
    c.update(_dft_tables())
    kl = (np.outer(np.arange(CTX), np.arange(CTX)) % CTX).astype(np.float64)
    a = 2 * np.pi * kl / CTX
    c["dftc_c"] = np.cos(a).astype(ml_dtypes.bfloat16)
    c["dftsn_c"] = (-np.sin(a)).astype(ml_dtypes.bfloat16)
    def invcnt(L):
        out = np.zeros((128, 2, L), np.float32)
        tt = np.arange(L)
        for g, w in enumerate((2, 4, 8, 16)):
            lo = np.clip(tt - w // 2, 0, L)
            hi = np.clip(tt + w - w // 2, 0, L)
            ic = 1.0 / (hi - lo).astype(np.float32)
            out[(g % 2) * 64:(g % 2) * 64 + 64, g // 2, :] = ic[None, :]
        return out
    c["invc"] = invcnt(SEQ)
    c["invc_c"] = invcnt(CTX)
    sel = np.zeros((8, 8, 128), np.float32)
    for e in range(8):
        sel[e, e, :] = 1.0
    c["sel"] = sel
    c["ident"] = np.eye(128, dtype=np.float32)
    return c


def _na_classes():
    return None


def _na_bias_tiles(rpb):
    H = 4
    qc = np.arange(64)
    win_c0 = np.clip(qc - 8, 0, 48)
    kc = np.arange(64)
    ok = (kc[:, None] >= win_c0[None, :]) & (kc[:, None] < win_c0[None, :] + 16)
    off = np.clip(kc[:, None] - qc[None, :] + 15, 0, 30)
    tiles = []

    def tile_for(pb, chunk):
        tl = np.full((H, 128, 128), NEG, np.float32)
        for a in range(2):
            for b in range(2):
                krow = 2 * chunk + a
                qrow = 2 * pb + b
                r0 = min(max(qrow - 4, 0), 120)
                if not (r0 <= krow < r0 + 8):
                    continue
                dr = krow - qrow + 7
                blk = np.where(ok[None], rpb[:, dr, :][:, off], NEG)
                tl[:, a * 64:(a + 1) * 64, b * 64:(b + 1) * 64] = blk
        return tl

    for cidx in range(5):
        tiles.append(tile_for(10, 10 - 2 + cidx))
    for pb in (0, 1):
        for ch in range(4):
            tiles.append(tile_for(pb, ch))
    for pb in (62, 63):
        for ch in range(60, 64):
            tiles.append(tile_for(pb, ch))
    arr = np.stack(tiles, 0)
    return np.ascontiguousarray(arr.transpose(2, 0, 1, 3))


def _pack_params(inp, l):
    pk = np.zeros((128, 80), np.float32)
    pk[:, 0:48] = inp["b_ada"][l].reshape(48, 128).T
    pk[:, 48:56] = inp["g_mix"][l].reshape(8, 128).T
    pk[:, 56:64] = inp["g_ffn"][l].reshape(8, 128).T
    pk[:, 64:66] = inp["g_cq"][l].reshape(2, 128).T
    pk[:, 66] = inp["g_ckv"][l]
    pk[:64, 67] = inp["g_mla_qn"][l]
    pk[64:96, 67] = inp["g_mla_qr"][l]
    pk[:64, 68] = inp["g_mla_kn"][l]
    pk[64:96, 68] = inp["g_mla_kr"][l]
    pk[:, 69:71] = inp["pool_scale"][l].reshape(2, 128).T
    pk[:, 71] = np.tile(inp["g_na_q"][l], 2)
    pk[:, 72] = np.tile(inp["g_na_k"][l], 2)
    return pk


def build_program(stop_after=None, debug=False):
    nc = bass.Bass("TRN2", target_bir_lowering=False)
    kb = KB(nc)
    dkind = "ExternalOutput" if debug else "Internal"

    def ein(name, shape, dt=F32):
        return kb.dram(shape, dt, name, kind="ExternalInput")

    x_in = ein("xT", [D, TT])
    cc_in = ein("cc", [128, 16])
    pk_in = ein("pk", [DEPTH, 128, 80])
    w_ada = ein("w_ada", [DEPTH, D, 6 * D])
    w_in = ein("w_in", [DEPTH, D, IN_W])
    w_out = ein("w_out", [DEPTH, D, D])
    w_uq = ein("w_uq", [DEPTH, 256, 384])
    w_ukv = ein("w_ukv", [DEPTH, 128, 512])
    w_fo = ein("w_fourier", [DEPTH, 256, 256])
    w_pl = ein("w_poolbd", [DEPTH, 2, 128, 128])
    nab = ein("na_bias", [DEPTH, 128, 21 * 4 * 128])
    w1d = ein("w1_dense", [1, D, D_FF])
    w3d = ein("w3_dense", [1, D, D_FF])
    w2d = ein("w2_dense", [1, D_FF, D])
    w_rt = ein("w_router", [1, D, NE])
    w1m = ein("w1_moe", [1, NE, D, D_FFE])
    w3m = ein("w3_moe", [1, NE, D, D_FFE])
    w2m = ein("w2_moe", [1, NE, D_FFE, D])
    c_ind96 = ein("ind96", [96, 96])
    c_ind128 = ein("ind128", [128, 128])
    c_r96t = ein("r96t", [96, 96])
    c_cos = ein("cos96", [96, SEQ])
    c_sin = ein("sin96", [96, SEQ])
    c_wcs = ein("wcs", [256, 512])
    c_dftc = ein("dftc", [SEQ, SEQ], BF16)
    c_dfts = ein("dftsn", [SEQ, SEQ], BF16)
    c_dftc_c = ein("dftc_c", [CTX, CTX], BF16)
    c_dfts_c = ein("dftsn_c", [CTX, CTX], BF16)
    c_invc = ein("invc", [128, 2, SEQ])
    c_invc_c = ein("invc_c", [128, 2, CTX])
    c_sel = ein("sel", [8, 8 * 128])
    c_ident = ein("ident", [128, 128])
    flg_in = ein("flg", [128, 2])
    out_d = kb.dram([D, HALF], F32, "outT", kind="ExternalOutput")

    xs = [x_in] + [kb.dram([D, TT], F32, f"xs{i}", kind=dkind) for i in range(1, 4)]
    uT = kb.dram([IN_W, TT], F32, "uT", kind=dkind)
    qT = kb.dram([4, 96, TT], BF16, "qT", kind=dkind)
    kT = kb.dram([4, 96, TT], BF16, "kT", kind=dkind)
    oT = kb.dram([D, TT], BF16, "oT", kind=dkind)
    fT = kb.dram([256, TT], BF16, "fT", kind=dkind)
    NBK = len(BLOCKS)
    xs_t = [[T(None, f"xs{i}_{b}") for b in range(NBK)] for i in range(4)]
    out_t = [T(None) for _ in range(NBK)]
    u_t = [[T(None) for _ in range(NBK)] for _ in range(14)]
    q_t = [[T(None) for _ in range(NBK)] for _ in range(4)]
    k_t = [[T(None) for _ in range(NBK)] for _ in range(4)]
    o_t = [[T(None) for _ in range(NBK)] for _ in range(8)]
    f_t = [T(None) for _ in range(NBK)]

    def u_tiles(r0, r1, bi):
        return [u_t[oc][bi] for oc in range(r0 // 128, (r1 - 1) // 128 + 1)]

    ind96 = kb.sb([96, 96]); ind128 = kb.sb([128, 128]); r96t = kb.sb([96, 96])
    onesD = kb.sb([128, 128]); ones256 = kb.sb([128, 128]); ones128 = kb.sb([128, 128]); onesf = kb.sb([128, 128])
    ident = kb.sb([128, 128]); sel = kb.sb([8, 8 * 128]); identb = kb.sb([128, 128], BF16)
    cc = kb.sb([128, 16]); sc = kb.sb([128, 8, 2]); flg = kb.sb([128, 2])
    pk = kb.sb([128, 80]); mod = kb.sb([128, 48, 2]); gm1 = kb.sb([128, 8, 2]); gm2 = kb.sb([128, 8, 2])
    NAR = 50560
    arena = kb.sb([128, NAR], F32, "arena")
    pst = [kb.ps([128, 512], F32, f"psb{i}") for i in range(8)]
    nps = RR(pst)

    class Carver:
        def __init__(self, lo, hi, dtype):
            self.lo, self.hi, self.dtype = lo, hi, dtype
            self.off = 0

        def reset(self):
            self.off = 0

        def get(self, shape, dtype=None):
            dtype = dtype or self.dtype
            esz = 2 if dtype == BF16 else 4
            osz = 2 if self.dtype == BF16 else 4
            n = int(np.prod(shape))
            byte0 = self.off * osz
            byte0 = (byte0 + 3) // 4 * 4
            nbytes = (n * esz + 3) // 4 * 4
            w0 = self.lo + byte0 // 4
            w1 = w0 + nbytes // 4
            assert w1 <= self.hi, (w1, self.hi)
            self.off = (byte0 + nbytes) // osz
            ap = arena.ap[:, w0:w1]
            if dtype == BF16:
                ap = ap.bitcast(BF16)[:, 0:n]
            if len(shape) == 2:
                ap = ap.rearrange("p (a b) -> p a b", a=shape[0])
            elif len(shape) == 3:
                ap = ap.rearrange("p (a b c) -> p a b c", a=shape[0], b=shape[1])
            elif len(shape) == 4:
                ap = ap.rearrange("p (a b c d) -> p a b c d", a=shape[0], b=shape[1], c=shape[2])
            return T(ap)

    cb = Carver(0, 33792, BF16)
    cf = Carver(33792, NAR, F32)
    cA = Carver(0, NAR, F32)

    for dst, src in ((ind96, c_ind96), (ind128, c_ind128), (r96t, c_r96t), (ident, c_ident), (sel, c_sel), (cc, cc_in), (flg, flg_in)):
        kb.dma("sp", dst[:], src[:], [], [dst])
    kb.op("pool", lambda e: e.memset(onesD[:], 1.0 / D), [], [onesD])
    kb.op("pool", lambda e: e.memset(ones256[:], 1.0 / 256), [], [ones256])
    kb.op("pool", lambda e: e.memset(ones128[:], 1.0 / 128), [], [ones128])
    kb.op("pool", lambda e: e.memset(onesf[:], 1.0), [], [onesf])
    kb.op("act", lambda e: e.activation(sc[:].rearrange("p k j -> p (k j)"), cc[:], AF.Silu), [cc], [sc])
    kb.op("dve", lambda e: e.tensor_copy(identb[:], ident[:]), [ident], [identb])

    evac_i = [0]

    def evac(out_ap, in_ap, reads, writes):
        evac_i[0] += 1
        if evac_i[0] % 2:
            kb.op("act", lambda e: e.copy(out_ap, in_ap), reads, writes)
        else:
            kb.op("dve", lambda e: e.tensor_copy(out_ap, in_ap), reads, writes)

    def adaln(l):
        cf.reset()
        wbs = RR([cf.get([8, 128]) for _ in range(3)])
        kb.dma("sp", pk[:], pk_in[l], [], [pk])
        wv = w_ada[l].rearrange("(k p) n -> p k n", p=128)
        for oc in range(48):
            wb = wbs()
            kb.dma("sp", wb[:], wv[:, :, oc * 128:(oc + 1) * 128], [], [wb])
            p = nps()
            kb.mm(p, p[:, 0:2], [(wb[:, k, :], sc[:, k, :]) for k in range(8)], [wb, sc])
            kb.op("dve", lambda e, p=p, oc=oc: e.tensor_scalar(mod[:, oc, :], p[:, 0:2], pk[:, oc:oc + 1], None, ALU.add),
                  [p, pk], [mod])
        for k in range(8):
            kb.op("dve", lambda e, k=k: e.tensor_scalar(gm1[:, k, :], mod[:, 8 + k, :], 1.0, pk[:, 48 + k:49 + k], ALU.add, ALU.mult),
                  [mod, pk], [gm1])
            kb.op("dve", lambda e, k=k: e.tensor_scalar(gm2[:, k, :], mod[:, 32 + k, :], 1.0, pk[:, 56 + k:57 + k], ALU.add, ALU.mult),
                  [mod, pk], [gm2])
        kb.barrier()

    def norm_mod(xb, sq, rs, w, j, gm, sh0, hb, hf=None):
        for k in range(8):
            kb.op("act", lambda e, k=k: e.activation(sq[:, k, :w], xb[:, k, :w], AF.Square), [xb], [sq])
        p = nps()
        kb.mm(p, p[:, :w], [(onesD[:, :], sq[:, k, :w]) for k in range(8)], [onesD, sq])
        kb.op("act", lambda e: e.activation(rs[:, :w], p[:, :w], AF.Sqrt, bias=EPS, scale=1.0), [p], [rs])
        kb.op("dve", lambda e: e.reciprocal(rs[:, :w], rs[:, :w]), [rs], [rs])
        for k in range(8):
            kb.op("dve", lambda e, k=k: e.tensor_tensor(sq[:, k, :w], xb[:, k, :w], rs[:, :w], ALU.mult), [xb, rs], [sq])
            kb.op("act", lambda e, k=k: e.activation(hb[:, k, :w], sq[:, k, :w], AF.Identity,
                                                      bias=mod[:, sh0 + k, j:j + 1], scale=gm[:, k, j:j + 1]),
                  [sq, mod, gm], [hb])
            if hf is not None:
                kb.op("act", lambda e, k=k: e.activation(hf[:, k, :w], sq[:, k, :w], AF.Identity,
                                                          bias=mod[:, sh0 + k, j:j + 1], scale=gm[:, k, j:j + 1]),
                      [sq, mod, gm], [hf])

    def inproj(l, xi, vna, nblocks):
        cb.off = vna_end
        cf.reset()
        win = cb.get([8, IN_W])
        hbs = RR([cb.get([8, 512]) for _ in range(2)])
        xbs = RR([cf.get([8, 512]) for _ in range(2)])
        sq = cf.get([8, 512])
        rs = cf.get([512])
        ubs = RR([cf.get([512]) for _ in range(2)])
        kb.dma("pool", win[:], w_in[l].rearrange("(k p) n -> p k n", p=128), [], [win])
        for bi in range(nblocks):
            c0, w, isc = BLOCKS[bi]
            j = 1 if isc else 0
            xb = xbs()
            kb.dma("sp", xb[:, :, :w], xs[xi].rearrange("(k p) t -> p k t", p=128)[:, :, c0:c0 + w], [xs_t[xi][bi]], [xb])
            hb = hbs()
            norm_mod(xb, sq, rs, w, j, gm1, 0, hb)
            for oc in range(14):
                r0 = oc * 128
                m = min(128, IN_W - r0)
                p = nps()
                kb.mm(p, p[:m, :w], [(win[:, k, r0:r0 + m], hb[:, k, :w]) for k in range(8)], [win, hb])
                ub = ubs()
                evac(ub[:m, :w], p[:m, :w], [p], [ub])
                kb.dma("sp", uT[r0:r0 + m, c0:c0 + w], ub[:m, :w], [ub], [u_t[oc][bi]])
            for s in range(w // 128):
                p = nps()
                kb.mm(p, p[:, 0:256], [(hb[:, k, s * 128:(s + 1) * 128], win[:, k, 1440:1696]) for k in range(8)], [win, hb])
                ch = c0 // 128 + s
                evac(vna[:, ch, :, 0:64], p[:, 0:256].rearrange("p (h d) -> p h d", h=4), [p], [vna])
        kb.barrier()

    cb.reset()
    vna = cb.get([66, 4, 65])
    vml = cb.get([66, 4, 65])
    ckvn = cb.get([TT])
    vna_end = cb.off
    kb.op("pool", lambda e: e.memset(vna[:], 1.0), [], [vna])
    kb.op("pool", lambda e: e.memset(vml[:], 1.0), [], [vml])

    nqT = kb.dram([256, TT], BF16, "nqT", kind=dkind)
    nkT = kb.dram([256, TT], BF16, "nkT", kind=dkind)
    nq_t = [T(None) for _ in range(NBK)]
    nk_t = [T(None) for _ in range(NBK)]
    psS = RR(pst[0:4])
    psO = RR(pst[4:6])
    psM = RR(pst[6:8])

    def uview(r0, r1):
        return uT[r0:r1, :].rearrange("(k p) t -> p k t", p=128)

    def norm96_rope(raw, wk, w, gcol, c0, rope, dst_ap, dst_t, cosb=None, sinb=None):
        sq, rs, qn, t1 = wk
        kb.op("act", lambda e: e.activation(sq[:96, :w], raw[:96, :w], AF.Square), [raw], [sq])
        p = nps()
        kb.mm(p, p[:96, :w], [(ind96[:, :], sq[:96, :w])], [ind96, sq])
        kb.op("act", lambda e: e.activation(rs[:96, :w], p[:96, :w], AF.Sqrt, bias=EPS, scale=1.0), [p], [rs])
        kb.op("dve", lambda e: e.reciprocal(rs[:96, :w], rs[:96, :w]), [rs], [rs])
        ob = obs()
        if rope:
            kb.op("dve", lambda e: e.scalar_tensor_tensor(qn[:96, :w], raw[:96, :w], gcol, rs[:96, :w], ALU.mult, ALU.mult),
                  [raw, rs, pk], [qn])
            p2 = nps()
            kb.mm(p2, p2[:96, :w], [(r96t[:, :], qn[:96, :w])], [r96t, qn])
            kb.op("pool", lambda e: e.tensor_tensor(t1[:96, :w], qn[:96, :w], cosb[:96, :w], ALU.mult), [qn, cosb], [t1])
            kb.op("dve", lambda e: e.tensor_tensor(sq[:96, :w], p2[:96, :w], sinb[:96, :w], ALU.mult), [p2, sinb], [sq])
            kb.op("dve", lambda e: e.tensor_tensor(ob[:96, :w], t1[:96, :w], sq[:96, :w], ALU.add), [t1, sq], [ob])
        else:
            kb.op("dve", lambda e: e.scalar_tensor_tensor(ob[:96, :w], raw[:96, :w], gcol, rs[:96, :w], ALU.mult, ALU.mult),
                  [raw, rs, pk], [ob])
        kb.dma("pool", dst_ap, ob[:96, :w], [ob], [dst_t])

    obs = None

    def qkprep(l, nblocks):
        nonlocal obs
        cb.off = vna_end
        cf.reset()
        wuq = cb.get([2, 384])
        wukv = cb.get([4, 128])
        cqn = cb.get([2, 512])
        nob = RR([cb.get([2, 512]) for _ in range(2)])
        obs = RR([cb.get([512]) for _ in range(4)])
        wks = RR([[cf.get([512]) for _ in range(4)] for _ in range(2)])
        raws = RR([cf.get([512]) for _ in range(3)])
        xfs = RR([cf.get([2, 512]) for _ in range(2)])
        sq2s = RR([cf.get([2, 512]) for _ in range(2)])
        rs2s = RR([cf.get([512]) for _ in range(2)])
        cbs = RR([cf.get([512]) for _ in range(2)])
        sbs_ = RR([cf.get([512]) for _ in range(2)])
        kb.dma("pool", wuq[:], w_uq[l].rearrange("(k p) n -> p k n", p=128), [], [wuq])
        kb.dma("pool", wukv[:], w_ukv[l].rearrange("p (h n) -> p h n", h=4), [], [wukv])
        for bi in range(nblocks):
            c0, w, isc = BLOCKS[bi]
            cosb = sinb = None
            if not isc:
                cosb, sinb = cbs(), sbs_()
                kb.dma("sp", cosb[:96, :w], c_cos[:, c0:c0 + w], [], [cosb])
                kb.dma("sp", sinb[:96, :w], c_sin[:, c0:c0 + w], [], [sinb])
            xf, sq2, rs2 = xfs(), sq2s(), rs2s()
            kb.dma("sp", xf[:, :, :w], uview(0, 256)[:, :, c0:c0 + w], u_tiles(0, 256, bi), [xf])
            for k in range(2):
                kb.op("act", lambda e, k=k: e.activation(sq2[:, k, :w], xf[:, k, :w], AF.Square), [xf], [sq2])
            p = nps()
            kb.mm(p, p[:, :w], [(ones256[:, :], sq2[:, k, :w]) for k in range(2)], [ones256, sq2])
            kb.op("act", lambda e, p=p: e.activation(rs2[:, :w], p[:, :w], AF.Sqrt, bias=EPS, scale=1.0), [p], [rs2])
            kb.op("dve", lambda e: e.reciprocal(rs2[:, :w], rs2[:, :w]), [rs2], [rs2])
            for k in range(2):
                kb.op("dve", lambda e, k=k: e.scalar_tensor_tensor(cqn[:, k, :w], xf[:, k, :w], pk[:, 64 + k:65 + k], rs2[:, :w],
                                                                    ALU.mult, ALU.mult), [xf, rs2, pk], [cqn])
            for h in range(4):
                p = nps()
                kb.mm(p, p[:96, :w], [(wuq[:, k, h * 96:(h + 1) * 96], cqn[:, k, :w]) for k in range(2)], [wuq, cqn])
                raw = raws()
                kb.op("act", lambda e, p=p, raw=raw: e.copy(raw[:96, :w], p[:96, :w]), [p], [raw])
                norm96_rope(raw, wks(), w, pk[:96, 67:68], c0, not isc, qT[h, :, c0:c0 + w], q_t[h][bi], cosb, sinb)
            xf, sq2, rs2 = xfs(), sq2s(), rs2s()
            kb.dma("sp", xf[:, 0, :w], uT[256:384, c0:c0 + w], u_tiles(256, 384, bi), [xf])
            kb.op("act", lambda e: e.activation(sq2[:, 0, :w], xf[:, 0, :w], AF.Square), [xf], [sq2])
            p = nps()
            kb.mm(p, p[:, :w], [(ones128[:, :], sq2[:, 0, :w])], [ones128, sq2])
            kb.op("act", lambda e, p=p: e.activation(rs2[:, :w], p[:, :w], AF.Sqrt, bias=EPS, scale=1.0), [p], [rs2])
            kb.op("dve", lambda e: e.reciprocal(rs2[:, :w], rs2[:, :w]), [rs2], [rs2])
            kb.op("dve", lambda e: e.scalar_tensor_tensor(ckvn[:, c0:c0 + w], xf[:, 0, :w], pk[:, 66:67], rs2[:, :w],
                                                           ALU.mult, ALU.mult), [xf, rs2, pk], [ckvn])
            for h in range(4):
                p = nps()
                kb.mm(p, p[:64, :w], [(wukv[:, h, 0:64], ckvn[:, c0:c0 + w])], [wukv, ckvn])
                raw = raws()
                kb.op("act", lambda e, p=p, raw=raw: e.copy(raw[:64, :w], p[:64, :w]), [p], [raw])
                kb.dma("sp", raw[64:96, :w], uT[384:416, c0:c0 + w], u_tiles(384, 416, bi), [raw])
                norm96_rope(raw, wks(), w, pk[:96, 68:69], c0, not isc, kT[h, :, c0:c0 + w], k_t[h][bi], cosb, sinb)
            for s in range(w // 128):
                p = nps()
                kb.mm(p, p[:, 0:256].rearrange("p (h d) -> p h d", h=4),
                      [(ckvn[:, c0 + s * 128:c0 + (s + 1) * 128], wukv[:, :, 64:128])], [wukv, ckvn])
                evac(vml[:, c0 // 128 + s, :, 0:64], p[:, 0:256].rearrange("p (h d) -> p h d", h=4), [p], [vml])
            for (r0, gc, dstT, dtl) in ((928, 71, nqT, nq_t), (1184, 72, nkT, nk_t)):
                xf, sq2, rs2 = xfs(), sq2s(), rs2s()
                kb.dma("sp", xf[:, :, :w], uview(r0, r0 + 256)[:, :, c0:c0 + w], u_tiles(r0, r0 + 256, bi), [xf])
                no = nob()
                for k in range(2):
                    kb.op("act", lambda e, k=k: e.activation(sq2[:, k, :w], xf[:, k, :w], AF.Square), [xf], [sq2])
                    p = nps()
                    kb.mm(p, p[:, :w], [(ind128[:, :], sq2[:, k, :w])], [ind128, sq2])
                    kb.op("act", lambda e, p=p: e.activation(rs2[:, :w], p[:, :w], AF.Sqrt, bias=EPS, scale=1.0), [p], [rs2])
                    kb.op("dve", lambda e: e.reciprocal(rs2[:, :w], rs2[:, :w]), [rs2], [rs2])
                    kb.op("dve", lambda e, k=k, no=no, gc=gc: e.scalar_tensor_tensor(no[:, k, :w], xf[:, k, :w], pk[:, gc:gc + 1],
                                                                                      rs2[:, :w], ALU.mult, ALU.mult),
                          [xf, rs2, pk], [no])
                kb.dma("pool", dstT.rearrange("(k p) t -> p k t", p=128)[:, :, c0:c0 + w], no[:, :, :w], [no], [dtl[bi]])
        kb.barrier()

    fin = {}

    def attn_finish(pO, w, dst_ap, src_view, dst_ts):
        osb = fin["osb"]()
        rec = fin["rec"]
        ob = fin["ob"]()
        kb.op("act", lambda e: e.copy(osb[:65, :w], pO[:65, :w]), [pO], [osb])
        kb.op("dve", lambda e: e.reciprocal(rec[64:65, :w], osb[64:65, :w]), [osb], [rec])
        pB = psM()
        kb.mm(pB, pB[:64, :w], [(onesf[64:65, 0:64], rec[64:65, :w])], [onesf, rec])
        kb.op("dve", lambda e: e.tensor_tensor(ob[:64, :w], osb[:64, :w], pB[:64, :w], ALU.mult), [osb, pB], [ob])
        kb.dma("pool", dst_ap, src_view(ob), [ob], dst_ts)

    def mla_attn(l, nblocks):
        cb.off = vna_end
        cf.reset()
        kbuf = cb.get([TT])
        qbs = RR([cb.get([512]) for _ in range(2)])
        pts = RR([cb.get([512]) for _ in range(4)])
        fin["ob"] = RR([cb.get([512]) for _ in range(2)])
        fin["osb"] = RR([cf.get([512]) for _ in range(2)])
        fin["rec"] = cf.get([512])
        for h in range(4):
            kb.dma("sp", kbuf[:96, :], kT[h], [k_t[h][b] for b in range(NBK)], [kbuf])
            for bi in range(nblocks):
                c0, w, isc = BLOCKS[bi]
                qb = qbs()
                kb.dma("sp", qb[:96, :w], qT[h, :, c0:c0 + w], [q_t[h][bi]], [qb])
                chunks = [64, 65] if isc else list(range(66))
                pO = psO()
                n = len(chunks)
                LA = 2
                ptl = {}
                for i in range(n + LA):
                    if i < n:
                        kc = chunks[i]
                        pS = psS()
                        kb.mm(pS, pS[:, :w], [(kbuf[:96, kc * 128:(kc + 1) * 128], qb[:96, :w])], [kbuf, qb])
                        pt = pts()
                        kb.op("act", lambda e: e.activation(pt[:, :w], pS[:, :w], AF.Exp, scale=MLA_SCALE), [pS], [pt])
                        ptl[i] = pt
                    if i >= LA:
                        ii = i - LA
                        pt = ptl.pop(ii)
                        kb.mm(pO, pO[:65, :w], [(vml[:, chunks[ii], h, :], pt[:, :w])], [vml, pt], start=(ii == 0), stop=(ii == n - 1))
                attn_finish(pO, w, oT[h * 64:(h + 1) * 64, c0:c0 + w], lambda ob, w=w: ob[:64, :w], [o_t[h // 2][bi]])
        kb.barrier()

    def na_attn(l, nblocks):
        cb.off = vna_end
        cf.reset()
        fin["osb"] = RR([cf.get([512]) for _ in range(2)])
        fin["rec"] = cf.get([512])
        bias = cb.get([21, 4, 128])
        fin["ob"] = RR([cb.get([512]) for _ in range(2)])
        kctx = cb.get([2, 256])
        qns = RR([cb.get([2, 128]) for _ in range(2)])
        kns = RR([cb.get([2, 640]) for _ in range(2)])
        pts = RR([cb.get([896]) for _ in range(3)])
        nqv = nqT.rearrange("(k p) t -> p k t", p=128)
        nkv = nkT.rearrange("(k p) t -> p k t", p=128)
        kb.dma("pool", bias[:].rearrange("p a h q -> p (a h q)"), nab[l], [], [bias])
        kb.op("dve", lambda e: e.tensor_scalar(bias[:].rearrange("p a h q -> p (a h q)"), bias[:].rearrange("p a h q -> p (a h q)"),
                                               1.0 / NA_SCALE, None, ALU.mult), [bias], [bias])
        kb.dma("sp", kctx[:], nkv[:, :, SEQ:TT], [nk_t[16]], [kctx])
        npb = 64 + (0 if nblocks == 16 else 2)
        pbinfo = {}

        def setup_pb(pb):
            if pb < 64:
                q0 = pb * 128
                if pb < 2:
                    loc, base = [0, 1, 2, 3], 5 + 4 * pb
                elif pb >= 62:
                    loc, base = [60, 61, 62, 63], 13 + 4 * (pb - 62)
                else:
                    loc, base = list(range(pb - 2, pb + 3)), 0
            else:
                q0 = SEQ + (pb - 64) * 128
                loc, base = [], 0
            nl = len(loc)
            qn = qns()
            kb.dma("sp", qn[:], nqv[:, :, q0:q0 + 128], [nq_t[q0 // 512]], [qn])
            kn = kns()
            if nl:
                kc0 = loc[0] * 128
                kb.dma("sp", kn[:, :, 0:nl * 128], nkv[:, :, kc0:kc0 + nl * 128],
                       [nk_t[b_] for b_ in range(kc0 // 512, (kc0 + nl * 128 - 1) // 512 + 1)], [kn])
            pbinfo[pb] = dict(q0=q0, loc=loc, base=base, nl=nl, qn=qn, kn=kn, pO=psO())

        def stage_a(pb, h):
            if h == 0:
                setup_pb(pb)
            I = pbinfo[pb]
            loc, base, nl, qn, kn = I["loc"], I["base"], I["nl"], I["qn"], I["kn"]
            nA = min(nl, 4)
            k_, po = h // 2, (h % 2) * 64
            pA = psS()
            for i in range(nA):
                kb.mm(pA, pA[:, i * 128:(i + 1) * 128], [(kn[po:po + 64, k_, i * 128:(i + 1) * 128], qn[po:po + 64, k_, :]),
                                                         (identb[:, :], bias[:, base + i, h, :])], [kn, qn, identb, bias])
            pB = psS()
            if nl == 5:
                kb.mm(pB, pB[:, 0:128], [(kn[po:po + 64, k_, 512:640], qn[po:po + 64, k_, :]),
                                         (identb[:, :], bias[:, base + 4, h, :])], [kn, qn, identb, bias])
            for c in range(2):
                kb.mm(pB, pB[:, 128 + c * 128:256 + c * 128], [(kctx[po:po + 64, k_, c * 128:(c + 1) * 128], qn[po:po + 64, k_, :])],
                      [kctx, qn])
            pt = pts()
            if nA:
                kb.op("act", lambda e: e.activation(pt[:, 0:nA * 128], pA[:, 0:nA * 128], AF.Exp, scale=NA_SCALE), [pA], [pt])
            if nl == 5:
                kb.op("act", lambda e: e.activation(pt[:, 512:896], pB[:, 0:384], AF.Exp, scale=NA_SCALE), [pB], [pt])
            else:
                kb.op("act", lambda e: e.activation(pt[:, 640:896], pB[:, 128:384], AF.Exp, scale=NA_SCALE), [pB], [pt])
            I[("pt", h)] = pt

        def stage_b(pb, h):
            I = pbinfo[pb]
            loc, nl, pO, q0 = I["loc"], I["nl"], I["pO"], I["q0"]
            pt = I.pop(("pt", h))
            for i in range(nl):
                kb.mm(pO, pO[:65, h * 128:(h + 1) * 128], [(vna[:, loc[i], h, :], pt[:, i * 128:(i + 1) * 128])], [vna, pt],
                      start=(i == 0), stop=False)
            kb.mm(pO, pO[:65, h * 128:(h + 1) * 128], [(vna[:, 64, h, :], pt[:, 640:768])], [vna, pt], start=(nl == 0), stop=False)
            kb.mm(pO, pO[:65, h * 128:(h + 1) * 128], [(vna[:, 65, h, :], pt[:, 768:896])], [vna, pt], start=False, stop=True)
            if h == 3:
                bi = q0 // 512
                attn_finish(pO, 512, oT[768:1024, q0:q0 + 128].rearrange("(h d) q -> d h q", h=4),
                            lambda ob: ob[:64, 0:512].rearrange("d (h q) -> d h q", h=4), [o_t[6][bi], o_t[7][bi]])
                del pbinfo[pb]

        items = [(pb, h) for pb in range(npb) for h in range(4)]
        LA = 1
        for idx in range(len(items) + LA):
            if idx < len(items):
                stage_a(*items[idx])
            if idx >= LA:
                stage_b(*items[idx - LA])
        kb.barrier()

    def fourier(l, nblocks):
        cb.reset()
        cf.reset()
        AB = cb.get([64, 512])
        xf = cb.get([2, SEQ])
        wcs = cb.get([2, 512])
        tcs = RR([cb.get([1024]) for _ in range(3)])
        tss = RR([cb.get([1024]) for _ in range(3)])
        fbs = RR([cb.get([512]) for _ in range(4)])
        kb.dma("pool", wcs[:], c_wcs.rearrange("(k p) n -> p k n", p=128), [], [wcs])
        kb.dma("pool", xf[:], uview(416, 672)[:, :, 0:SEQ], [t for b in range(16) for t in u_tiles(416, 672, b)], [xf])
        for tc in range(64):
            p = nps()
            kb.mm(p, p[:, :], [(xf[:, k, tc * 128:(tc + 1) * 128], wcs[:, k, :]) for k in range(2)], [xf, wcs])
            evac(AB[:, tc, :], p[:, :], [p], [AB])
        sc_l = float(1.0 / np.sqrt(SEQ * 64.0))
        for kb2 in range(8):
            pF = [[pst[0], pst[1]], [pst[2], pst[3]]]
            for lc in range(64):
                tcn = tcs()
                tsn = tss()
                kb.dma("sp", tcn[:], c_dftc[lc * 128:(lc + 1) * 128, kb2 * 1024:(kb2 + 1) * 1024], [], [tcn])
                kb.dma("pool", tsn[:], c_dfts[lc * 128:(lc + 1) * 128, kb2 * 1024:(kb2 + 1) * 1024], [], [tsn])
                for fc in range(2):
                    for hf in range(2):
                        kb.mm(pF[fc][hf], pF[fc][hf][:, :], [(AB[:, lc, fc * 128:(fc + 1) * 128], tcn[:, hf * 512:(hf + 1) * 512]),
                                                             (AB[:, lc, 256 + fc * 128:256 + (fc + 1) * 128], tsn[:, hf * 512:(hf + 1) * 512])],
                              [AB, tcn, tsn], start=(lc == 0), stop=(lc == 63))
            for fc in range(2):
                for hf in range(2):
                    kbk = kb2 * 2 + hf
                    fb = fbs()
                    kb.op("act", lambda e: e.activation(fb[:], pF[fc][hf][:, :], AF.Copy, scale=sc_l), [pF[fc][hf]], [fb])
                    kb.dma("sp", fT[fc * 128:(fc + 1) * 128, kbk * 512:(kbk + 1) * 512], fb[:], [fb], [f_t[kbk]])
        if nblocks == 17:
            xc = cb.get([2, CTX])
            ABc = cb.get([2, 512])
            tcc = cb.get([2, CTX])
            tsc = cb.get([2, CTX])
            kb.dma("pool", xc[:], uview(416, 672)[:, :, SEQ:TT], u_tiles(416, 672, 16), [xc])
            kb.dma("sp", tcc[:], c_dftc_c.rearrange("(k p) n -> p k n", p=128), [], [tcc])
            kb.dma("sp", tsc[:], c_dfts_c.rearrange("(k p) n -> p k n", p=128), [], [tsc])
            for tc in range(2):
                p = nps()
                kb.mm(p, p[:, :], [(xc[:, k, tc * 128:(tc + 1) * 128], wcs[:, k, :]) for k in range(2)], [xc, wcs])
                evac(ABc[:, tc, :], p[:, :], [p], [ABc])
            sc_c = float(1.0 / np.sqrt(CTX * 64.0))
            for fc in range(2):
                p = nps()
                prs = []
                for lc in range(2):
                    prs.append((ABc[:, lc, fc * 128:(fc + 1) * 128], tcc[:, lc, :]))
                    prs.append((ABc[:, lc, 256 + fc * 128:256 + (fc + 1) * 128], tsc[:, lc, :]))
                kb.mm(p, p[:, 0:CTX], prs, [ABc, tcc, tsc])
                fb = fbs()
                kb.op("act", lambda e, fb=fb, p=p: e.activation(fb[:, 0:CTX], p[:, 0:CTX], AF.Copy, scale=sc_c), [p], [fb])
                kb.dma("sp", fT[fc * 128:(fc + 1) * 128, SEQ:TT], fb[:, 0:CTX], [fb], [f_t[16]])
        wf = cb.get([2, 256])
        fls = RR([cb.get([2, 512]) for _ in range(2)])
        kb.dma("pool", wf[:], w_fo[l].rearrange("(k p) n -> p k n", p=128), [], [wf])
        for bi in range(nblocks):
            c0, w, isc = BLOCKS[bi]
            fl = fls()
            kb.dma("sp", fl[:, :, :w], fT.rearrange("(k p) t -> p k t", p=128)[:, :, c0:c0 + w], [f_t[bi]], [fl])
            for oc in range(2):
                p = nps()
                kb.mm(p, p[:, :w], [(wf[:, k, oc * 128:(oc + 1) * 128], fl[:, k, :w]) for k in range(2)], [wf, fl])
                fb = fbs()
                evac(fb[:, :w], p[:, :w], [p], [fb])
                kb.dma("sp", oT[256 + oc * 128:256 + (oc + 1) * 128, c0:c0 + w], fb[:, :w], [fb], [o_t[2 + oc][bi]])
        kb.barrier()

    def pool(l, nblocks):
        cb.reset()
        cf.reset()
        wpl = cb.get([2, 128])
        pds = RR([cb.get([2, 512]) for _ in range(2)])
        pos = RR([cb.get([512]) for _ in range(2)])
        xps = RR([cf.get([2, 528]) for _ in range(2)])
        s2 = cf.get([2, 528]); s4 = cf.get([2, 528]); s8 = cf.get([2, 528]); s16 = cf.get([2, 528])
        ivs = RR([cf.get([2, 512]) for _ in range(2)])
        tmp = cf.get([2, 512])
        kb.dma("pool", wpl[:], w_pl[l].rearrange("c p n -> p c n"), [], [wpl])
        uv = uview(672, 928)
        for bi in range(nblocks):
            c0, w, isc = BLOCKS[bi]
            xp = xps()
            lo = c0 - 8 if (not isc and bi > 0) else c0
            hi = c0 + w + 8 if (not isc and bi < 15) else c0 + w
            kb.op("pool", lambda e, xp=xp: e.memset(xp[:], 0.0), [], [xp])
            deps = []
            for b in range(max(0, bi - 1), min(NBK, bi + 2)):
                deps += u_tiles(672, 928, b)
            kb.dma("sp", xp[:, :, 8 + lo - c0:8 + hi - c0], uv[:, :, lo:hi], deps, [xp])
            n = w + 16
            kb.op("dve", lambda e, xp=xp: e.tensor_tensor(s2[:, :, 1:n], xp[:, :, 0:n - 1], xp[:, :, 1:n], ALU.add), [xp], [s2])
            kb.op("dve", lambda e: e.tensor_tensor(s4[:, :, 2:n - 1], s2[:, :, 1:n - 2], s2[:, :, 3:n], ALU.add), [s2], [s4])
            kb.op("dve", lambda e: e.tensor_tensor(s8[:, :, 4:n - 3], s4[:, :, 2:n - 5], s4[:, :, 6:n - 1], ALU.add), [s4], [s8])
            kb.op("dve", lambda e: e.tensor_tensor(s16[:, :, 8:n - 7], s8[:, :, 4:n - 11], s8[:, :, 12:n - 3], ALU.add), [s8], [s16])
            iv = ivs()
            if isc:
                kb.dma("sp", iv[:, :, :w], c_invc_c[:, :, :], [], [iv])
            else:
                kb.dma("sp", iv[:, :, :w], c_invc[:, :, c0:c0 + w], [], [iv])
            pd = pds()
            for g, sg in enumerate((s2, s4, s8, s16)):
                ch, po = g // 2, (g % 2) * 64
                kb.op("dve", lambda e, sg=sg, ch=ch, po=po, iv=iv: e.tensor_tensor(tmp[po:po + 64, ch, :w], sg[po:po + 64, ch, 8:8 + w],
                                                                                   iv[po:po + 64, ch, :w], ALU.mult), [sg, iv], [tmp])
                kb.op("dve", lambda e, ch=ch, po=po, xp=xp, pd=pd: e.tensor_tensor(pd[po:po + 64, ch, :w], tmp[po:po + 64, ch, :w],
                                                                                   xp[po:po + 64, ch, 8:8 + w], ALU.subtract),
                      [tmp, xp], [pd])
            for ch in range(2):
                p = nps()
                kb.mm(p, p[:, :w], [(wpl[:, ch, :], pd[:, ch, :w])], [wpl, pd])
                po_ = pos()
                kb.op("act", lambda e, p=p, po_=po_, ch=ch: e.activation(po_[:, :w], p[:, :w], AF.Copy, scale=pk[:, 69 + ch:70 + ch]),
                      [p, pk], [po_])
                kb.dma("sp", oT[512 + ch * 128:512 + (ch + 1) * 128, c0:c0 + w], po_[:, :w], [po_], [o_t[4 + ch][bi]])
        kb.barrier()

    def outproj(l, xi, nblocks):
        cb.reset()
        cf.reset()
        wo = cb.get([8, D])
        obs_ = RR([cb.get([8, 512]) for _ in range(2)])
        xbs = RR([cf.get([8, 512]) for _ in range(2)])
        kb.dma("pool", wo[:], w_out[l].rearrange("(k p) n -> p k n", p=128), [], [wo])
        for bi in range(nblocks):
            c0, w, isc = BLOCKS[bi]
            j = 1 if isc else 0
            ob = obs_()
            kb.dma("sp", ob[:, :, :w], oT.rearrange("(k p) t -> p k t", p=128)[:, :, c0:c0 + w], [o_t[r][bi] for r in range(8)], [ob])
            xb = xbs()
            kb.dma("sp", xb[:, :, :w], xs[xi].rearrange("(k p) t -> p k t", p=128)[:, :, c0:c0 + w], [xs_t[xi][bi]], [xb])
            for oc in range(8):
                p = nps()
                kb.mm(p, p[:, :w], [(wo[:, k, oc * 128:(oc + 1) * 128], ob[:, k, :w]) for k in range(8)], [wo, ob])
                kb.op("dve", lambda e, p=p, xb=xb, oc=oc, j=j: e.scalar_tensor_tensor(xb[:, oc, :w], p[:, :w], mod[:, 16 + oc, j:j + 1],
                                                                                       xb[:, oc, :w], ALU.mult, ALU.add),
                      [p, mod, xb], [xb])
            kb.dma("sp", xs[xi + 1].rearrange("(k p) t -> p k t", p=128)[:, :, c0:c0 + w], xb[:, :, :w], [xb], [xs_t[xi + 1][bi]])
        kb.barrier()

    def ffn(l, xi, nblocks, moe, last):
        F = D_FFE if moe else D_FF
        NF = F // 128
        NEX = NE if moe else 1
        groups = [list(range(i, min(i + 7, NF))) for i in range(0, NF, 7)]
        SBW = 2048
        cA.reset()
        R0 = cA.get([8, SBW], F32)
        r0f = R0.ap.rearrange("p k t -> p (k t)")
        xb = r0f[:, 0:4096].rearrange("p (k t) -> p k t", k=8)
        sq = r0f[:, 4096:8192].rearrange("p (k t) -> p k t", k=8)
        yv = R0.ap
        h2 = cA.get([8, SBW], BF16)
        actq = cA.get([7, SBW], BF16)
        WB = RR([(cA.get([4096], BF16), cA.get([4096], BF16)) for _ in range(3)])
        g1s = RR([cA.get([512], BF16) for _ in range(3)])
        gbc = cA.get([SBW], BF16)
        gT = cA.get([SBW], F32)
        rs = cA.get([512], F32)
        xsl = RR([cA.get([512], F32) for _ in range(4 if last else 2)])
        lg = cA.get([8], F32); eq = cA.get([8], F32); l2 = cA.get([8], F32); ex = cA.get([8], F32)
        msk = cA.get([8], F32); gt = cA.get([8], F32); sm = cA.get([8], F32)
        wr = cA.get([8, 8], F32)
        if moe:
            kb.dma("sp", wr[:], w_rt[0].rearrange("(k p) e -> p k e", p=128), [], [wr])
        split = last
        ntok = HALF if split else SEQ
        xBv = r0f[:, 8192:12288].rearrange("p (k t) -> p k t", k=8)
        sbs = [(i * SBW, [(q * 512, 512) for q in range(SBW // 512)], False) for i in range(ntok // SBW)]
        if nblocks == 17:
            sbs.append((SEQ, [(0, 256)], True))
        xin = xs[xi].rearrange("(k p) t -> p k t", p=128)
        for (s0, subs, isc) in sbs:
            j = 1 if isc else 0
            for (so, w) in subs:
                c0 = s0 + so
                bi = c0 // 512
                kb.dma("sp", xb[:, :, :w], xin[:, :, c0:c0 + w], [xs_t[xi][bi]], [R0])
                if split:
                    kb.dma("sp", xBv[:, :, :w], xin[:, :, HALF + c0:HALF + c0 + w], [xs_t[xi][bi + 8]], [R0])
                    kb.op("dve", lambda e: e.tensor_scalar(r0f[:, 0:4096], r0f[:, 0:4096], flg[:, 0:1], None, ALU.mult), [R0, flg], [R0])
                    kb.op("dve", lambda e: e.scalar_tensor_tensor(r0f[:, 0:4096], r0f[:, 8192:12288], flg[:, 1:2], r0f[:, 0:4096],
                                                                  ALU.mult, ALU.add), [R0, flg], [R0])
                for k in range(8):
                    kb.op("act", lambda e: e.activation(sq[:, k, :w], xb[:, k, :w], AF.Square), [R0], [R0])
                p = nps()
                kb.mm(p, p[:, :w], [(onesD[:, :], sq[:, k, :w]) for k in range(8)], [onesD, R0])
                kb.op("act", lambda e: e.activation(rs[:, :w], p[:, :w], AF.Sqrt, bias=EPS, scale=1.0), [p], [rs])
                kb.op("dve", lambda e: e.reciprocal(rs[:, :w], rs[:, :w]), [rs], [rs])
                for k in range(8):
                    kb.op("dve", lambda e: e.tensor_tensor(sq[:, k, :w], xb[:, k, :w], rs[:, :w], ALU.mult), [R0, rs], [R0])
                    kb.op("act", lambda e: e.activation(sq[:, k, :w], sq[:, k, :w], AF.Identity,
                                                        bias=mod[:, 24 + k, j:j + 1], scale=gm2[:, k, j:j + 1]),
                          [R0, mod, gm2], [R0])
                    kb.op("dve", lambda e: e.tensor_copy(h2[:, k, so:so + w], sq[:, k, :w]), [R0], [h2])
                if moe:
                    for t4 in range(w // 128):
                        p = psM()
                        kb.mm(p, p[:, 0:8], [(sq[:, k, t4 * 128:(t4 + 1) * 128], wr[:, k, :]) for k in range(8)], [R0, wr])
                        kb.op("act", lambda e: e.copy(lg[:, :], p[:, 0:8]), [p], [lg])
                        kb.op("dve", lambda e: e.tensor_reduce(sm[:, 0:1], lg[:, :], mybir.AxisListType.X, ALU.max), [lg], [sm])
                        kb.op("dve", lambda e: e.tensor_scalar(eq[:, :], lg[:, :], sm[:, 0:1], None, ALU.is_equal), [lg, sm], [eq])
                        kb.op("dve", lambda e: e.scalar_tensor_tensor(l2[:, :], eq[:, :], -1e30, lg[:, :], ALU.mult, ALU.add), [eq, lg], [l2])
                        kb.op("dve", lambda e: e.tensor_reduce(sm[:, 1:2], l2[:, :], mybir.AxisListType.X, ALU.max), [l2], [sm])
                        kb.op("dve", lambda e: e.tensor_scalar(msk[:, :], lg[:, :], sm[:, 1:2], None, ALU.is_ge), [lg, sm], [msk])
                        kb.op("dve", lambda e: e.tensor_scalar(sm[:, 2:3], sm[:, 0:1], -1.0, None, ALU.mult), [sm], [sm])
                        kb.op("act", lambda e: e.activation(ex[:, :], lg[:, :], AF.Exp, bias=sm[:, 2:3], scale=1.0), [lg, sm], [ex])
                        kb.op("act", lambda e: e.activation(sm[:, 3:4], sm[:, 1:2], AF.Exp, bias=sm[:, 2:3], scale=1.0), [sm], [sm])
                        kb.op("dve", lambda e: e.tensor_scalar(sm[:, 4:5], sm[:, 3:4], 1.0, None, ALU.add), [sm], [sm])
                        kb.op("dve", lambda e: e.reciprocal(sm[:, 5:6], sm[:, 4:5]), [sm], [sm])
                        kb.op("dve", lambda e: e.scalar_tensor_tensor(gt[:, :], ex[:, :], sm[:, 5:6], msk[:, :], ALU.mult, ALU.mult),
                              [ex, sm, msk], [gt])
                        p2 = psM()
                        kb.mm(p2, p2[:8, 0:128], [(gt[:, :], ident[:, :])], [gt, ident])
                        o = so + t4 * 128
                        kb.op("act", lambda e: e.copy(gT[:8, o:o + 128], p2[:8, 0:128]), [p2], [gT])
            kb.barrier()
            first_y = True
            for ex_i in range(NEX):
                if moe:
                    w1v = w1m[0, ex_i].rearrange("(k p) f -> p k f", p=128)
                    w3v = w3m[0, ex_i].rearrange("(k p) f -> p k f", p=128)
                    w2v = w2m[0, ex_i].rearrange("(c p) n -> p c n", p=128)
                    for (so, w) in subs:
                        p = psM()
                        kb.mm(p, p[:, :w], [(sel[:, ex_i * 128:(ex_i + 1) * 128], gT[:8, so:so + w])], [sel, gT])
                        kb.op("act", lambda e: e.copy(gbc[:, so:so + w], p[:, :w]), [p], [gbc])
                else:
                    w1v = w1d[0].rearrange("(k p) f -> p k f", p=128)
                    w3v = w3d[0].rearrange("(k p) f -> p k f", p=128)
                    w2v = w2d[0].rearrange("(c p) n -> p c n", p=128)
                for grp in groups:
                    f0 = grp[0] * 128
                    ncols = len(grp) * 128
                    for cc0 in range(0, ncols, 512):
                        ncc = min(512, ncols - cc0)
                        b1, b3 = WB()
                        w1t = b1.ap.rearrange("p (k f) -> p k f", k=8)
                        w3t = b3.ap.rearrange("p (k f) -> p k f", k=8)
                        kb.dma("pool", w1t[:, :, :ncc], w1v[:, :, f0 + cc0:f0 + cc0 + ncc], [], [b1])
                        kb.dma("pool", w3t[:, :, :ncc], w3v[:, :, f0 + cc0:f0 + cc0 + ncc], [], [b3])
                        for fl in range(ncc // 128):
                            fcl = cc0 // 128 + fl
                            for (so, w) in subs:
                                pa = nps()
                                kb.mm(pa, pa[:, :w], [(w1t[:, k, fl * 128:(fl + 1) * 128], h2[:, k, so:so + w]) for k in range(8)], [b1, h2])
                                pb_ = nps()
                                kb.mm(pb_, pb_[:, :w], [(w3t[:, k, fl * 128:(fl + 1) * 128], h2[:, k, so:so + w]) for k in range(8)], [b3, h2])
                                g1 = g1s()
                                kb.op("act", lambda e: e.activation(g1[:, :w], pa[:, :w], AF.Silu), [pa], [g1])
                                if moe:
                                    kb.op("dve", lambda e: e.tensor_tensor(g1[:, :w], g1[:, :w], gbc[:, so:so + w], ALU.mult), [g1, gbc], [g1])
                                kb.op("dve", lambda e: e.tensor_tensor(actq[:, fcl, so:so + w], g1[:, :w], pb_[:, :w], ALU.mult),
                                      [g1, pb_], [actq])
                    b1, b3 = WB()
                    w2t = b1.ap[:, 0:4096]
                    ng = len(grp)
                    w2pair = T(None)
                    w2a = b1.ap.rearrange("p (c n) -> p c n", n=1024)
                    w2b = b3.ap.rearrange("p (c n) -> p c n", n=1024)
                    na = min(ng, 4)
                    kb.dma("pool", w2a[:, 0:na, :], w2v[:, grp[0]:grp[0] + na, :], [], [b1])
                    if ng > 4:
                        kb.dma("pool", w2b[:, 0:ng - 4, :], w2v[:, grp[0] + 4:grp[0] + ng, :], [], [b3])
                    for oc in range(8):
                        for (so, w) in subs:
                            py = nps()
                            prs = []
                            for c in range(ng):
                                src = w2a if c < 4 else w2b
                                prs.append((src[:, c % 4, oc * 128:(oc + 1) * 128], actq[:, c, so:so + w]))
                            kb.mm(py, py[:, :w], prs, [b1, b3, actq])
                            if first_y:
                                kb.op("act", lambda e: e.copy(yv[:, oc, so:so + w], py[:, :w]), [py], [R0])
                            else:
                                kb.op("dve", lambda e: e.tensor_tensor(yv[:, oc, so:so + w], yv[:, oc, so:so + w], py[:, :w], ALU.add),
                                      [py, R0], [R0])
                    first_y = False
            for (so, w) in subs:
                c0 = s0 + so
                bi = c0 // 512
                for oc in range(8):
                    xl = xsl()
                    kb.dma("sp", xl[:, :w], xs[xi][oc * 128:(oc + 1) * 128, c0:c0 + w], [xs_t[xi][bi]], [xl])
                    if split:
                        xl2 = xsl()
                        kb.dma("sp", xl2[:, :w], xs[xi][oc * 128:(oc + 1) * 128, HALF + c0:HALF + c0 + w], [xs_t[xi][bi + 8]], [xl2])
                        kb.op("dve", lambda e: e.tensor_scalar(xl[:, :w], xl[:, :w], flg[:, 0:1], None, ALU.mult), [xl, flg], [xl])
                        kb.op("dve", lambda e: e.scalar_tensor_tensor(xl[:, :w], xl2[:, :w], flg[:, 1:2], xl[:, :w], ALU.mult, ALU.add),
                              [xl2, xl, flg], [xl])
                    kb.op("dve", lambda e: e.scalar_tensor_tensor(xl[:, :w], yv[:, oc, so:so + w], mod[:, 40 + oc, j:j + 1], xl[:, :w],
                                                                  ALU.mult, ALU.add), [R0, mod, xl], [xl])
                    if last:
                        kb.dma("sp", out_d[oc * 128:(oc + 1) * 128, c0:c0 + w], xl[:, :w], [xl], [out_t[bi]])
                    else:
                        kb.dma("sp", xs[xi + 1][oc * 128:(oc + 1) * 128, c0:c0 + w], xl[:, :w], [xl], [xs_t[xi + 1][bi]])
            kb.barrier()

    def finish():
        kb.barrier()
        kb.emit()
        return nc

    for l in range(DEPTH):
        last = l == DEPTH - 1
        nblocks = 16 if last else 17
        xi = 2 * l
        adaln(l)
        kb.op("pool", lambda e: e.memset(vna[:], 1.0), [], [vna])
        kb.op("pool", lambda e: e.memset(vml[:], 1.0), [], [vml])
        inproj(l, xi, vna, 17)
        if stop_after == f"inproj{l}":
            return finish()
        qkprep(l, 17)
        if stop_after == f"qkprep{l}":
            return finish()
        mla_attn(l, nblocks)
        if stop_after == f"mla{l}":
            return finish()
        na_attn(l, nblocks)
        if stop_after == f"na{l}":
            return finish()
        fourier(l, nblocks)
        pool(l, nblocks)
        if stop_after == f"mix{l}":
            return finish()
        outproj(l, xi, nblocks)
        if stop_after == f"outproj{l}":
            return finish()
        ffn(l, xi + 1, nblocks, moe=(l % 2 == 1), last=last)
        if stop_after == f"ffn{l}":
            return finish()
    return finish()


_CONSTS = None


def _dft_tables():
    c = {}
    k = np.arange(SEQ, dtype=np.int64)
    dc = np.empty((SEQ, SEQ), ml_dtypes.bfloat16)
    ds = np.empty((SEQ, SEQ), ml_dtypes.bfloat16)
    for r0 in range(0, SEQ, 1024):
        kl = (np.outer(k[r0:r0 + 1024], k) % SEQ).astype(np.float32) * np.float32(2 * np.pi / SEQ)
        dc[r0:r0 + 1024] = np.cos(kl).astype(ml_dtypes.bfloat16)
        ds[r0:r0 + 1024] = (-np.sin(kl)).astype(ml_dtypes.bfloat16)
    c["dftc"] = dc
    c["dftsn"] = ds
    return c


def prep_inputs(inp, cores):
    global _CONSTS
    if _CONSTS is None:
        _CONSTS = _const_tables()
    inp = {k: np.asarray(v) for k, v in inp.items()}
    shared = dict(_CONSTS)
    shared["sel"] = shared["sel"].reshape(8, 8 * 128)
    shared["pk"] = np.stack([_pack_params(inp, l) for l in range(DEPTH)], 0)
    for name in ("w_ada", "w_in", "w_out", "w_uq", "w_ukv", "w_fourier", "w1_dense", "w3_dense", "w2_dense",
                 "w_router", "w1_moe", "w3_moe", "w2_moe"):
        shared[name] = np.ascontiguousarray(inp[name], dtype=np.float32)
    bd = np.zeros((DEPTH, 2, 128, 128), np.float32)
    for l in range(DEPTH):
        for g in range(4):
            o = (g % 2) * 64
            bd[l, g // 2, o:o + 64, o:o + 64] = inp["w_pool"][l, g]
    shared["w_poolbd"] = bd
    shared["na_bias"] = np.stack([_na_bias_tiles(inp["na_rpb"][l]).reshape(128, -1) for l in range(DEPTH)], 0)
    maps = []
    for b in cores:
        m = dict(shared)
        m["xT"] = np.ascontiguousarray(np.concatenate([inp["x"][b].T, inp["ctx"][b].T], axis=1), dtype=np.float32)
        cc = np.zeros((128, 16), np.float32)
        cc[:, 0::2] = inp["c"][b].reshape(8, 128).T
        cc[:, 1::2] = inp["c_ctx"].reshape(8, 128).T
        m["cc"] = cc
        maps.append(m)
    return maps


def prep_inputs8(inp):
    base = prep_inputs(inp, list(range(4)))
    maps = []
    for i in range(NCORES):
        m = dict(base[i // 2])
        f = np.zeros((128, 2), np.float32)
        f[:, i % 2] = 1.0
        m["flg"] = f
        maps.append(m)
    return maps


_NC = None


def kernel(**inputs):
    global _NC
    if _NC is None:
        _NC = build_program()
    maps = prep_inputs8(inputs)
    res = run_bass_kernel_spmd(_NC, maps, core_ids=list(range(NCORES)))
    halves = [np.ascontiguousarray(r["outT"].T) for r in res.results]
    out = np.stack([np.concatenate([halves[2 * b], halves[2 * b + 1]], axis=0) for b in range(4)], 0)
    return out.astype(np.float32)
```

```python
import numpy as np
import ml_dtypes
from contextlib import ExitStack
import concourse.bass as bass
import concourse.mybir as mybir
from concourse.bass_utils import run_bass_kernel_spmd

F32 = mybir.dt.float32
BF16 = mybir.dt.bfloat16
AF = mybir.ActivationFunctionType
ALU = mybir.AluOpType

D = 1024
SEQ = 8192
CTX = 256
TT = SEQ + CTX
DEPTH = 2
GRID_W = 64
IN_W = 1696
D_FF = 2816
NE = 8
D_FFE = 3584
MLA_SCALE = 96 ** -0.5
NA_SCALE = 64 ** -0.5
EPS = 1e-6
NEG = -30000.0
BLOCKS = [(i * 512, 512, False) for i in range(16)] + [(SEQ, 256, True)]
NCORES = 8
HALF = SEQ // 2


class T:
    __slots__ = ("ap", "lw", "rd", "name")

    def __init__(self, ap, name=""):
        self.ap = ap
        self.lw = {}
        self.rd = {}
        self.name = name

    def __getitem__(self, idx):
        return self.ap[idx]


class _Rec:
    def __getattr__(self, name):
        def f(*a, **k):
            self.call = (name, a, k)
            return self
        return f


class KB:
    ENG = ["pe", "act", "dve", "pool", "sp"]
    NDS = 8

    def __init__(self, nc):
        self.nc = nc
        self.es = ExitStack()
        self.semobj = {}
        self.cnt = {}
        self.latest = {}
        for e in self.ENG:
            self.semobj[e] = self.es.enter_context(nc.semaphore("s_" + e))
            self.cnt[e] = 0
        self.dcnt = {}
        for q in ("sp", "pool", "act"):
            self.dcnt[q] = 0
            for i in range(self.NDS):
                self.semobj[f"d_{q}{i}"] = self.es.enter_context(nc.semaphore(f"d_{q}{i}"))
        self.prog = {e: [] for e in self.ENG}
        self.waited = {e: {} for e in self.ENG}
        self.n_alloc = 0

    def sb(self, shape, dtype=F32, name=None):
        self.n_alloc += 1
        name = name or f"sb{self.n_alloc}"
        t = self.es.enter_context(self.nc.sbuf_tensor(name, list(shape), dtype))
        return T(t, name)

    def ps(self, shape, dtype=F32, name=None):
        self.n_alloc += 1
        name = name or f"ps{self.n_alloc}"
        t = self.es.enter_context(self.nc.psum_tensor(name, list(shape), dtype))
        return T(t, name)

    def dram(self, shape, dtype=F32, name=None, kind="Internal"):
        self.n_alloc += 1
        name = name or f"dr{self.n_alloc}"
        t = self.nc.dram_tensor(name, list(shape), dtype, kind=kind)
        return t.ap()

    def _deps(self, E, reads, writes, skip_same=False):
        deps = {}

        def add(k, v):
            if skip_same and k == E:
                return
            if deps.get(k, 0) < v:
                deps[k] = v

        for t in reads:
            for k, v in t.lw.items():
                add(k, v)
        for t in writes:
            for k, v in t.lw.items():
                add(k, v)
            for k, v in t.rd.items():
                add(k, v)
        w = self.waited[E]
        out = []
        for k, v in deps.items():
            if w.get(k, 0) < v:
                w[k] = v
                out.append((k, v))
        return out

    def _commit(self, tok, reads, writes):
        k, v = tok
        self.latest[k] = v
        for t in writes:
            t.lw[k] = v
            t.rd = {}
        for t in reads:
            if t.rd.get(k, 0) < v:
                t.rd[k] = v

    def op(self, E, fn0, reads=(), writes=()):
        rec = _Rec()
        fn0(rec)
        name, a, k = rec.call

        def fn(eng):
            return getattr(eng, name)(*a, **k)

        waits = self._deps(E, reads, writes, skip_same=(E == "pe"))
        self.cnt[E] += 1
        tok = (E, self.cnt[E])
        self.prog[E].append((waits, fn, (E, 1)))
        self._commit(tok, reads, writes)

    def mm(self, out_t, out_ap, pairs, reads, start=True, stop=True):
        waits = self._deps("pe", reads, [out_t], skip_same=True)
        n = len(pairs)
        self.cnt["pe"] += 1
        tok = ("pe", self.cnt["pe"])
        for i, (l, r) in enumerate(pairs):
            st = start and i == 0
            sp = stop and i == n - 1

            def fn(pe, l=l, r=r, st=st, sp=sp):
                return pe.matmul(out_ap, l, r, start=st, stop=sp)

            self.prog["pe"].append((waits if i == 0 else [], fn, ("pe", 1) if i == n - 1 else None))
        self._commit(tok, reads, [out_t])

    def dma(self, q, out_ap, in_ap, reads=(), writes=()):
        i = self.dcnt[q]
        self.dcnt[q] += 1
        s = i % self.NDS
        val = 16 * (i // self.NDS + 1)
        key = f"d_{q}{s}"
        waits = self._deps(q, reads, writes)
        if i >= self.NDS and self.waited[q].get(key, 0) < val - 16:
            self.waited[q][key] = val - 16
            waits.append((key, val - 16))

        def fn(eng):
            src = in_ap() if callable(in_ap) else in_ap
            return eng.dma_start(out=out_ap, in_=src)

        self.prog[q].append((waits, fn, (key, 16)))
        self._commit((key, val), reads, writes)

    def barrier(self):
        for E in self.ENG:
            w = self.waited[E]
            waits = []
            for k, v in self.latest.items():
                if k != E and w.get(k, 0) < v:
                    w[k] = v
                    waits.append((k, v))
            self.prog[E].append((waits, None, None))

    def emit(self):
        nc = self.nc
        with nc.Block() as block:
            def run(eng, E):
                for waits, fn, inc in self.prog[E]:
                    for k, v in waits:
                        eng.wait_ge(self.semobj[k], v)
                    if fn is not None:
                        ins = fn(eng)
                        if inc is not None and ins is not None:
                            ins.then_inc(self.semobj[inc[0]], inc[1])

            @block.tensor
            def _(e):
                run(e, "pe")

            @block.scalar
            def _(e):
                run(e, "act")

            @block.vector
            def _(e):
                run(e, "dve")

            @block.gpsimd
            def _(e):
                run(e, "pool")

            @block.sync
            def _(e):
                run(e, "sp")
        self.es.close()


class RR:
    def __init__(self, items):
        self.items = items
        self.i = 0

    def __call__(self):
        t = self.items[self.i % len(self.items)]
        self.i += 1
        return t


def _const_tables():
    c = {}
    ind96 = np.zeros((96, 96), np.float32)
    ind96[:64, :64] = 1.0 / 64
    ind96[64:, 64:] = 1.0 / 32
    c["ind96"] = ind96
    ind128 = np.zeros((128, 128), np.float32)
    ind128[:64, :64] = 1.0 / 64
    ind128[64:, 64:] = 1.0 / 64
    c["ind128"] = ind128
    R = np.zeros((96, 96), np.float32)
    for base in (64, 80):
        for i in range(8):
            R[base + i, base + 8 + i] = -1.0
            R[base + 8 + i, base + i] = 1.0
    c["r96t"] = np.ascontiguousarray(R.T)
    t = np.arange(SEQ)
    row = (t // GRID_W).astype(np.float32)
    col = (t % GRID_W).astype(np.float32)
    inv = (1.0 / (10000.0 ** (np.arange(0, 16, 2, dtype=np.float32) / 16))).astype(np.float32)
    ang_r = row[:, None] * inv
    ang_c = col[:, None] * inv
    cos96 = np.ones((96, SEQ), np.float32)
    sin96 = np.zeros((96, SEQ), np.float32)
    cos96[64:72] = np.cos(ang_r).T
    cos96[72:80] = np.cos(ang_r).T
    cos96[80:88] = np.cos(ang_c).T
    cos96[88:96] = np.cos(ang_c).T
    sin96[64:72] = np.sin(ang_r).T
    sin96[72:80] = np.sin(ang_r).T
    sin96[80:88] = np.sin(ang_c).T
    sin96[88:96] = np.sin(ang_c).T
    c["cos96"] = cos96
    c["sin96"] = sin96
    m = np.arange(64)
    ang = 2 * np.pi * np.outer(m, m) / 64.0
    wcs = np.zeros((256, 512), np.float32)
    for g in range(4):
        wcs[g * 64:(g + 1) * 64, g * 64:(g + 1) * 64] = np.cos(ang)
        wcs[g * 64:(g + 1) * 64, 256 + g * 64:256 + (g + 1) * 64] = np.sin(ang)
    c["wcs"] = wcs
    c.update(_dft_tables())
    kl = (np.outer(np.arange(CTX), np.arange(CTX)) % CTX).astype(np.float64)
    a = 2 * np.pi * kl / CTX
    c["dftc_c"] = np.cos(a).astype(ml_dtypes.bfloat16)
    c["dftsn_c"] = (-np.sin(a)).astype(ml_dtypes.bfloat16)
    def invcnt(L):
        out = np.zeros((128, 2, L), np.float32)
        tt = np.arange(L)
        for g, w in enumerate((2, 4, 8, 16)):
            lo = np.clip(tt - w // 2, 0, L)
            hi = np.clip(tt + w - w // 2, 0, L)
            ic = 1.0 / (hi - lo).astype(np.float32)
            out[(g % 2) * 64:(g % 2) * 64 + 64, g // 2, :] = ic[None, :]
        return out
    c["invc"] = invcnt(SEQ)
    c["invc_c"] = invcnt(CTX)
    sel = np.zeros((8, 8, 128), np.float32)
    for e in range(8):
        sel[e, e, :] = 1.0
    c["sel"] = sel
    c["ident"] = np.eye(128, dtype=np.float32)
    return c


def _na_classes():
    return None


def _na_bias_tiles(rpb):
    H = 4
    qc = np.arange(64)
    win_c0 = np.clip(qc - 8, 0, 48)
    kc = np.arange(64)
    ok = (kc[:, None] >= win_c0[None, :]) & (kc[:, None] < win_c0[None, :] + 16)
    off = np.clip(kc[:, None] - qc[None, :] + 15, 0, 30)
    tiles = []

    def tile_for(pb, chunk):
        tl = np.full((H, 128, 128), NEG, np.float32)
        for a in range(2):
            for b in range(2):
                krow = 2 * chunk + a
                qrow = 2 * pb + b
                r0 = min(max(qrow - 4, 0), 120)
                if not (r0 <= krow < r0 + 8):
                    continue
                dr = krow - qrow + 7
                blk = np.where(ok[None], rpb[:, dr, :][:, off], NEG)
                tl[:, a * 64:(a + 1) * 64, b * 64:(b + 1) * 64] = blk
        return tl

    for cidx in range(5):
        tiles.append(tile_for(10, 10 - 2 + cidx))
    for pb in (0, 1):
        for ch in range(4):
            tiles.append(tile_for(pb, ch))
    for pb in (62, 63):
        for ch in range(60, 64):
            tiles.append(tile_for(pb, ch))
    arr = np.stack(tiles, 0)
    return np.ascontiguousarray(arr.transpose(2, 0, 1, 3))


def _pack_params(inp, l):
    pk = np.zeros((128, 80), np.float32)
    pk[:, 0:48] = inp["b_ada"][l].reshape(48, 128).T
    pk[:, 48:56] = inp["g_mix"][l].reshape(8, 128).T
    pk[:, 56:64] = inp["g_ffn"][l].reshape(8, 128).T
    pk[:, 64:66] = inp["g_cq"][l].reshape(2, 128).T
    pk[:, 66] = inp["g_ckv"][l]
    pk[:64, 67] = inp["g_mla_qn"][l]
    pk[64:96, 67] = inp["g_mla_qr"][l]
    pk[:64, 68] = inp["g_mla_kn"][l]
    pk[64:96, 68] = inp["g_mla_kr"][l]
    pk[:, 69:71] = inp["pool_scale"][l].reshape(2, 128).T
    pk[:, 71] = np.tile(inp["g_na_q"][l], 2)
    pk[:, 72] = np.tile(inp["g_na_k"][l], 2)
    return pk


def build_program(stop_after=None, debug=False):
    nc = bass.Bass("TRN2", target_bir_lowering=False)
    kb = KB(nc)
    dkind = "ExternalOutput" if debug else "Internal"

    def ein(name, shape, dt=F32):
        return kb.dram(shape, dt, name, kind="ExternalInput")

    x_in = ein("xT", [D, TT])
    cc_in = ein("cc", [128, 16])
    pk_in = ein("pk", [DEPTH, 128, 80])
    w_ada = ein("w_ada", [DEPTH, D, 6 * D])
    w_in = ein("w_in", [DEPTH, D, IN_W])
    w_out = ein("w_out", [DEPTH, D, D])
    w_uq = ein("w_uq", [DEPTH, 256, 384])
    w_ukv = ein("w_ukv", [DEPTH, 128, 512])
    w_fo = ein("w_fourier", [DEPTH, 256, 256])
    w_pl = ein("w_poolbd", [DEPTH, 2, 128, 128])
    nab = ein("na_bias", [DEPTH, 128, 21 * 4 * 128])
    w1d = ein("w1_dense", [1, D, D_FF])
    w3d = ein("w3_dense", [1, D, D_FF])
    w2d = ein("w2_dense", [1, D_FF, D])
    w_rt = ein("w_router", [1, D, NE])
    w1m = ein("w1_moe", [1, NE, D, D_FFE])
    w3m = ein("w3_moe", [1, NE, D, D_FFE])
    w2m = ein("w2_moe", [1, NE, D_FFE, D])
    c_ind96 = ein("ind96", [96, 96])
    c_ind128 = ein("ind128", [128, 128])
    c_r96t = ein("r96t", [96, 96])
    c_cos = ein("cos96", [96, SEQ])
    c_sin = ein("sin96", [96, SEQ])
    c_wcs = ein("wcs", [256, 512])
    c_dftc = ein("dftc", [SEQ, SEQ], BF16)
    c_dfts = ein("dftsn", [SEQ, SEQ], BF16)
    c_dftc_c = ein("dftc_c", [CTX, CTX], BF16)
    c_dfts_c = ein("dftsn_c", [CTX, CTX], BF16)
    c_invc = ein("invc", [128, 2, SEQ])
    c_invc_c = ein("invc_c", [128, 2, CTX])
    c_sel = ein("sel", [8, 8 * 128])
    c_ident = ein("ident", [128, 128])
    flg_in = ein("flg", [128, 2])
    c_cosq = ein("cosq", [96, HALF])
    c_sinq = ein("sinq", [96, HALF])
    out_d = kb.dram([D, HALF], F32, "outT", kind="ExternalOutput")

    xs = [x_in] + [kb.dram([D, TT], F32, f"xs{i}", kind=dkind) for i in range(1, 4)]
    uT = kb.dram([IN_W, TT], F32, "uT", kind=dkind)
    qT = kb.dram([4, 96, TT], BF16, "qT", kind=dkind)
    kT = kb.dram([4, 96, TT], BF16, "kT", kind=dkind)
    oT = kb.dram([D, TT], BF16, "oT", kind=dkind)
    fT = kb.dram([256, TT], BF16, "fT", kind=dkind)
    NBK = len(BLOCKS)
    xs_t = [[T(None, f"xs{i}_{b}") for b in range(NBK)] for i in range(4)]
    out_t = [T(None) for _ in range(NBK)]
    u_t = [[T(None) for _ in range(NBK)] for _ in range(14)]
    q_t = [[T(None) for _ in range(NBK)] for _ in range(4)]
    k_t = [[T(None) for _ in range(NBK)] for _ in range(4)]
    o_t = [[T(None) for _ in range(NBK)] for _ in range(8)]
    f_t = [T(None) for _ in range(NBK)]

    def u_tiles(r0, r1, bi):
        return [u_t[oc][bi] for oc in range(r0 // 128, (r1 - 1) // 128 + 1)]

    ind96 = kb.sb([96, 96]); ind128 = kb.sb([128, 128]); r96t = kb.sb([96, 96])
    onesD = kb.sb([128, 128]); ones256 = kb.sb([128, 128]); ones128 = kb.sb([128, 128]); onesf = kb.sb([128, 128])
    ident = kb.sb([128, 128]); sel = kb.sb([8, 8 * 128]); identb = kb.sb([128, 128], BF16)
    cc = kb.sb([128, 16]); sc = kb.sb([128, 8, 2]); flg = kb.sb([128, 2])
    pk = kb.sb([128, 80]); mod = kb.sb([128, 48, 2]); gm1 = kb.sb([128, 8, 2]); gm2 = kb.sb([128, 8, 2])
    NAR = 50560
    arena = kb.sb([128, NAR], F32, "arena")
    pst = [kb.ps([128, 512], F32, f"psb{i}") for i in range(8)]
    nps = RR(pst)

    class Carver:
        def __init__(self, lo, hi, dtype):
            self.lo, self.hi, self.dtype = lo, hi, dtype
            self.off = 0

        def reset(self):
            self.off = 0

        def get(self, shape, dtype=None):
            dtype = dtype or self.dtype
            esz = 2 if dtype == BF16 else 4
            osz = 2 if self.dtype == BF16 else 4
            n = int(np.prod(shape))
            byte0 = self.off * osz
            byte0 = (byte0 + 3) // 4 * 4
            nbytes = (n * esz + 3) // 4 * 4
            w0 = self.lo + byte0 // 4
            w1 = w0 + nbytes // 4
            assert w1 <= self.hi, (w1, self.hi)
            self.off = (byte0 + nbytes) // osz
            ap = arena.ap[:, w0:w1]
            if dtype == BF16:
                ap = ap.bitcast(BF16)[:, 0:n]
            if len(shape) == 2:
                ap = ap.rearrange("p (a b) -> p a b", a=shape[0])
            elif len(shape) == 3:
                ap = ap.rearrange("p (a b c) -> p a b c", a=shape[0], b=shape[1])
            elif len(shape) == 4:
                ap = ap.rearrange("p (a b c d) -> p a b c d", a=shape[0], b=shape[1], c=shape[2])
            return T(ap)

    cb = Carver(0, 33792, BF16)
    cf = Carver(33792, NAR, F32)
    cA = Carver(0, NAR, F32)

    for dst, src in ((ind96, c_ind96), (ind128, c_ind128), (r96t, c_r96t), (ident, c_ident), (sel, c_sel), (cc, cc_in), (flg, flg_in)):
        kb.dma("sp", dst[:], src[:], [], [dst])
    kb.op("pool", lambda e: e.memset(onesD[:], 1.0 / D), [], [onesD])
    kb.op("pool", lambda e: e.memset(ones256[:], 1.0 / 256), [], [ones256])
    kb.op("pool", lambda e: e.memset(ones128[:], 1.0 / 128), [], [ones128])
    kb.op("pool", lambda e: e.memset(onesf[:], 1.0), [], [onesf])
    kb.op("act", lambda e: e.activation(sc[:].rearrange("p k j -> p (k j)"), cc[:], AF.Silu), [cc], [sc])
    kb.op("dve", lambda e: e.tensor_copy(identb[:], ident[:]), [ident], [identb])

    evac_i = [0]

    def evac(out_ap, in_ap, reads, writes):
        evac_i[0] += 1
        if evac_i[0] % 2:
            kb.op("act", lambda e: e.copy(out_ap, in_ap), reads, writes)
        else:
            kb.op("dve", lambda e: e.tensor_copy(out_ap, in_ap), reads, writes)

    def adaln(l):
        cf.reset()
        wbs = RR([cf.get([8, 128]) for _ in range(3)])
        kb.dma("sp", pk[:], pk_in[l], [], [pk])
        wv = w_ada[l].rearrange("(k p) n -> p k n", p=128)
        for oc in range(48):
            wb = wbs()
            kb.dma("sp", wb[:], wv[:, :, oc * 128:(oc + 1) * 128], [], [wb])
            p = nps()
            kb.mm(p, p[:, 0:2], [(wb[:, k, :], sc[:, k, :]) for k in range(8)], [wb, sc])
            kb.op("dve", lambda e, p=p, oc=oc: e.tensor_scalar(mod[:, oc, :], p[:, 0:2], pk[:, oc:oc + 1], None, ALU.add),
                  [p, pk], [mod])
        for k in range(8):
            kb.op("dve", lambda e, k=k: e.tensor_scalar(gm1[:, k, :], mod[:, 8 + k, :], 1.0, pk[:, 48 + k:49 + k], ALU.add, ALU.mult),
                  [mod, pk], [gm1])
            kb.op("dve", lambda e, k=k: e.tensor_scalar(gm2[:, k, :], mod[:, 32 + k, :], 1.0, pk[:, 56 + k:57 + k], ALU.add, ALU.mult),
                  [mod, pk], [gm2])
        kb.barrier()

    def norm_mod(xb, sq, rs, w, j, gm, sh0, hb, hf=None):
        for k in range(8):
            kb.op("act", lambda e, k=k: e.activation(sq[:, k, :w], xb[:, k, :w], AF.Square), [xb], [sq])
        p = nps()
        kb.mm(p, p[:, :w], [(onesD[:, :], sq[:, k, :w]) for k in range(8)], [onesD, sq])
        kb.op("act", lambda e: e.activation(rs[:, :w], p[:, :w], AF.Sqrt, bias=EPS, scale=1.0), [p], [rs])
        kb.op("dve", lambda e: e.reciprocal(rs[:, :w], rs[:, :w]), [rs], [rs])
        for k in range(8):
            kb.op("dve", lambda e, k=k: e.tensor_tensor(sq[:, k, :w], xb[:, k, :w], rs[:, :w], ALU.mult), [xb, rs], [sq])
            kb.op("act", lambda e, k=k: e.activation(hb[:, k, :w], sq[:, k, :w], AF.Identity,
                                                      bias=mod[:, sh0 + k, j:j + 1], scale=gm[:, k, j:j + 1]),
                  [sq, mod, gm], [hb])
            if hf is not None:
                kb.op("act", lambda e, k=k: e.activation(hf[:, k, :w], sq[:, k, :w], AF.Identity,
                                                          bias=mod[:, sh0 + k, j:j + 1], scale=gm[:, k, j:j + 1]),
                      [sq, mod, gm], [hf])

    def inproj(l, xi, vna, nblocks):
        cb.off = vna_end
        cf.reset()
        win = cb.get([8, IN_W])
        hbs = RR([cb.get([8, 512]) for _ in range(2)])
        xbs = RR([cf.get([8, 512]) for _ in range(2)])
        sq = cf.get([8, 512])
        rs = cf.get([512])
        ubs = RR([cf.get([512]) for _ in range(2)])
        kb.dma("pool", win[:], w_in[l].rearrange("(k p) n -> p k n", p=128), [], [win])
        for bi in range(nblocks):
            c0, w, isc = BLOCKS[bi]
            j = 1 if isc else 0
            xb = xbs()
            kb.dma("sp", xb[:, :, :w], xs[xi].rearrange("(k p) t -> p k t", p=128)[:, :, c0:c0 + w], [xs_t[xi][bi]], [xb])
            hb = hbs()
            norm_mod(xb, sq, rs, w, j, gm1, 0, hb)
            for oc in range(14):
                r0 = oc * 128
                m = min(128, IN_W - r0)
                p = nps()
                kb.mm(p, p[:m, :w], [(win[:, k, r0:r0 + m], hb[:, k, :w]) for k in range(8)], [win, hb])
                ub = ubs()
                evac(ub[:m, :w], p[:m, :w], [p], [ub])
                kb.dma("sp", uT[r0:r0 + m, c0:c0 + w], ub[:m, :w], [ub], [u_t[oc][bi]])
            for s in range(w // 128):
                p = nps()
                kb.mm(p, p[:, 0:256], [(hb[:, k, s * 128:(s + 1) * 128], win[:, k, 1440:1696]) for k in range(8)], [win, hb])
                ch = c0 // 128 + s
                evac(vna[:, ch, :, 0:64], p[:, 0:256].rearrange("p (h d) -> p h d", h=4), [p], [vna])
        kb.barrier()

    cb.reset()
    vna = cb.get([66, 4, 65])
    vml = cb.get([66, 4, 65])
    ckvn = cb.get([TT])
    vna_end = cb.off
    kb.op("pool", lambda e: e.memset(vna[:], 1.0), [], [vna])
    kb.op("pool", lambda e: e.memset(vml[:], 1.0), [], [vml])

    nqT = kb.dram([256, TT], BF16, "nqT", kind=dkind)
    nkT = kb.dram([256, TT], BF16, "nkT", kind=dkind)
    nq_t = [T(None) for _ in range(NBK)]
    nk_t = [T(None) for _ in range(NBK)]
    psS = RR(pst[0:4])
    psO = RR(pst[4:6])
    psM = RR(pst[6:8])

    def uview(r0, r1):
        return uT[r0:r1, :].rearrange("(k p) t -> p k t", p=128)

    def norm96_rope(raw, wk, w, gcol, c0, rope, dst_ap, dst_t, cosb=None, sinb=None):
        sq, rs, qn, t1 = wk
        kb.op("act", lambda e: e.activation(sq[:96, :w], raw[:96, :w], AF.Square), [raw], [sq])
        p = nps()
        kb.mm(p, p[:96, :w], [(ind96[:, :], sq[:96, :w])], [ind96, sq])
        kb.op("act", lambda e: e.activation(rs[:96, :w], p[:96, :w], AF.Sqrt, bias=EPS, scale=1.0), [p], [rs])
        kb.op("dve", lambda e: e.reciprocal(rs[:96, :w], rs[:96, :w]), [rs], [rs])
        ob = obs()
        if rope:
            kb.op("dve", lambda e: e.scalar_tensor_tensor(qn[:96, :w], raw[:96, :w], gcol, rs[:96, :w], ALU.mult, ALU.mult),
                  [raw, rs, pk], [qn])
            p2 = nps()
            kb.mm(p2, p2[:96, :w], [(r96t[:, :], qn[:96, :w])], [r96t, qn])
            kb.op("pool", lambda e: e.tensor_tensor(t1[:96, :w], qn[:96, :w], cosb[:96, :w], ALU.mult), [qn, cosb], [t1])
            kb.op("dve", lambda e: e.tensor_tensor(sq[:96, :w], p2[:96, :w], sinb[:96, :w], ALU.mult), [p2, sinb], [sq])
            kb.op("dve", lambda e: e.tensor_tensor(ob[:96, :w], t1[:96, :w], sq[:96, :w], ALU.add), [t1, sq], [ob])
        else:
            kb.op("dve", lambda e: e.scalar_tensor_tensor(ob[:96, :w], raw[:96, :w], gcol, rs[:96, :w], ALU.mult, ALU.mult),
                  [raw, rs, pk], [ob])
        kb.dma("pool", dst_ap, ob[:96, :w], [ob], [dst_t])

    obs = None

    def qkprep(l, nblocks, split=False):
        nonlocal obs
        cb.off = vna_end
        cf.reset()
        wuq = cb.get([2, 384])
        wukv = cb.get([4, 128])
        cqn = cb.get([2, 512])
        nob = RR([cb.get([2, 512]) for _ in range(2)])
        obs = RR([cb.get([512]) for _ in range(4)])
        wks = RR([[cf.get([512]) for _ in range(4)] for _ in range(2)])
        raws = RR([cf.get([512]) for _ in range(3)])
        xfs = RR([cf.get([2, 512]) for _ in range(2)])
        sq2s = RR([cf.get([2, 512]) for _ in range(2)])
        rs2s = RR([cf.get([512]) for _ in range(2)])
        cbs = RR([cf.get([512]) for _ in range(2)])
        sbs_ = RR([cf.get([512]) for _ in range(2)])
        kb.dma("pool", wuq[:], w_uq[l].rearrange("(k p) n -> p k n", p=128), [], [wuq])
        kb.dma("pool", wukv[:], w_ukv[l].rearrange("p (h n) -> p h n", h=4), [], [wukv])
        def qpath(bi, c0, w, isc, cosb, sinb, blend):
            xf, sq2, rs2 = xfs(), sq2s(), rs2s()
            kb.dma("sp", xf[:, :, :w], uview(0, 256)[:, :, c0:c0 + w], u_tiles(0, 256, bi), [xf])
            if blend:
                xf2 = xfs()
                kb.dma("sp", xf2[:, :, :w], uview(0, 256)[:, :, HALF + c0:HALF + c0 + w], u_tiles(0, 256, bi + 8), [xf2])
                kb.op("dve", lambda e: e.tensor_scalar(xf[:, :, :w], xf[:, :, :w], flg[:, 0:1], None, ALU.mult), [xf, flg], [xf])
                kb.op("dve", lambda e: e.scalar_tensor_tensor(xf[:, :, :w], xf2[:, :, :w], flg[:, 1:2], xf[:, :, :w], ALU.mult, ALU.add),
                      [xf2, xf, flg], [xf])
            for k in range(2):
                kb.op("act", lambda e: e.activation(sq2[:, k, :w], xf[:, k, :w], AF.Square), [xf], [sq2])
            p = nps()
            kb.mm(p, p[:, :w], [(ones256[:, :], sq2[:, k, :w]) for k in range(2)], [ones256, sq2])
            kb.op("act", lambda e: e.activation(rs2[:, :w], p[:, :w], AF.Sqrt, bias=EPS, scale=1.0), [p], [rs2])
            kb.op("dve", lambda e: e.reciprocal(rs2[:, :w], rs2[:, :w]), [rs2], [rs2])
            for k in range(2):
                kb.op("dve", lambda e: e.scalar_tensor_tensor(cqn[:, k, :w], xf[:, k, :w], pk[:, 64 + k:65 + k], rs2[:, :w],
                                                              ALU.mult, ALU.mult), [xf, rs2, pk], [cqn])
            for h in range(4):
                p = nps()
                kb.mm(p, p[:96, :w], [(wuq[:, k, h * 96:(h + 1) * 96], cqn[:, k, :w]) for k in range(2)], [wuq, cqn])
                raw = raws()
                kb.op("act", lambda e: e.copy(raw[:96, :w], p[:96, :w]), [p], [raw])
                norm96_rope(raw, wks(), w, pk[:96, 67:68], c0, not isc, qT[h, :, c0:c0 + w], q_t[h][bi], cosb, sinb)

        for bi in range(nblocks):
            c0, w, isc = BLOCKS[bi]
            cosb = sinb = None
            if not isc:
                cosb, sinb = cbs(), sbs_()
                kb.dma("sp", cosb[:96, :w], c_cos[:, c0:c0 + w], [], [cosb])
                kb.dma("sp", sinb[:96, :w], c_sin[:, c0:c0 + w], [], [sinb])
            if not split:
                qpath(bi, c0, w, isc, cosb, sinb, False)
            xf, sq2, rs2 = xfs(), sq2s(), rs2s()
            kb.dma("sp", xf[:, 0, :w], uT[256:384, c0:c0 + w], u_tiles(256, 384, bi), [xf])
            kb.op("act", lambda e: e.activation(sq2[:, 0, :w], xf[:, 0, :w], AF.Square), [xf], [sq2])
            p = nps()
            kb.mm(p, p[:, :w], [(ones128[:, :], sq2[:, 0, :w])], [ones128, sq2])
            kb.op("act", lambda e, p=p: e.activation(rs2[:, :w], p[:, :w], AF.Sqrt, bias=EPS, scale=1.0), [p], [rs2])
            kb.op("dve", lambda e: e.reciprocal(rs2[:, :w], rs2[:, :w]), [rs2], [rs2])
            kb.op("dve", lambda e: e.scalar_tensor_tensor(ckvn[:, c0:c0 + w], xf[:, 0, :w], pk[:, 66:67], rs2[:, :w],
                                                           ALU.mult, ALU.mult), [xf, rs2, pk], [ckvn])
            for h in range(4):
                p = nps()
                kb.mm(p, p[:64, :w], [(wukv[:, h, 0:64], ckvn[:, c0:c0 + w])], [wukv, ckvn])
                raw = raws()
                kb.op("act", lambda e, p=p, raw=raw: e.copy(raw[:64, :w], p[:64, :w]), [p], [raw])
                kb.dma("sp", raw[64:96, :w], uT[384:416, c0:c0 + w], u_tiles(384, 416, bi), [raw])
                norm96_rope(raw, wks(), w, pk[:96, 68:69], c0, not isc, kT[h, :, c0:c0 + w], k_t[h][bi], cosb, sinb)
            for s in range(w // 128):
                p = nps()
                kb.mm(p, p[:, 0:256].rearrange("p (h d) -> p h d", h=4),
                      [(ckvn[:, c0 + s * 128:c0 + (s + 1) * 128], wukv[:, :, 64:128])], [wukv, ckvn])
                evac(vml[:, c0 // 128 + s, :, 0:64], p[:, 0:256].rearrange("p (h d) -> p h d", h=4), [p], [vml])
            for (r0, gc, dstT, dtl) in ((928, 71, nqT, nq_t), (1184, 72, nkT, nk_t)):
                xf, sq2, rs2 = xfs(), sq2s(), rs2s()
                kb.dma("sp", xf[:, :, :w], uview(r0, r0 + 256)[:, :, c0:c0 + w], u_tiles(r0, r0 + 256, bi), [xf])
                no = nob()
                for k in range(2):
                    kb.op("act", lambda e, k=k: e.activation(sq2[:, k, :w], xf[:, k, :w], AF.Square), [xf], [sq2])
                    p = nps()
                    kb.mm(p, p[:, :w], [(ind128[:, :], sq2[:, k, :w])], [ind128, sq2])
                    kb.op("act", lambda e, p=p: e.activation(rs2[:, :w], p[:, :w], AF.Sqrt, bias=EPS, scale=1.0), [p], [rs2])
                    kb.op("dve", lambda e: e.reciprocal(rs2[:, :w], rs2[:, :w]), [rs2], [rs2])
                    kb.op("dve", lambda e, k=k, no=no, gc=gc: e.scalar_tensor_tensor(no[:, k, :w], xf[:, k, :w], pk[:, gc:gc + 1],
                                                                                      rs2[:, :w], ALU.mult, ALU.mult),
                          [xf, rs2, pk], [no])
                kb.dma("pool", dstT.rearrange("(k p) t -> p k t", p=128)[:, :, c0:c0 + w], no[:, :, :w], [no], [dtl[bi]])
        if split:
            for bj in range(8):
                cosb, sinb = cbs(), sbs_()
                kb.dma("sp", cosb[:96, :], c_cosq[:, bj * 512:(bj + 1) * 512], [], [cosb])
                kb.dma("sp", sinb[:96, :], c_sinq[:, bj * 512:(bj + 1) * 512], [], [sinb])
                qpath(bj, bj * 512, 512, False, cosb, sinb, True)
        kb.barrier()

    fin = {}

    def attn_finish(pO, w, dst_ap, src_view, dst_ts):
        osb = fin["osb"]()
        rec = fin["rec"]
        ob = fin["ob"]()
        kb.op("act", lambda e: e.copy(osb[:65, :w], pO[:65, :w]), [pO], [osb])
        kb.op("dve", lambda e: e.reciprocal(rec[64:65, :w], osb[64:65, :w]), [osb], [rec])
        pB = psM()
        kb.mm(pB, pB[:64, :w], [(onesf[64:65, 0:64], rec[64:65, :w])], [onesf, rec])
        kb.op("dve", lambda e: e.tensor_tensor(ob[:64, :w], osb[:64, :w], pB[:64, :w], ALU.mult), [osb, pB], [ob])
        kb.dma("pool", dst_ap, src_view(ob), [ob], dst_ts)

    def mla_attn(l, nblocks, split=False):
        cb.off = vna_end
        cf.reset()
        kbuf = cb.get([TT])
        qbs = RR([cb.get([512]) for _ in range(2)])
        pts = RR([cb.get([512]) for _ in range(4)])
        fin["ob"] = RR([cb.get([512]) for _ in range(2)])
        fin["osb"] = RR([cf.get([512]) for _ in range(2)])
        fin["rec"] = cf.get([512])
        for h in range(4):
            kb.dma("sp", kbuf[:96, :], kT[h], [k_t[h][b] for b in range(NBK)], [kbuf])
            for bi in range(8 if split else nblocks):
                c0, w, isc = BLOCKS[bi]
                qb = qbs()
                kb.dma("sp", qb[:96, :w], qT[h, :, c0:c0 + w], [q_t[h][bi]], [qb])
                chunks = [64, 65] if isc else list(range(66))
                pO = psO()
                n = len(chunks)
                LA = 2
                ptl = {}
                for i in range(n + LA):
                    if i < n:
                        kc = chunks[i]
                        pS = psS()
                        kb.mm(pS, pS[:, :w], [(kbuf[:96, kc * 128:(kc + 1) * 128], qb[:96, :w])], [kbuf, qb])
                        pt = pts()
                        kb.op("act", lambda e: e.activation(pt[:, :w], pS[:, :w], AF.Exp, scale=MLA_SCALE), [pS], [pt])
                        ptl[i] = pt
                    if i >= LA:
                        ii = i - LA
                        pt = ptl.pop(ii)
                        kb.mm(pO, pO[:65, :w], [(vml[:, chunks[ii], h, :], pt[:, :w])], [vml, pt], start=(ii == 0), stop=(ii == n - 1))
                attn_finish(pO, w, oT[h * 64:(h + 1) * 64, c0:c0 + w], lambda ob, w=w: ob[:64, :w], [o_t[h // 2][bi]])
        kb.barrier()

    def na_attn(l, nblocks):
        cb.off = vna_end
        cf.reset()
        fin["osb"] = RR([cf.get([512]) for _ in range(2)])
        fin["rec"] = cf.get([512])
        bias = cb.get([21, 4, 128])
        fin["ob"] = RR([cb.get([512]) for _ in range(2)])
        kctx = cb.get([2, 256])
        qns = RR([cb.get([2, 128]) for _ in range(2)])
        kns = RR([cb.get([2, 640]) for _ in range(2)])
        pts = RR([cb.get([896]) for _ in range(3)])
        nqv = nqT.rearrange("(k p) t -> p k t", p=128)
        nkv = nkT.rearrange("(k p) t -> p k t", p=128)
        kb.dma("pool", bias[:].rearrange("p a h q -> p (a h q)"), nab[l], [], [bias])
        kb.op("dve", lambda e: e.tensor_scalar(bias[:].rearrange("p a h q -> p (a h q)"), bias[:].rearrange("p a h q -> p (a h q)"),
                                               1.0 / NA_SCALE, None, ALU.mult), [bias], [bias])
        kb.dma("sp", kctx[:], nkv[:, :, SEQ:TT], [nk_t[16]], [kctx])
        npb = 64 + (0 if nblocks == 16 else 2)
        pbinfo = {}

        def setup_pb(pb):
            if pb < 64:
                q0 = pb * 128
                if pb < 2:
                    loc, base = [0, 1, 2, 3], 5 + 4 * pb
                elif pb >= 62:
                    loc, base = [60, 61, 62, 63], 13 + 4 * (pb - 62)
                else:
                    loc, base = list(range(pb - 2, pb + 3)), 0
            else:
                q0 = SEQ + (pb - 64) * 128
                loc, base = [], 0
            nl = len(loc)
            qn = qns()
            kb.dma("sp", qn[:], nqv[:, :, q0:q0 + 128], [nq_t[q0 // 512]], [qn])
            kn = kns()
            if nl:
                kc0 = loc[0] * 128
                kb.dma("sp", kn[:, :, 0:nl * 128], nkv[:, :, kc0:kc0 + nl * 128],
                       [nk_t[b_] for b_ in range(kc0 // 512, (kc0 + nl * 128 - 1) // 512 + 1)], [kn])
            pbinfo[pb] = dict(q0=q0, loc=loc, base=base, nl=nl, qn=qn, kn=kn, pO=psO())

        def stage_a(pb, h):
            if h == 0:
                setup_pb(pb)
            I = pbinfo[pb]
            loc, base, nl, qn, kn = I["loc"], I["base"], I["nl"], I["qn"], I["kn"]
            nA = min(nl, 4)
            k_, po = h // 2, (h % 2) * 64
            pA = psS()
            for i in range(nA):
                kb.mm(pA, pA[:, i * 128:(i + 1) * 128], [(kn[po:po + 64, k_, i * 128:(i + 1) * 128], qn[po:po + 64, k_, :]),
                                                         (identb[:, :], bias[:, base + i, h, :])], [kn, qn, identb, bias])
            pB = psS()
            if nl == 5:
                kb.mm(pB, pB[:, 0:128], [(kn[po:po + 64, k_, 512:640], qn[po:po + 64, k_, :]),
                                         (identb[:, :], bias[:, base + 4, h, :])], [kn, qn, identb, bias])
            for c in range(2):
                kb.mm(pB, pB[:, 128 + c * 128:256 + c * 128], [(kctx[po:po + 64, k_, c * 128:(c + 1) * 128], qn[po:po + 64, k_, :])],
                      [kctx, qn])
            pt = pts()
            if nA:
                kb.op("act", lambda e: e.activation(pt[:, 0:nA * 128], pA[:, 0:nA * 128], AF.Exp, scale=NA_SCALE), [pA], [pt])
            if nl == 5:
                kb.op("act", lambda e: e.activation(pt[:, 512:896], pB[:, 0:384], AF.Exp, scale=NA_SCALE), [pB], [pt])
            else:
                kb.op("act", lambda e: e.activation(pt[:, 640:896], pB[:, 128:384], AF.Exp, scale=NA_SCALE), [pB], [pt])
            I[("pt", h)] = pt

        def stage_b(pb, h):
            I = pbinfo[pb]
            loc, nl, pO, q0 = I["loc"], I["nl"], I["pO"], I["q0"]
            pt = I.pop(("pt", h))
            for i in range(nl):
                kb.mm(pO, pO[:65, h * 128:(h + 1) * 128], [(vna[:, loc[i], h, :], pt[:, i * 128:(i + 1) * 128])], [vna, pt],
                      start=(i == 0), stop=False)
            kb.mm(pO, pO[:65, h * 128:(h + 1) * 128], [(vna[:, 64, h, :], pt[:, 640:768])], [vna, pt], start=(nl == 0), stop=False)
            kb.mm(pO, pO[:65, h * 128:(h + 1) * 128], [(vna[:, 65, h, :], pt[:, 768:896])], [vna, pt], start=False, stop=True)
            if h == 3:
                bi = q0 // 512
                attn_finish(pO, 512, oT[768:1024, q0:q0 + 128].rearrange("(h d) q -> d h q", h=4),
                            lambda ob: ob[:64, 0:512].rearrange("d (h q) -> d h q", h=4), [o_t[6][bi], o_t[7][bi]])
                del pbinfo[pb]

        items = [(pb, h) for pb in range(npb) for h in range(4)]
        LA = 1
        for idx in range(len(items) + LA):
            if idx < len(items):
                stage_a(*items[idx])
            if idx >= LA:
                stage_b(*items[idx - LA])
        kb.barrier()

    def fourier(l, nblocks):
        cb.reset()
        cf.reset()
        AB = cb.get([64, 512])
        xf = cb.get([2, SEQ])
        wcs = cb.get([2, 512])
        tcs = RR([cb.get([1024]) for _ in range(3)])
        tss = RR([cb.get([1024]) for _ in range(3)])
        fbs = RR([cb.get([512]) for _ in range(4)])
        kb.dma("pool", wcs[:], c_wcs.rearrange("(k p) n -> p k n", p=128), [], [wcs])
        kb.dma("pool", xf[:], uview(416, 672)[:, :, 0:SEQ], [t for b in range(16) for t in u_tiles(416, 672, b)], [xf])
        for tc in range(64):
            p = nps()
            kb.mm(p, p[:, :], [(xf[:, k, tc * 128:(tc + 1) * 128], wcs[:, k, :]) for k in range(2)], [xf, wcs])
            evac(AB[:, tc, :], p[:, :], [p], [AB])
        sc_l = float(1.0 / np.sqrt(SEQ * 64.0))
        for kb2 in range(8):
            pF = [[pst[0], pst[1]], [pst[2], pst[3]]]
            for lc in range(64):
                tcn = tcs()
                tsn = tss()
                kb.dma("sp", tcn[:], c_dftc[lc * 128:(lc + 1) * 128, kb2 * 1024:(kb2 + 1) * 1024], [], [tcn])
                kb.dma("pool", tsn[:], c_dfts[lc * 128:(lc + 1) * 128, kb2 * 1024:(kb2 + 1) * 1024], [], [tsn])
                for fc in range(2):
                    for hf in range(2):
                        kb.mm(pF[fc][hf], pF[fc][hf][:, :], [(AB[:, lc, fc * 128:(fc + 1) * 128], tcn[:, hf * 512:(hf + 1) * 512]),
                                                             (AB[:, lc, 256 + fc * 128:256 + (fc + 1) * 128], tsn[:, hf * 512:(hf + 1) * 512])],
                              [AB, tcn, tsn], start=(lc == 0), stop=(lc == 63))
            for fc in range(2):
                for hf in range(2):
                    kbk = kb2 * 2 + hf
                    fb = fbs()
                    kb.op("act", lambda e: e.activation(fb[:], pF[fc][hf][:, :], AF.Copy, scale=sc_l), [pF[fc][hf]], [fb])
                    kb.dma("sp", fT[fc * 128:(fc + 1) * 128, kbk * 512:(kbk + 1) * 512], fb[:], [fb], [f_t[kbk]])
        if nblocks == 17:
            xc = cb.get([2, CTX])
            ABc = cb.get([2, 512])
            tcc = cb.get([2, CTX])
            tsc = cb.get([2, CTX])
            kb.dma("pool", xc[:], uview(416, 672)[:, :, SEQ:TT], u_tiles(416, 672, 16), [xc])
            kb.dma("sp", tcc[:], c_dftc_c.rearrange("(k p) n -> p k n", p=128), [], [tcc])
            kb.dma("sp", tsc[:], c_dfts_c.rearrange("(k p) n -> p k n", p=128), [], [tsc])
            for tc in range(2):
                p = nps()
                kb.mm(p, p[:, :], [(xc[:, k, tc * 128:(tc + 1) * 128], wcs[:, k, :]) for k in range(2)], [xc, wcs])
                evac(ABc[:, tc, :], p[:, :], [p], [ABc])
            sc_c = float(1.0 / np.sqrt(CTX * 64.0))
            for fc in range(2):
                p = nps()
                prs = []
                for lc in range(2):
                    prs.append((ABc[:, lc, fc * 128:(fc + 1) * 128], tcc[:, lc, :]))
                    prs.append((ABc[:, lc, 256 + fc * 128:256 + (fc + 1) * 128], tsc[:, lc, :]))
                kb.mm(p, p[:, 0:CTX], prs, [ABc, tcc, tsc])
                fb = fbs()
                kb.op("act", lambda e, fb=fb, p=p: e.activation(fb[:, 0:CTX], p[:, 0:CTX], AF.Copy, scale=sc_c), [p], [fb])
                kb.dma("sp", fT[fc * 128:(fc + 1) * 128, SEQ:TT], fb[:, 0:CTX], [fb], [f_t[16]])
        wf = cb.get([2, 256])
        fls = RR([cb.get([2, 512]) for _ in range(2)])
        kb.dma("pool", wf[:], w_fo[l].rearrange("(k p) n -> p k n", p=128), [], [wf])
        for bi in range(nblocks):
            c0, w, isc = BLOCKS[bi]
            fl = fls()
            kb.dma("sp", fl[:, :, :w], fT.rearrange("(k p) t -> p k t", p=128)[:, :, c0:c0 + w], [f_t[bi]], [fl])
            for oc in range(2):
                p = nps()
                kb.mm(p, p[:, :w], [(wf[:, k, oc * 128:(oc + 1) * 128], fl[:, k, :w]) for k in range(2)], [wf, fl])
                fb = fbs()
                evac(fb[:, :w], p[:, :w], [p], [fb])
                kb.dma("sp", oT[256 + oc * 128:256 + (oc + 1) * 128, c0:c0 + w], fb[:, :w], [fb], [o_t[2 + oc][bi]])
        kb.barrier()

    def pool(l, nblocks):
        cb.reset()
        cf.reset()
        wpl = cb.get([2, 128])
        pds = RR([cb.get([2, 512]) for _ in range(2)])
        pos = RR([cb.get([512]) for _ in range(2)])
        xps = RR([cf.get([2, 528]) for _ in range(2)])
        s2 = cf.get([2, 528]); s4 = cf.get([2, 528]); s8 = cf.get([2, 528]); s16 = cf.get([2, 528])
        ivs = RR([cf.get([2, 512]) for _ in range(2)])
        tmp = cf.get([2, 512])
        kb.dma("pool", wpl[:], w_pl[l].rearrange("c p n -> p c n"), [], [wpl])
        uv = uview(672, 928)
        for bi in range(nblocks):
            c0, w, isc = BLOCKS[bi]
            xp = xps()
            lo = c0 - 8 if (not isc and bi > 0) else c0
            hi = c0 + w + 8 if (not isc and bi < 15) else c0 + w
            kb.op("pool", lambda e, xp=xp: e.memset(xp[:], 0.0), [], [xp])
            deps = []
            for b in range(max(0, bi - 1), min(NBK, bi + 2)):
                deps += u_tiles(672, 928, b)
            kb.dma("sp", xp[:, :, 8 + lo - c0:8 + hi - c0], uv[:, :, lo:hi], deps, [xp])
            n = w + 16
            kb.op("dve", lambda e, xp=xp: e.tensor_tensor(s2[:, :, 1:n], xp[:, :, 0:n - 1], xp[:, :, 1:n], ALU.add), [xp], [s2])
            kb.op("dve", lambda e: e.tensor_tensor(s4[:, :, 2:n - 1], s2[:, :, 1:n - 2], s2[:, :, 3:n], ALU.add), [s2], [s4])
            kb.op("dve", lambda e: e.tensor_tensor(s8[:, :, 4:n - 3], s4[:, :, 2:n - 5], s4[:, :, 6:n - 1], ALU.add), [s4], [s8])
            kb.op("dve", lambda e: e.tensor_tensor(s16[:, :, 8:n - 7], s8[:, :, 4:n - 11], s8[:, :, 12:n - 3], ALU.add), [s8], [s16])
            iv = ivs()
            if isc:
                kb.dma("sp", iv[:, :, :w], c_invc_c[:, :, :], [], [iv])
            else:
                kb.dma("sp", iv[:, :, :w], c_invc[:, :, c0:c0 + w], [], [iv])
            pd = pds()
            for g, sg in enumerate((s2, s4, s8, s16)):
                ch, po = g // 2, (g % 2) * 64
                kb.op("dve", lambda e, sg=sg, ch=ch, po=po, iv=iv: e.tensor_tensor(tmp[po:po + 64, ch, :w], sg[po:po + 64, ch, 8:8 + w],
                                                                                   iv[po:po + 64, ch, :w], ALU.mult), [sg, iv], [tmp])
                kb.op("dve", lambda e, ch=ch, po=po, xp=xp, pd=pd: e.tensor_tensor(pd[po:po + 64, ch, :w], tmp[po:po + 64, ch, :w],
                                                                                   xp[po:po + 64, ch, 8:8 + w], ALU.subtract),
                      [tmp, xp], [pd])
            for ch in range(2):
                p = nps()
                kb.mm(p, p[:, :w], [(wpl[:, ch, :], pd[:, ch, :w])], [wpl, pd])
                po_ = pos()
                kb.op("act", lambda e, p=p, po_=po_, ch=ch: e.activation(po_[:, :w], p[:, :w], AF.Copy, scale=pk[:, 69 + ch:70 + ch]),
                      [p, pk], [po_])
                kb.dma("sp", oT[512 + ch * 128:512 + (ch + 1) * 128, c0:c0 + w], po_[:, :w], [po_], [o_t[4 + ch][bi]])
        kb.barrier()

    def outproj(l, xi, nblocks, split=False):
        cb.reset()
        cf.reset()
        wo = cb.get([8, D])
        obs_ = RR([cb.get([8, 512]) for _ in range(2)])
        xbs = RR([cf.get([8, 512]) for _ in range(2)])
        kb.dma("pool", wo[:], w_out[l].rearrange("(k p) n -> p k n", p=128), [], [wo])
        if split:
            ob2s = RR([cb.get([6, 512]) for _ in range(2)])
            xb2s = RR([cf.get([8, 512]) for _ in range(2)])
        oTv = oT.rearrange("(k p) t -> p k t", p=128)
        for bi in range(8 if split else nblocks):
            c0, w, isc = BLOCKS[bi]
            j = 1 if isc else 0
            ob = obs_()
            kb.dma("sp", ob[:, :, :w], oTv[:, :, c0:c0 + w], [o_t[r][bi] for r in range(8)], [ob])
            xb = xbs()
            kb.dma("sp", xb[:, :, :w], xs[xi].rearrange("(k p) t -> p k t", p=128)[:, :, c0:c0 + w], [xs_t[xi][bi]], [xb])
            if split:
                ob2 = ob2s()
                kb.dma("pool", ob2[:, :, :], oTv[:, 2:8, HALF + c0:HALF + c0 + w], [o_t[r][bi + 8] for r in range(2, 8)], [ob2])
                kb.op("dve", lambda e: e.tensor_scalar(ob[:, 2:8, :], ob[:, 2:8, :], flg[:, 0:1], None, ALU.mult), [ob, flg], [ob])
                kb.op("dve", lambda e: e.scalar_tensor_tensor(ob[:, 2:8, :], ob2[:, :, :], flg[:, 1:2], ob[:, 2:8, :], ALU.mult, ALU.add),
                      [ob2, ob, flg], [ob])
                xb2 = xb2s()
                kb.dma("sp", xb2[:, :, :], xs[xi].rearrange("(k p) t -> p k t", p=128)[:, :, HALF + c0:HALF + c0 + w],
                       [xs_t[xi][bi + 8]], [xb2])
                kb.op("dve", lambda e: e.tensor_scalar(xb[:, :, :], xb[:, :, :], flg[:, 0:1], None, ALU.mult), [xb, flg], [xb])
                kb.op("dve", lambda e: e.scalar_tensor_tensor(xb[:, :, :], xb2[:, :, :], flg[:, 1:2], xb[:, :, :], ALU.mult, ALU.add),
                      [xb2, xb, flg], [xb])
            for oc in range(8):
                p = nps()
                kb.mm(p, p[:, :w], [(wo[:, k, oc * 128:(oc + 1) * 128], ob[:, k, :w]) for k in range(8)], [wo, ob])
                kb.op("dve", lambda e, p=p, xb=xb, oc=oc, j=j: e.scalar_tensor_tensor(xb[:, oc, :w], p[:, :w], mod[:, 16 + oc, j:j + 1],
                                                                                       xb[:, oc, :w], ALU.mult, ALU.add),
                      [p, mod, xb], [xb])
            kb.dma("sp", xs[xi + 1].rearrange("(k p) t -> p k t", p=128)[:, :, c0:c0 + w], xb[:, :, :w], [xb], [xs_t[xi + 1][bi]])
        kb.barrier()

    def ffn(l, xi, nblocks, moe, last):
        F = D_FFE if moe else D_FF
        NF = F // 128
        NEX = NE if moe else 1
        groups = [list(range(i, min(i + 7, NF))) for i in range(0, NF, 7)]
        SBW = 2048
        cA.reset()
        R0 = cA.get([8, SBW], F32)
        r0f = R0.ap.rearrange("p k t -> p (k t)")
        xb = r0f[:, 0:4096].rearrange("p (k t) -> p k t", k=8)
        sq = r0f[:, 4096:8192].rearrange("p (k t) -> p k t", k=8)
        yv = R0.ap
        h2 = cA.get([8, SBW], BF16)
        actq = cA.get([7, SBW], BF16)
        WB = RR([(cA.get([4096], BF16), cA.get([4096], BF16)) for _ in range(3)])
        g1s = RR([cA.get([512], BF16) for _ in range(3)])
        gbc = cA.get([SBW], BF16)
        gT = cA.get([SBW], F32)
        rs = cA.get([512], F32)
        xsl = RR([cA.get([512], F32) for _ in range(4 if last else 2)])
        lg = cA.get([8], F32); eq = cA.get([8], F32); l2 = cA.get([8], F32); ex = cA.get([8], F32)
        msk = cA.get([8], F32); gt = cA.get([8], F32); sm = cA.get([8], F32)
        wr = cA.get([8, 8], F32)
        if moe:
            kb.dma("sp", wr[:], w_rt[0].rearrange("(k p) e -> p k e", p=128), [], [wr])
        split = False
        ntok_override = HALF if last else SEQ
        ntok = ntok_override
        xBv = r0f[:, 8192:12288].rearrange("p (k t) -> p k t", k=8)
        sbs = [(i * SBW, [(q * 512, 512) for q in range(SBW // 512)], False) for i in range(ntok // SBW)]
        if nblocks == 17:
            sbs.append((SEQ, [(0, 256)], True))
        xin = xs[xi].rearrange("(k p) t -> p k t", p=128)
        for (s0, subs, isc) in sbs:
            j = 1 if isc else 0
            for (so, w) in subs:
                c0 = s0 + so
                bi = c0 // 512
                kb.dma("sp", xb[:, :, :w], xin[:, :, c0:c0 + w], [xs_t[xi][bi]], [R0])
                if split:
                    kb.dma("sp", xBv[:, :, :w], xin[:, :, HALF + c0:HALF + c0 + w], [xs_t[xi][bi + 8]], [R0])
                    kb.op("dve", lambda e: e.tensor_scalar(r0f[:, 0:4096], r0f[:, 0:4096], flg[:, 0:1], None, ALU.mult), [R0, flg], [R0])
                    kb.op("dve", lambda e: e.scalar_tensor_tensor(r0f[:, 0:4096], r0f[:, 8192:12288], flg[:, 1:2], r0f[:, 0:4096],
                                                                  ALU.mult, ALU.add), [R0, flg], [R0])
                for k in range(8):
                    kb.op("act", lambda e: e.activation(sq[:, k, :w], xb[:, k, :w], AF.Square), [R0], [R0])
                p = nps()
                kb.mm(p, p[:, :w], [(onesD[:, :], sq[:, k, :w]) for k in range(8)], [onesD, R0])
                kb.op("act", lambda e: e.activation(rs[:, :w], p[:, :w], AF.Sqrt, bias=EPS, scale=1.0), [p], [rs])
                kb.op("dve", lambda e: e.reciprocal(rs[:, :w], rs[:, :w]), [rs], [rs])
                for k in range(8):
                    kb.op("dve", lambda e: e.tensor_tensor(sq[:, k, :w], xb[:, k, :w], rs[:, :w], ALU.mult), [R0, rs], [R0])
                    kb.op("act", lambda e: e.activation(sq[:, k, :w], sq[:, k, :w], AF.Identity,
                                                        bias=mod[:, 24 + k, j:j + 1], scale=gm2[:, k, j:j + 1]),
                          [R0, mod, gm2], [R0])
                    kb.op("dve", lambda e: e.tensor_copy(h2[:, k, so:so + w], sq[:, k, :w]), [R0], [h2])
                if moe:
                    for t4 in range(w // 128):
                        p = psM()
                        kb.mm(p, p[:, 0:8], [(sq[:, k, t4 * 128:(t4 + 1) * 128], wr[:, k, :]) for k in range(8)], [R0, wr])
                        kb.op("act", lambda e: e.copy(lg[:, :], p[:, 0:8]), [p], [lg])
                        kb.op("dve", lambda e: e.tensor_reduce(sm[:, 0:1], lg[:, :], mybir.AxisListType.X, ALU.max), [lg], [sm])
                        kb.op("dve", lambda e: e.tensor_scalar(eq[:, :], lg[:, :], sm[:, 0:1], None, ALU.is_equal), [lg, sm], [eq])
                        kb.op("dve", lambda e: e.scalar_tensor_tensor(l2[:, :], eq[:, :], -1e30, lg[:, :], ALU.mult, ALU.add), [eq, lg], [l2])
                        kb.op("dve", lambda e: e.tensor_reduce(sm[:, 1:2], l2[:, :], mybir.AxisListType.X, ALU.max), [l2], [sm])
                        kb.op("dve", lambda e: e.tensor_scalar(msk[:, :], lg[:, :], sm[:, 1:2], None, ALU.is_ge), [lg, sm], [msk])
                        kb.op("dve", lambda e: e.tensor_scalar(sm[:, 2:3], sm[:, 0:1], -1.0, None, ALU.mult), [sm], [sm])
                        kb.op("act", lambda e: e.activation(ex[:, :], lg[:, :], AF.Exp, bias=sm[:, 2:3], scale=1.0), [lg, sm], [ex])
                        kb.op("act", lambda e: e.activation(sm[:, 3:4], sm[:, 1:2], AF.Exp, bias=sm[:, 2:3], scale=1.0), [sm], [sm])
                        kb.op("dve", lambda e: e.tensor_scalar(sm[:, 4:5], sm[:, 3:4], 1.0, None, ALU.add), [sm], [sm])
                        kb.op("dve", lambda e: e.reciprocal(sm[:, 5:6], sm[:, 4:5]), [sm], [sm])
                        kb.op("dve", lambda e: e.scalar_tensor_tensor(gt[:, :], ex[:, :], sm[:, 5:6], msk[:, :], ALU.mult, ALU.mult),
                              [ex, sm, msk], [gt])
                        p2 = psM()
                        kb.mm(p2, p2[:8, 0:128], [(gt[:, :], ident[:, :])], [gt, ident])
                        o = so + t4 * 128
                        kb.op("act", lambda e: e.copy(gT[:8, o:o + 128], p2[:8, 0:128]), [p2], [gT])
            kb.barrier()
            first_y = True
            for ex_i in range(NEX):
                if moe:
                    w1v = w1m[0, ex_i].rearrange("(k p) f -> p k f", p=128)
                    w3v = w3m[0, ex_i].rearrange("(k p) f -> p k f", p=128)
                    w2v = w2m[0, ex_i].rearrange("(c p) n -> p c n", p=128)
                    for (so, w) in subs:
                        p = psM()
                        kb.mm(p, p[:, :w], [(sel[:, ex_i * 128:(ex_i + 1) * 128], gT[:8, so:so + w])], [sel, gT])
                        kb.op("act", lambda e: e.copy(gbc[:, so:so + w], p[:, :w]), [p], [gbc])
                else:
                    w1v = w1d[0].rearrange("(k p) f -> p k f", p=128)
                    w3v = w3d[0].rearrange("(k p) f -> p k f", p=128)
                    w2v = w2d[0].rearrange("(c p) n -> p c n", p=128)
                for grp in groups:
                    f0 = grp[0] * 128
                    ncols = len(grp) * 128
                    for cc0 in range(0, ncols, 512):
                        ncc = min(512, ncols - cc0)
                        b1, b3 = WB()
                        w1t = b1.ap.rearrange("p (k f) -> p k f", k=8)
                        w3t = b3.ap.rearrange("p (k f) -> p k f", k=8)
                        kb.dma("pool", w1t[:, :, :ncc], w1v[:, :, f0 + cc0:f0 + cc0 + ncc], [], [b1])
                        kb.dma("pool", w3t[:, :, :ncc], w3v[:, :, f0 + cc0:f0 + cc0 + ncc], [], [b3])
                        for fl in range(ncc // 128):
                            fcl = cc0 // 128 + fl
                            for (so, w) in subs:
                                pa = nps()
                                kb.mm(pa, pa[:, :w], [(w1t[:, k, fl * 128:(fl + 1) * 128], h2[:, k, so:so + w]) for k in range(8)], [b1, h2])
                                pb_ = nps()
                                kb.mm(pb_, pb_[:, :w], [(w3t[:, k, fl * 128:(fl + 1) * 128], h2[:, k, so:so + w]) for k in range(8)], [b3, h2])
                                g1 = g1s()
                                kb.op("act", lambda e: e.activation(g1[:, :w], pa[:, :w], AF.Silu), [pa], [g1])
                                if moe:
                                    kb.op("dve", lambda e: e.tensor_tensor(g1[:, :w], g1[:, :w], gbc[:, so:so + w], ALU.mult), [g1, gbc], [g1])
                                kb.op("dve", lambda e: e.tensor_tensor(actq[:, fcl, so:so + w], g1[:, :w], pb_[:, :w], ALU.mult),
                                      [g1, pb_], [actq])
                    b1, b3 = WB()
                    w2t = b1.ap[:, 0:4096]
                    ng = len(grp)
                    w2pair = T(None)
                    w2a = b1.ap.rearrange("p (c n) -> p c n", n=1024)
                    w2b = b3.ap.rearrange("p (c n) -> p c n", n=1024)
                    na = min(ng, 4)
                    kb.dma("pool", w2a[:, 0:na, :], w2v[:, grp[0]:grp[0] + na, :], [], [b1])
                    if ng > 4:
                        kb.dma("pool", w2b[:, 0:ng - 4, :], w2v[:, grp[0] + 4:grp[0] + ng, :], [], [b3])
                    for oc in range(8):
                        for (so, w) in subs:
                            py = nps()
                            prs = []
                            for c in range(ng):
                                src = w2a if c < 4 else w2b
                                prs.append((src[:, c % 4, oc * 128:(oc + 1) * 128], actq[:, c, so:so + w]))
                            kb.mm(py, py[:, :w], prs, [b1, b3, actq])
                            if first_y:
                                kb.op("act", lambda e: e.copy(yv[:, oc, so:so + w], py[:, :w]), [py], [R0])
                            else:
                                kb.op("dve", lambda e: e.tensor_tensor(yv[:, oc, so:so + w], yv[:, oc, so:so + w], py[:, :w], ALU.add),
                                      [py, R0], [R0])
                    first_y = False
            for (so, w) in subs:
                c0 = s0 + so
                bi = c0 // 512
                for oc in range(8):
                    xl = xsl()
                    kb.dma("sp", xl[:, :w], xs[xi][oc * 128:(oc + 1) * 128, c0:c0 + w], [xs_t[xi][bi]], [xl])
                    if split:
                        xl2 = xsl()
                        kb.dma("sp", xl2[:, :w], xs[xi][oc * 128:(oc + 1) * 128, HALF + c0:HALF + c0 + w], [xs_t[xi][bi + 8]], [xl2])
                        kb.op("dve", lambda e: e.tensor_scalar(xl[:, :w], xl[:, :w], flg[:, 0:1], None, ALU.mult), [xl, flg], [xl])
                        kb.op("dve", lambda e: e.scalar_tensor_tensor(xl[:, :w], xl2[:, :w], flg[:, 1:2], xl[:, :w], ALU.mult, ALU.add),
                              [xl2, xl, flg], [xl])
                    kb.op("dve", lambda e: e.scalar_tensor_tensor(xl[:, :w], yv[:, oc, so:so + w], mod[:, 40 + oc, j:j + 1], xl[:, :w],
                                                                  ALU.mult, ALU.add), [R0, mod, xl], [xl])
                    if last:
                        kb.dma("sp", out_d[oc * 128:(oc + 1) * 128, c0:c0 + w], xl[:, :w], [xl], [out_t[bi]])
                    else:
                        kb.dma("sp", xs[xi + 1][oc * 128:(oc + 1) * 128, c0:c0 + w], xl[:, :w], [xl], [xs_t[xi + 1][bi]])
            kb.barrier()

    def finish():
        kb.barrier()
        kb.emit()
        return nc

    for l in range(DEPTH):
        last = l == DEPTH - 1
        nblocks = 16 if last else 17
        xi = 2 * l
        adaln(l)
        kb.op("pool", lambda e: e.memset(vna[:], 1.0), [], [vna])
        kb.op("pool", lambda e: e.memset(vml[:], 1.0), [], [vml])
        inproj(l, xi, vna, 17)
        if stop_after == f"inproj{l}":
            return finish()
        qkprep(l, 17, split=last)
        if stop_after == f"qkprep{l}":
            return finish()
        mla_attn(l, nblocks, split=last)
        if stop_after == f"mla{l}":
            return finish()
        na_attn(l, nblocks)
        if stop_after == f"na{l}":
            return finish()
        fourier(l, nblocks)
        pool(l, nblocks)
        if stop_after == f"mix{l}":
            return finish()
        outproj(l, xi, nblocks, split=last)
        if stop_after == f"outproj{l}":
            return finish()
        ffn(l, xi + 1, nblocks, moe=(l % 2 == 1), last=last)
        if stop_after == f"ffn{l}":
            return finish()
    return finish()


_CONSTS = None


def _dft_tables():
    c = {}
    k = np.arange(SEQ, dtype=np.int64)
    dc = np.empty((SEQ, SEQ), ml_dtypes.bfloat16)
    ds = np.empty((SEQ, SEQ), ml_dtypes.bfloat16)
    for r0 in range(0, SEQ, 1024):
        kl = (np.outer(k[r0:r0 + 1024], k) % SEQ).astype(np.float32) * np.float32(2 * np.pi / SEQ)
        dc[r0:r0 + 1024] = np.cos(kl).astype(ml_dtypes.bfloat16)
        ds[r0:r0 + 1024] = (-np.sin(kl)).astype(ml_dtypes.bfloat16)
    c["dftc"] = dc
    c["dftsn"] = ds
    return c


def prep_inputs(inp, cores):
    global _CONSTS
    if _CONSTS is None:
        _CONSTS = _const_tables()
    inp = {k: np.asarray(v) for k, v in inp.items()}
    shared = dict(_CONSTS)
    shared["sel"] = shared["sel"].reshape(8, 8 * 128)
    shared["pk"] = np.stack([_pack_params(inp, l) for l in range(DEPTH)], 0)
    for name in ("w_ada", "w_in", "w_out", "w_uq", "w_ukv", "w_fourier", "w1_dense", "w3_dense", "w2_dense",
                 "w_router", "w1_moe", "w3_moe", "w2_moe"):
        shared[name] = np.ascontiguousarray(inp[name], dtype=np.float32)
    bd = np.zeros((DEPTH, 2, 128, 128), np.float32)
    for l in range(DEPTH):
        for g in range(4):
            o = (g % 2) * 64
            bd[l, g // 2, o:o + 64, o:o + 64] = inp["w_pool"][l, g]
    shared["w_poolbd"] = bd
    shared["na_bias"] = np.stack([_na_bias_tiles(inp["na_rpb"][l]).reshape(128, -1) for l in range(DEPTH)], 0)
    maps = []
    for b in cores:
        m = dict(shared)
        m["xT"] = np.ascontiguousarray(np.concatenate([inp["x"][b].T, inp["ctx"][b].T], axis=1), dtype=np.float32)
        cc = np.zeros((128, 16), np.float32)
        cc[:, 0::2] = inp["c"][b].reshape(8, 128).T
        cc[:, 1::2] = inp["c_ctx"].reshape(8, 128).T
        m["cc"] = cc
        maps.append(m)
    return maps


def prep_inputs8(inp):
    base = prep_inputs(inp, list(range(4)))
    maps = []
    for i in range(NCORES):
        m = dict(base[i // 2])
        f = np.zeros((128, 2), np.float32)
        f[:, i % 2] = 1.0
        m["flg"] = f
        o = (i % 2) * HALF
        m["cosq"] = np.ascontiguousarray(base[i // 2]["cos96"][:, o:o + HALF])
        m["sinq"] = np.ascontiguousarray(base[i // 2]["sin96"][:, o:o + HALF])
        maps.append(m)
    return maps


_NC = None


def kernel(**inputs):
    global _NC
    if _NC is None:
        _NC = build_program()
    maps = prep_inputs8(inputs)
    res = run_bass_kernel_spmd(_NC, maps, core_ids=list(range(NCORES)))
    halves = [np.ascontiguousarray(r["outT"].T) for r in res.results]
    out = np.stack([np.concatenate([halves[2 * b], halves[2 * b + 1]], axis=0) for b in range(4)], 0)
    return out.astype(np.float32)
```

```python
import numpy as np
import ml_dtypes
from contextlib import ExitStack
import concourse.bass as bass
import concourse.mybir as mybir
from concourse.bass_utils import run_bass_kernel_spmd

F32 = mybir.dt.float32
BF16 = mybir.dt.bfloat16
AF = mybir.ActivationFunctionType
ALU = mybir.AluOpType

D = 1024
SEQ = 8192
CTX = 256
TT = SEQ + CTX
DEPTH = 2
GRID_W = 64
IN_W = 1696
D_FF = 2816
NE = 8
D_FFE = 3584
MLA_SCALE = 96 ** -0.5
NA_SCALE = 64 ** -0.5
EPS = 1e-6
NEG = -30000.0
BLOCKS = [(i * 512, 512, False) for i in range(16)] + [(SEQ, 256, True)]
NCORES = 8
HALF = SEQ // 2


class T:
    __slots__ = ("ap", "lw", "rd", "name")

    def __init__(self, ap, name=""):
        self.ap = ap
        self.lw = {}
        self.rd = {}
        self.name = name

    def __getitem__(self, idx):
        return self.ap[idx]


class _Rec:
    def __getattr__(self, name):
        def f(*a, **k):
            self.call = (name, a, k)
            return self
        return f


class KB:
    ENG = ["pe", "act", "dve", "pool", "sp"]
    NDS = 8

    def __init__(self, nc):
        self.nc = nc
        self.es = ExitStack()
        self.semobj = {}
        self.cnt = {}
        self.latest = {}
        for e in self.ENG:
            self.semobj[e] = self.es.enter_context(nc.semaphore("s_" + e))
            self.cnt[e] = 0
        self.dcnt = {}
        for q in ("sp", "pool", "act"):
            self.dcnt[q] = 0
            for i in range(self.NDS):
                self.semobj[f"d_{q}{i}"] = self.es.enter_context(nc.semaphore(f"d_{q}{i}"))
        self.prog = {e: [] for e in self.ENG}
        self.waited = {e: {} for e in self.ENG}
        self.n_alloc = 0

    def sb(self, shape, dtype=F32, name=None):
        self.n_alloc += 1
        name = name or f"sb{self.n_alloc}"
        t = self.es.enter_context(self.nc.sbuf_tensor(name, list(shape), dtype))
        return T(t, name)

    def ps(self, shape, dtype=F32, name=None):
        self.n_alloc += 1
        name = name or f"ps{self.n_alloc}"
        t = self.es.enter_context(self.nc.psum_tensor(name, list(shape), dtype))
        return T(t, name)

    def dram(self, shape, dtype=F32, name=None, kind="Internal"):
        self.n_alloc += 1
        name = name or f"dr{self.n_alloc}"
        t = self.nc.dram_tensor(name, list(shape), dtype, kind=kind)
        return t.ap()

    def _deps(self, E, reads, writes, skip_same=False):
        deps = {}

        def add(k, v):
            if skip_same and k == E:
                return
            if deps.get(k, 0) < v:
                deps[k] = v

        for t in reads:
            for k, v in t.lw.items():
                add(k, v)
        for t in writes:
            for k, v in t.lw.items():
                add(k, v)
            for k, v in t.rd.items():
                add(k, v)
        w = self.waited[E]
        out = []
        for k, v in deps.items():
            if w.get(k, 0) < v:
                w[k] = v
                out.append((k, v))
        return out

    def _commit(self, tok, reads, writes):
        k, v = tok
        self.latest[k] = v
        for t in writes:
            t.lw[k] = v
            t.rd = {}
        for t in reads:
            if t.rd.get(k, 0) < v:
                t.rd[k] = v

    def op(self, E, fn0, reads=(), writes=()):
        rec = _Rec()
        fn0(rec)
        name, a, k = rec.call

        def fn(eng):
            return getattr(eng, name)(*a, **k)

        waits = self._deps(E, reads, writes, skip_same=(E == "pe"))
        self.cnt[E] += 1
        tok = (E, self.cnt[E])
        self.prog[E].append((waits, fn, (E, 1)))
        self._commit(tok, reads, writes)

    def mm(self, out_t, out_ap, pairs, reads, start=True, stop=True):
        waits = self._deps("pe", reads, [out_t], skip_same=True)
        n = len(pairs)
        self.cnt["pe"] += 1
        tok = ("pe", self.cnt["pe"])
        for i, (l, r) in enumerate(pairs):
            st = start and i == 0
            sp = stop and i == n - 1

            def fn(pe, l=l, r=r, st=st, sp=sp):
                return pe.matmul(out_ap, l, r, start=st, stop=sp)

            self.prog["pe"].append((waits if i == 0 else [], fn, ("pe", 1) if i == n - 1 else None))
        self._commit(tok, reads, [out_t])

    def dma(self, q, out_ap, in_ap, reads=(), writes=()):
        i = self.dcnt[q]
        self.dcnt[q] += 1
        s = i % self.NDS
        val = 16 * (i // self.NDS + 1)
        key = f"d_{q}{s}"
        waits = self._deps(q, reads, writes)
        if i >= self.NDS and self.waited[q].get(key, 0) < val - 16:
            self.waited[q][key] = val - 16
            waits.append((key, val - 16))

        def fn(eng):
            src = in_ap() if callable(in_ap) else in_ap
            return eng.dma_start(out=out_ap, in_=src)

        self.prog[q].append((waits, fn, (key, 16)))
        self._commit((key, val), reads, writes)

    def barrier(self):
        for E in self.ENG:
            w = self.waited[E]
            waits = []
            for k, v in self.latest.items():
                if k != E and w.get(k, 0) < v:
                    w[k] = v
                    waits.append((k, v))
            self.prog[E].append((waits, None, None))

    def emit(self):
        nc = self.nc
        with nc.Block() as block:
            def run(eng, E):
                for waits, fn, inc in self.prog[E]:
                    for k, v in waits:
                        eng.wait_ge(self.semobj[k], v)
                    if fn is not None:
                        ins = fn(eng)
                        if inc is not None and ins is not None:
                            ins.then_inc(self.semobj[inc[0]], inc[1])

            @block.tensor
            def _(e):
                run(e, "pe")

            @block.scalar
            def _(e):
                run(e, "act")

            @block.vector
            def _(e):
                run(e, "dve")

            @block.gpsimd
            def _(e):
                run(e, "pool")

            @block.sync
            def _(e):
                run(e, "sp")
        self.es.close()


class RR:
    def __init__(self, items):
        self.items = items
        self.i = 0

    def __call__(self):
        t = self.items[self.i % len(self.items)]
        self.i += 1
        return t


def _const_tables():
    c = {}
    ind96 = np.zeros((96, 96), np.float32)
    ind96[:64, :64] = 1.0 / 64
    ind96[64:, 64:] = 1.0 / 32
    c["ind96"] = ind96
    ind128 = np.zeros((128, 128), np.float32)
    ind128[:64, :64] = 1.0 / 64
    ind128[64:, 64:] = 1.0 / 64
    c["ind128"] = ind128
    R = np.zeros((96, 96), np.float32)
    for base in (64, 80):
        for i in range(8):
            R[base + i, base + 8 + i] = -1.0
            R[base + 8 + i, base + i] = 1.0
    c["r96t"] = np.ascontiguousarray(R.T)
    t = np.arange(SEQ)
    row = (t // GRID_W).astype(np.float32)
    col = (t % GRID_W).astype(np.float32)
    inv = (1.0 / (10000.0 ** (np.arange(0, 16, 2, dtype=np.float32) / 16))).astype(np.float32)
    ang_r = row[:, None] * inv
    ang_c = col[:, None] * inv
    cos96 = np.ones((96, SEQ), np.float32)
    sin96 = np.zeros((96, SEQ), np.float32)
    cos96[64:72] = np.cos(ang_r).T
    cos96[72:80] = np.cos(ang_r).T
    cos96[80:88] = np.cos(ang_c).T
    cos96[88:96] = np.cos(ang_c).T
    sin96[64:72] = np.sin(ang_r).T
    sin96[72:80] = np.sin(ang_r).T
    sin96[80:88] = np.sin(ang_c).T
    sin96[88:96] = np.sin(ang_c).T
    c["cos96"] = cos96
    c["sin96"] = sin96
    m = np.arange(64)
    ang = 2 * np.pi * np.outer(m, m) / 64.0
    wcs = np.zeros((256, 512), np.float32)
    for g in range(4):
        wcs[g * 64:(g + 1) * 64, g * 64:(g + 1) * 64] = np.cos(ang)
        wcs[g * 64:(g + 1) * 64, 256 + g * 64:256 + (g + 1) * 64] = np.sin(ang)
    c["wcs"] = wcs
    c.update(_dft_tables())
    kl = (np.outer(np.arange(CTX), np.arange(CTX)) % CTX).astype(np.float64)
    a = 2 * np.pi * kl / CTX
    c["dftc_c"] = np.cos(a).astype(ml_dtypes.bfloat16)
    c["dftsn_c"] = (-np.sin(a)).astype(ml_dtypes.bfloat16)
    def invcnt(L):
        out = np.zeros((128, 2, L), np.float32)
        tt = np.arange(L)
        for g, w in enumerate((2, 4, 8, 16)):
            lo = np.clip(tt - w // 2, 0, L)
            hi = np.clip(tt + w - w // 2, 0, L)
            ic = 1.0 / (hi - lo).astype(np.float32)
            out[(g % 2) * 64:(g % 2) * 64 + 64, g // 2, :] = ic[None, :]
        return out
    c["invc"] = invcnt(SEQ)
    c["invc_c"] = invcnt(CTX)
    sel = np.zeros((8, 8, 128), np.float32)
    for e in range(8):
        sel[e, e, :] = 1.0
    c["sel"] = sel
    c["ident"] = np.eye(128, dtype=np.float32)
    return c


def _na_classes():
    return None


def _na_bias_tiles(rpb):
    H = 4
    qc = np.arange(64)
    win_c0 = np.clip(qc - 8, 0, 48)
    kc = np.arange(64)
    ok = (kc[:, None] >= win_c0[None, :]) & (kc[:, None] < win_c0[None, :] + 16)
    off = np.clip(kc[:, None] - qc[None, :] + 15, 0, 30)
    tiles = []

    def tile_for(pb, chunk):
        tl = np.full((H, 128, 128), NEG, np.float32)
        for a in range(2):
            for b in range(2):
                krow = 2 * chunk + a
                qrow = 2 * pb + b
                r0 = min(max(qrow - 4, 0), 120)
                if not (r0 <= krow < r0 + 8):
                    continue
                dr = krow - qrow + 7
                blk = np.where(ok[None], rpb[:, dr, :][:, off], NEG)
                tl[:, a * 64:(a + 1) * 64, b * 64:(b + 1) * 64] = blk
        return tl

    for cidx in range(5):
        tiles.append(tile_for(10, 10 - 2 + cidx))
    for pb in (0, 1):
        for ch in range(4):
            tiles.append(tile_for(pb, ch))
    for pb in (62, 63):
        for ch in range(60, 64):
            tiles.append(tile_for(pb, ch))
    arr = np.stack(tiles, 0)
    return np.ascontiguousarray(arr.transpose(2, 0, 1, 3))


def _pack_params(inp, l):
    pk = np.zeros((128, 80), np.float32)
    pk[:, 0:48] = inp["b_ada"][l].reshape(48, 128).T
    pk[:, 48:56] = inp["g_mix"][l].reshape(8, 128).T
    pk[:, 56:64] = inp["g_ffn"][l].reshape(8, 128).T
    pk[:, 64:66] = inp["g_cq"][l].reshape(2, 128).T
    pk[:, 66] = inp["g_ckv"][l]
    pk[:64, 67] = inp["g_mla_qn"][l]
    pk[64:96, 67] = inp["g_mla_qr"][l]
    pk[:64, 68] = inp["g_mla_kn"][l]
    pk[64:96, 68] = inp["g_mla_kr"][l]
    pk[:, 69:71] = inp["pool_scale"][l].reshape(2, 128).T
    pk[:, 71] = np.tile(inp["g_na_q"][l], 2)
    pk[:, 72] = np.tile(inp["g_na_k"][l], 2)
    return pk


def build_program(stop_after=None, debug=False):
    nc = bass.Bass("TRN2", target_bir_lowering=False)
    kb = KB(nc)
    dkind = "ExternalOutput" if debug else "Internal"

    def ein(name, shape, dt=F32):
        return kb.dram(shape, dt, name, kind="ExternalInput")

    x_in = ein("xT", [D, TT])
    cc_in = ein("cc", [128, 16])
    pk_in = ein("pk", [DEPTH, 128, 80])
    w_ada = ein("w_ada", [DEPTH, D, 6 * D])
    w_in = ein("w_in", [DEPTH, D, IN_W])
    w_out = ein("w_out", [DEPTH, D, D])
    w_uq = ein("w_uq", [DEPTH, 256, 384])
    w_ukv = ein("w_ukv", [DEPTH, 128, 512])
    w_fo = ein("w_fourier", [DEPTH, 256, 256])
    w_pl = ein("w_poolbd", [DEPTH, 2, 128, 128])
    nab = ein("na_bias", [DEPTH, 128, 21 * 4 * 128])
    w1d = ein("w1_dense", [1, D, D_FF])
    w3d = ein("w3_dense", [1, D, D_FF])
    w2d = ein("w2_dense", [1, D_FF, D])
    w_rt = ein("w_router", [1, D, NE])
    w1m = ein("w1_moe", [1, NE, D, D_FFE])
    w3m = ein("w3_moe", [1, NE, D, D_FFE])
    w2m = ein("w2_moe", [1, NE, D_FFE, D])
    c_ind96 = ein("ind96", [96, 96])
    c_ind128 = ein("ind128", [128, 128])
    c_r96t = ein("r96t", [96, 96])
    c_cos = ein("cos96", [96, SEQ])
    c_sin = ein("sin96", [96, SEQ])
    c_wcs = ein("wcs", [256, 512])
    c_dftc = ein("dftc", [SEQ, SEQ], BF16)
    c_dfts = ein("dftsn", [SEQ, SEQ], BF16)
    c_dftc_c = ein("dftc_c", [CTX, CTX], BF16)
    c_dfts_c = ein("dftsn_c", [CTX, CTX], BF16)
    c_invc = ein("invc", [128, 2, SEQ])
    c_invc_c = ein("invc_c", [128, 2, CTX])
    c_sel = ein("sel", [8, 8 * 128])
    c_ident = ein("ident", [128, 128])
    flg_in = ein("flg", [128, 2])
    c_cosq = ein("cosq", [96, HALF])
    c_sinq = ein("sinq", [96, HALF])
    out_d = kb.dram([D, HALF], F32, "outT", kind="ExternalOutput")

    xs = [x_in] + [kb.dram([D, TT], F32, f"xs{i}", kind=dkind) for i in range(1, 4)]
    uT = kb.dram([IN_W, TT], F32, "uT", kind=dkind)
    qT = kb.dram([4, 96, TT], BF16, "qT", kind=dkind)
    kT = kb.dram([4, 96, TT], BF16, "kT", kind=dkind)
    oT = kb.dram([D, TT], BF16, "oT", kind=dkind)
    fT = kb.dram([256, TT], BF16, "fT", kind=dkind)
    NBK = len(BLOCKS)
    xs_t = [[T(None, f"xs{i}_{b}") for b in range(NBK)] for i in range(4)]
    out_t = [T(None) for _ in range(NBK)]
    u_t = [[T(None) for _ in range(NBK)] for _ in range(14)]
    q_t = [[T(None) for _ in range(NBK)] for _ in range(4)]
    k_t = [[T(None) for _ in range(NBK)] for _ in range(4)]
    o_t = [[T(None) for _ in range(NBK)] for _ in range(8)]
    f_t = [T(None) for _ in range(NBK)]

    def u_tiles(r0, r1, bi):
        return [u_t[oc][bi] for oc in range(r0 // 128, (r1 - 1) // 128 + 1)]

    ind96 = kb.sb([96, 96]); ind128 = kb.sb([128, 128]); r96t = kb.sb([96, 96])
    onesD = kb.sb([128, 128]); ones256 = kb.sb([128, 128]); ones128 = kb.sb([128, 128]); onesf = kb.sb([128, 128])
    ident = kb.sb([128, 128]); sel = kb.sb([8, 8 * 128]); identb = kb.sb([128, 128], BF16)
    cc = kb.sb([128, 16]); sc = kb.sb([128, 8, 2]); flg = kb.sb([128, 2])
    pk = kb.sb([128, 80]); mod = kb.sb([128, 48, 2]); gm1 = kb.sb([128, 8, 2]); gm2 = kb.sb([128, 8, 2])
    NAR = 50560
    arena = kb.sb([128, NAR], F32, "arena")
    pst = [kb.ps([128, 512], F32, f"psb{i}") for i in range(8)]
    nps = RR(pst)

    class Carver:
        def __init__(self, lo, hi, dtype):
            self.lo, self.hi, self.dtype = lo, hi, dtype
            self.off = 0

        def reset(self):
            self.off = 0

        def get(self, shape, dtype=None):
            dtype = dtype or self.dtype
            esz = 2 if dtype == BF16 else 4
            osz = 2 if self.dtype == BF16 else 4
            n = int(np.prod(shape))
            byte0 = self.off * osz
            byte0 = (byte0 + 3) // 4 * 4
            nbytes = (n * esz + 3) // 4 * 4
            w0 = self.lo + byte0 // 4
            w1 = w0 + nbytes // 4
            assert w1 <= self.hi, (w1, self.hi)
            self.off = (byte0 + nbytes) // osz
            ap = arena.ap[:, w0:w1]
            if dtype == BF16:
                ap = ap.bitcast(BF16)[:, 0:n]
            if len(shape) == 2:
                ap = ap.rearrange("p (a b) -> p a b", a=shape[0])
            elif len(shape) == 3:
                ap = ap.rearrange("p (a b c) -> p a b c", a=shape[0], b=shape[1])
            elif len(shape) == 4:
                ap = ap.rearrange("p (a b c d) -> p a b c d", a=shape[0], b=shape[1], c=shape[2])
            return T(ap)

    cb = Carver(0, 33792, BF16)
    cf = Carver(33792, NAR, F32)
    cA = Carver(0, NAR, F32)

    for dst, src in ((ind96, c_ind96), (ind128, c_ind128), (r96t, c_r96t), (ident, c_ident), (sel, c_sel), (cc, cc_in), (flg, flg_in)):
        kb.dma("sp", dst[:], src[:], [], [dst])
    kb.op("pool", lambda e: e.memset(onesD[:], 1.0 / D), [], [onesD])
    kb.op("pool", lambda e: e.memset(ones256[:], 1.0 / 256), [], [ones256])
    kb.op("pool", lambda e: e.memset(ones128[:], 1.0 / 128), [], [ones128])
    kb.op("pool", lambda e: e.memset(onesf[:], 1.0), [], [onesf])
    kb.op("act", lambda e: e.activation(sc[:].rearrange("p k j -> p (k j)"), cc[:], AF.Silu), [cc], [sc])
    kb.op("dve", lambda e: e.tensor_copy(identb[:], ident[:]), [ident], [identb])

    evac_i = [0]

    def evac(out_ap, in_ap, reads, writes):
        evac_i[0] += 1
        if evac_i[0] % 2:
            kb.op("act", lambda e: e.copy(out_ap, in_ap), reads, writes)
        else:
            kb.op("dve", lambda e: e.tensor_copy(out_ap, in_ap), reads, writes)

    def adaln(l):
        cf.reset()
        wbs = RR([cf.get([8, 128]) for _ in range(3)])
        kb.dma("sp", pk[:], pk_in[l], [], [pk])
        wv = w_ada[l].rearrange("(k p) n -> p k n", p=128)
        for oc in range(48):
            wb = wbs()
            kb.dma("sp", wb[:], wv[:, :, oc * 128:(oc + 1) * 128], [], [wb])
            p = nps()
            kb.mm(p, p[:, 0:2], [(wb[:, k, :], sc[:, k, :]) for k in range(8)], [wb, sc])
            kb.op("dve", lambda e, p=p, oc=oc: e.tensor_scalar(mod[:, oc, :], p[:, 0:2], pk[:, oc:oc + 1], None, ALU.add),
                  [p, pk], [mod])
        for k in range(8):
            kb.op("dve", lambda e, k=k: e.tensor_scalar(gm1[:, k, :], mod[:, 8 + k, :], 1.0, pk[:, 48 + k:49 + k], ALU.add, ALU.mult),
                  [mod, pk], [gm1])
            kb.op("dve", lambda e, k=k: e.tensor_scalar(gm2[:, k, :], mod[:, 32 + k, :], 1.0, pk[:, 56 + k:57 + k], ALU.add, ALU.mult),
                  [mod, pk], [gm2])
        kb.barrier()

    def norm_mod(xb, sq, rs, w, j, gm, sh0, hb, hf=None):
        for k in range(8):
            kb.op("act", lambda e, k=k: e.activation(sq[:, k, :w], xb[:, k, :w], AF.Square), [xb], [sq])
        p = nps()
        kb.mm(p, p[:, :w], [(onesD[:, :], sq[:, k, :w]) for k in range(8)], [onesD, sq])
        kb.op("act", lambda e: e.activation(rs[:, :w], p[:, :w], AF.Ln, bias=EPS, scale=1.0), [p], [rs])
        kb.op("act", lambda e: e.activation(rs[:, :w], rs[:, :w], AF.Exp, scale=-0.5), [rs], [rs])
        for k in range(8):
            kb.op("dve", lambda e, k=k: e.tensor_tensor(sq[:, k, :w], xb[:, k, :w], rs[:, :w], ALU.mult), [xb, rs], [sq])
            kb.op("act", lambda e, k=k: e.activation(hb[:, k, :w], sq[:, k, :w], AF.Identity,
                                                      bias=mod[:, sh0 + k, j:j + 1], scale=gm[:, k, j:j + 1]),
                  [sq, mod, gm], [hb])
            if hf is not None:
                kb.op("act", lambda e, k=k: e.activation(hf[:, k, :w], sq[:, k, :w], AF.Identity,
                                                          bias=mod[:, sh0 + k, j:j + 1], scale=gm[:, k, j:j + 1]),
                      [sq, mod, gm], [hf])

    def inproj(l, xi, vna, nblocks):
        cb.off = vna_end
        cf.reset()
        win = cb.get([8, IN_W])
        hbs = RR([cb.get([8, 512]) for _ in range(2)])
        xbs = RR([cf.get([8, 512]) for _ in range(2)])
        sq = cf.get([8, 512])
        rs = cf.get([512])
        ubs = RR([cf.get([512]) for _ in range(2)])
        kb.dma("pool", win[:], w_in[l].rearrange("(k p) n -> p k n", p=128), [], [win])
        for bi in range(nblocks):
            c0, w, isc = BLOCKS[bi]
            j = 1 if isc else 0
            xb = xbs()
            kb.dma("sp", xb[:, :, :w], xs[xi].rearrange("(k p) t -> p k t", p=128)[:, :, c0:c0 + w], [xs_t[xi][bi]], [xb])
            hb = hbs()
            norm_mod(xb, sq, rs, w, j, gm1, 0, hb)
            for oc in range(14):
                r0 = oc * 128
                m = min(128, IN_W - r0)
                p = nps()
                kb.mm(p, p[:m, :w], [(win[:, k, r0:r0 + m], hb[:, k, :w]) for k in range(8)], [win, hb])
                ub = ubs()
                evac(ub[:m, :w], p[:m, :w], [p], [ub])
                kb.dma("sp", uT[r0:r0 + m, c0:c0 + w], ub[:m, :w], [ub], [u_t[oc][bi]])
            for s in range(w // 128):
                p = nps()
                kb.mm(p, p[:, 0:256], [(hb[:, k, s * 128:(s + 1) * 128], win[:, k, 1440:1696]) for k in range(8)], [win, hb])
                ch = c0 // 128 + s
                evac(vna[:, ch, :, 0:64], p[:, 0:256].rearrange("p (h d) -> p h d", h=4), [p], [vna])
        kb.barrier()

    cb.reset()
    vna = cb.get([66, 4, 65])
    vml = cb.get([66, 4, 65])
    ckvn = cb.get([TT])
    vna_end = cb.off
    kb.op("pool", lambda e: e.memset(vna[:], 1.0), [], [vna])
    kb.op("pool", lambda e: e.memset(vml[:], 1.0), [], [vml])

    nqT = kb.dram([256, TT], BF16, "nqT", kind=dkind)
    nkT = kb.dram([256, TT], BF16, "nkT", kind=dkind)
    nq_t = [T(None) for _ in range(NBK)]
    nk_t = [T(None) for _ in range(NBK)]
    psS = RR(pst[0:4])
    psO = RR(pst[4:6])
    psM = RR(pst[6:8])

    def uview(r0, r1):
        return uT[r0:r1, :].rearrange("(k p) t -> p k t", p=128)

    def norm96_rope(raw, wk, w, gcol, c0, rope, dst_ap, dst_t, cosb=None, sinb=None):
        sq, rs, qn, t1 = wk
        kb.op("act", lambda e: e.activation(sq[:96, :w], raw[:96, :w], AF.Square), [raw], [sq])
        p = nps()
        kb.mm(p, p[:96, :w], [(ind96[:, :], sq[:96, :w])], [ind96, sq])
        kb.op("act", lambda e: e.activation(rs[:96, :w], p[:96, :w], AF.Ln, bias=EPS, scale=1.0), [p], [rs])
        kb.op("act", lambda e: e.activation(rs[:96, :w], rs[:96, :w], AF.Exp, scale=-0.5), [rs], [rs])
        ob = obs()
        if rope:
            kb.op("dve", lambda e: e.scalar_tensor_tensor(qn[:96, :w], raw[:96, :w], gcol, rs[:96, :w], ALU.mult, ALU.mult),
                  [raw, rs, pk], [qn])
            p2 = nps()
            kb.mm(p2, p2[:96, :w], [(r96t[:, :], qn[:96, :w])], [r96t, qn])
            kb.op("pool", lambda e: e.tensor_tensor(t1[:96, :w], qn[:96, :w], cosb[:96, :w], ALU.mult), [qn, cosb], [t1])
            kb.op("dve", lambda e: e.tensor_tensor(sq[:96, :w], p2[:96, :w], sinb[:96, :w], ALU.mult), [p2, sinb], [sq])
            kb.op("dve", lambda e: e.tensor_tensor(ob[:96, :w], t1[:96, :w], sq[:96, :w], ALU.add), [t1, sq], [ob])
        else:
            kb.op("dve", lambda e: e.scalar_tensor_tensor(ob[:96, :w], raw[:96, :w], gcol, rs[:96, :w], ALU.mult, ALU.mult),
                  [raw, rs, pk], [ob])
        kb.dma("pool", dst_ap, ob[:96, :w], [ob], [dst_t])

    obs = None

    def qkprep(l, nblocks, split=False):
        nonlocal obs
        cb.off = vna_end
        cf.reset()
        wuq = cb.get([2, 384])
        wukv = cb.get([4, 128])
        cqn = cb.get([2, 512])
        nob = RR([cb.get([2, 512]) for _ in range(2)])
        obs = RR([cb.get([512]) for _ in range(4)])
        wks = RR([[cf.get([512]) for _ in range(4)] for _ in range(2)])
        raws = RR([cf.get([512]) for _ in range(3)])
        xfs = RR([cf.get([2, 512]) for _ in range(2)])
        sq2s = RR([cf.get([2, 512]) for _ in range(2)])
        rs2s = RR([cf.get([512]) for _ in range(2)])
        cbs = RR([cf.get([512]) for _ in range(2)])
        sbs_ = RR([cf.get([512]) for _ in range(2)])
        kb.dma("pool", wuq[:], w_uq[l].rearrange("(k p) n -> p k n", p=128), [], [wuq])
        kb.dma("pool", wukv[:], w_ukv[l].rearrange("p (h n) -> p h n", h=4), [], [wukv])
        def qpath(bi, c0, w, isc, cosb, sinb, blend):
            xf, sq2, rs2 = xfs(), sq2s(), rs2s()
            kb.dma("sp", xf[:, :, :w], uview(0, 256)[:, :, c0:c0 + w], u_tiles(0, 256, bi), [xf])
            if blend:
                xf2 = xfs()
                kb.dma("sp", xf2[:, :, :w], uview(0, 256)[:, :, HALF + c0:HALF + c0 + w], u_tiles(0, 256, bi + 8), [xf2])
                kb.op("dve", lambda e: e.tensor_scalar(xf[:, :, :w], xf[:, :, :w], flg[:, 0:1], None, ALU.mult), [xf, flg], [xf])
                kb.op("dve", lambda e: e.scalar_tensor_tensor(xf[:, :, :w], xf2[:, :, :w], flg[:, 1:2], xf[:, :, :w], ALU.mult, ALU.add),
                      [xf2, xf, flg], [xf])
            for k in range(2):
                kb.op("act", lambda e: e.activation(sq2[:, k, :w], xf[:, k, :w], AF.Square), [xf], [sq2])
            p = nps()
            kb.mm(p, p[:, :w], [(ones256[:, :], sq2[:, k, :w]) for k in range(2)], [ones256, sq2])
            kb.op("act", lambda e: e.activation(rs2[:, :w], p[:, :w], AF.Ln, bias=EPS, scale=1.0), [p], [rs2])
            kb.op("act", lambda e: e.activation(rs2[:, :w], rs2[:, :w], AF.Exp, scale=-0.5), [rs2], [rs2])
            for k in range(2):
                kb.op("dve", lambda e: e.scalar_tensor_tensor(cqn[:, k, :w], xf[:, k, :w], pk[:, 64 + k:65 + k], rs2[:, :w],
                                                              ALU.mult, ALU.mult), [xf, rs2, pk], [cqn])
            for h in range(4):
                p = nps()
                kb.mm(p, p[:96, :w], [(wuq[:, k, h * 96:(h + 1) * 96], cqn[:, k, :w]) for k in range(2)], [wuq, cqn])
                raw = raws()
                kb.op("act", lambda e: e.copy(raw[:96, :w], p[:96, :w]), [p], [raw])
                norm96_rope(raw, wks(), w, pk[:96, 67:68], c0, not isc, qT[h, :, c0:c0 + w], q_t[h][bi], cosb, sinb)

        for bi in range(nblocks):
            c0, w, isc = BLOCKS[bi]
            cosb = sinb = None
            if not isc:
                cosb, sinb = cbs(), sbs_()
                kb.dma("sp", cosb[:96, :w], c_cos[:, c0:c0 + w], [], [cosb])
                kb.dma("sp", sinb[:96, :w], c_sin[:, c0:c0 + w], [], [sinb])
            if not split:
                qpath(bi, c0, w, isc, cosb, sinb, False)
            xf, sq2, rs2 = xfs(), sq2s(), rs2s()
            kb.dma("sp", xf[:, 0, :w], uT[256:384, c0:c0 + w], u_tiles(256, 384, bi), [xf])
            kb.op("act", lambda e: e.activation(sq2[:, 0, :w], xf[:, 0, :w], AF.Square), [xf], [sq2])
            p = nps()
            kb.mm(p, p[:, :w], [(ones128[:, :], sq2[:, 0, :w])], [ones128, sq2])
            kb.op("act", lambda e, p=p: e.activation(rs2[:, :w], p[:, :w], AF.Ln, bias=EPS, scale=1.0), [p], [rs2])
            kb.op("act", lambda e: e.activation(rs2[:, :w], rs2[:, :w], AF.Exp, scale=-0.5), [rs2], [rs2])
            kb.op("dve", lambda e: e.scalar_tensor_tensor(ckvn[:, c0:c0 + w], xf[:, 0, :w], pk[:, 66:67], rs2[:, :w],
                                                           ALU.mult, ALU.mult), [xf, rs2, pk], [ckvn])
            for h in range(4):
                p = nps()
                kb.mm(p, p[:64, :w], [(wukv[:, h, 0:64], ckvn[:, c0:c0 + w])], [wukv, ckvn])
                raw = raws()
                kb.op("act", lambda e, p=p, raw=raw: e.copy(raw[:64, :w], p[:64, :w]), [p], [raw])
                kb.dma("sp", raw[64:96, :w], uT[384:416, c0:c0 + w], u_tiles(384, 416, bi), [raw])
                norm96_rope(raw, wks(), w, pk[:96, 68:69], c0, not isc, kT[h, :, c0:c0 + w], k_t[h][bi], cosb, sinb)
            for s in range(w // 128):
                p = nps()
                kb.mm(p, p[:, 0:256].rearrange("p (h d) -> p h d", h=4),
                      [(ckvn[:, c0 + s * 128:c0 + (s + 1) * 128], wukv[:, :, 64:128])], [wukv, ckvn])
                evac(vml[:, c0 // 128 + s, :, 0:64], p[:, 0:256].rearrange("p (h d) -> p h d", h=4), [p], [vml])
            for (r0, gc, dstT, dtl) in ((928, 71, nqT, nq_t), (1184, 72, nkT, nk_t)):
                xf, sq2, rs2 = xfs(), sq2s(), rs2s()
                kb.dma("sp", xf[:, :, :w], uview(r0, r0 + 256)[:, :, c0:c0 + w], u_tiles(r0, r0 + 256, bi), [xf])
                no = nob()
                for k in range(2):
                    kb.op("act", lambda e, k=k: e.activation(sq2[:, k, :w], xf[:, k, :w], AF.Square), [xf], [sq2])
                    p = nps()
                    kb.mm(p, p[:, :w], [(ind128[:, :], sq2[:, k, :w])], [ind128, sq2])
                    kb.op("act", lambda e, p=p: e.activation(rs2[:, :w], p[:, :w], AF.Ln, bias=EPS, scale=1.0), [p], [rs2])
                    kb.op("act", lambda e: e.activation(rs2[:, :w], rs2[:, :w], AF.Exp, scale=-0.5), [rs2], [rs2])
                    kb.op("dve", lambda e, k=k, no=no, gc=gc: e.scalar_tensor_tensor(no[:, k, :w], xf[:, k, :w], pk[:, gc:gc + 1],
                                                                                      rs2[:, :w], ALU.mult, ALU.mult),
                          [xf, rs2, pk], [no])
                kb.dma("pool", dstT.rearrange("(k p) t -> p k t", p=128)[:, :, c0:c0 + w], no[:, :, :w], [no], [dtl[bi]])
        if split:
            for bj in range(8):
                cosb, sinb = cbs(), sbs_()
                kb.dma("sp", cosb[:96, :], c_cosq[:, bj * 512:(bj + 1) * 512], [], [cosb])
                kb.dma("sp", sinb[:96, :], c_sinq[:, bj * 512:(bj + 1) * 512], [], [sinb])
                qpath(bj, bj * 512, 512, False, cosb, sinb, True)
        kb.barrier()

    fin = {}

    def attn_finish(pO, w, dst_ap, src_view, dst_ts):
        osb = fin["osb"]()
        rec = fin["rec"]
        ob = fin["ob"]()
        kb.op("act", lambda e: e.copy(osb[:65, :w], pO[:65, :w]), [pO], [osb])
        kb.op("dve", lambda e: e.reciprocal(rec[64:65, :w], osb[64:65, :w]), [osb], [rec])
        pB = psM()
        kb.mm(pB, pB[:64, :w], [(onesf[64:65, 0:64], rec[64:65, :w])], [onesf, rec])
        kb.op("dve", lambda e: e.tensor_tensor(ob[:64, :w], osb[:64, :w], pB[:64, :w], ALU.mult), [osb, pB], [ob])
        kb.dma("pool", dst_ap, src_view(ob), [ob], dst_ts)

    def mla_attn(l, nblocks, split=False):
        cb.off = vna_end
        cf.reset()
        kbuf = cb.get([TT])
        qbs = RR([cb.get([512]) for _ in range(2)])
        pts = RR([cb.get([512]) for _ in range(4)])
        fin["ob"] = RR([cb.get([512]) for _ in range(2)])
        fin["osb"] = RR([cf.get([512]) for _ in range(2)])
        fin["rec"] = cf.get([512])
        for h in range(4):
            kb.dma("sp", kbuf[:96, :], kT[h], [k_t[h][b] for b in range(NBK)], [kbuf])
            for bi in range(8 if split else nblocks):
                c0, w, isc = BLOCKS[bi]
                qb = qbs()
                kb.dma("sp", qb[:96, :w], qT[h, :, c0:c0 + w], [q_t[h][bi]], [qb])
                chunks = [64, 65] if isc else list(range(66))
                pO = psO()
                n = len(chunks)
                LA = 3
                ptl = {}
                for i in range(n + LA):
                    if i < n:
                        kc = chunks[i]
                        pS = psS()
                        kb.mm(pS, pS[:, :w], [(kbuf[:96, kc * 128:(kc + 1) * 128], qb[:96, :w])], [kbuf, qb])
                        pt = pts()
                        kb.op("act", lambda e: e.activation(pt[:, :w], pS[:, :w], AF.Exp, scale=MLA_SCALE), [pS], [pt])
                        ptl[i] = pt
                    if i >= LA:
                        ii = i - LA
                        pt = ptl.pop(ii)
                        kb.mm(pO, pO[:65, :w], [(vml[:, chunks[ii], h, :], pt[:, :w])], [vml, pt], start=(ii == 0), stop=(ii == n - 1))
                attn_finish(pO, w, oT[h * 64:(h + 1) * 64, c0:c0 + w], lambda ob, w=w: ob[:64, :w], [o_t[h // 2][bi]])
        kb.barrier()

    def na_attn(l, nblocks):
        cb.off = vna_end
        cf.reset()
        fin["osb"] = RR([cf.get([512]) for _ in range(2)])
        fin["rec"] = cf.get([512])
        bias = cb.get([21, 4, 128])
        fin["ob"] = RR([cb.get([512]) for _ in range(2)])
        kctx = cb.get([2, 256])
        qns = RR([cb.get([2, 128]) for _ in range(2)])
        kns = RR([cb.get([2, 640]) for _ in range(2)])
        pts = RR([cb.get([896]) for _ in range(3)])
        nqv = nqT.rearrange("(k p) t -> p k t", p=128)
        nkv = nkT.rearrange("(k p) t -> p k t", p=128)
        kb.dma("pool", bias[:].rearrange("p a h q -> p (a h q)"), nab[l], [], [bias])
        kb.op("dve", lambda e: e.tensor_scalar(bias[:].rearrange("p a h q -> p (a h q)"), bias[:].rearrange("p a h q -> p (a h q)"),
                                               1.0 / NA_SCALE, None, ALU.mult), [bias], [bias])
        kb.dma("sp", kctx[:], nkv[:, :, SEQ:TT], [nk_t[16]], [kctx])
        npb = 64 + (0 if nblocks == 16 else 2)
        pbinfo = {}

        def setup_pb(pb):
            if pb < 64:
                q0 = pb * 128
                if pb < 2:
                    loc, base = [0, 1, 2, 3], 5 + 4 * pb
                elif pb >= 62:
                    loc, base = [60, 61, 62, 63], 13 + 4 * (pb - 62)
                else:
                    loc, base = list(range(pb - 2, pb + 3)), 0
            else:
                q0 = SEQ + (pb - 64) * 128
                loc, base = [], 0
            nl = len(loc)
            qn = qns()
            kb.dma("sp", qn[:], nqv[:, :, q0:q0 + 128], [nq_t[q0 // 512]], [qn])
            kn = kns()
            if nl:
                kc0 = loc[0] * 128
                kb.dma("sp", kn[:, :, 0:nl * 128], nkv[:, :, kc0:kc0 + nl * 128],
                       [nk_t[b_] for b_ in range(kc0 // 512, (kc0 + nl * 128 - 1) // 512 + 1)], [kn])
            pbinfo[pb] = dict(q0=q0, loc=loc, base=base, nl=nl, qn=qn, kn=kn, pO=psO())

        def stage_a(pb, h):
            if h == 0:
                setup_pb(pb)
            I = pbinfo[pb]
            loc, base, nl, qn, kn = I["loc"], I["base"], I["nl"], I["qn"], I["kn"]
            nA = min(nl, 4)
            k_, po = h // 2, (h % 2) * 64
            pA = psS()
            for i in range(nA):
                kb.mm(pA, pA[:, i * 128:(i + 1) * 128], [(kn[po:po + 64, k_, i * 128:(i + 1) * 128], qn[po:po + 64, k_, :]),
                                                         (identb[:, :], bias[:, base + i, h, :])], [kn, qn, identb, bias])
            pB = psS()
            if nl == 5:
                kb.mm(pB, pB[:, 0:128], [(kn[po:po + 64, k_, 512:640], qn[po:po + 64, k_, :]),
                                         (identb[:, :], bias[:, base + 4, h, :])], [kn, qn, identb, bias])
            for c in range(2):
                kb.mm(pB, pB[:, 128 + c * 128:256 + c * 128], [(kctx[po:po + 64, k_, c * 128:(c + 1) * 128], qn[po:po + 64, k_, :])],
                      [kctx, qn])
            pt = pts()
            if nA:
                kb.op("act", lambda e: e.activation(pt[:, 0:nA * 128], pA[:, 0:nA * 128], AF.Exp, scale=NA_SCALE), [pA], [pt])
            if nl == 5:
                kb.op("act", lambda e: e.activation(pt[:, 512:896], pB[:, 0:384], AF.Exp, scale=NA_SCALE), [pB], [pt])
            else:
                kb.op("act", lambda e: e.activation(pt[:, 640:896], pB[:, 128:384], AF.Exp, scale=NA_SCALE), [pB], [pt])
            I[("pt", h)] = pt

        def stage_b(pb, h):
            I = pbinfo[pb]
            loc, nl, pO, q0 = I["loc"], I["nl"], I["pO"], I["q0"]
            pt = I.pop(("pt", h))
            for i in range(nl):
                kb.mm(pO, pO[:65, h * 128:(h + 1) * 128], [(vna[:, loc[i], h, :], pt[:, i * 128:(i + 1) * 128])], [vna, pt],
                      start=(i == 0), stop=False)
            kb.mm(pO, pO[:65, h * 128:(h + 1) * 128], [(vna[:, 64, h, :], pt[:, 640:768])], [vna, pt], start=(nl == 0), stop=False)
            kb.mm(pO, pO[:65, h * 128:(h + 1) * 128], [(vna[:, 65, h, :], pt[:, 768:896])], [vna, pt], start=False, stop=True)
            if h == 3:
                bi = q0 // 512
                attn_finish(pO, 512, oT[768:1024, q0:q0 + 128].rearrange("(h d) q -> d h q", h=4),
                            lambda ob: ob[:64, 0:512].rearrange("d (h q) -> d h q", h=4), [o_t[6][bi], o_t[7][bi]])
                del pbinfo[pb]

        items = [(pb, h) for pb in range(npb) for h in range(4)]
        LA = 1
        for idx in range(len(items) + LA):
            if idx < len(items):
                stage_a(*items[idx])
            if idx >= LA:
                stage_b(*items[idx - LA])
        kb.barrier()

    def fourier(l, nblocks):
        cb.reset()
        cf.reset()
        AB = cb.get([64, 512])
        xf = cb.get([2, SEQ])
        wcs = cb.get([2, 512])
        tcs = RR([cb.get([1024]) for _ in range(3)])
        tss = RR([cb.get([1024]) for _ in range(3)])
        fbs = RR([cb.get([512]) for _ in range(4)])
        kb.dma("pool", wcs[:], c_wcs.rearrange("(k p) n -> p k n", p=128), [], [wcs])
        kb.dma("pool", xf[:], uview(416, 672)[:, :, 0:SEQ], [t for b in range(16) for t in u_tiles(416, 672, b)], [xf])
        for tc in range(64):
            p = nps()
            kb.mm(p, p[:, :], [(xf[:, k, tc * 128:(tc + 1) * 128], wcs[:, k, :]) for k in range(2)], [xf, wcs])
            evac(AB[:, tc, :], p[:, :], [p], [AB])
        sc_l = float(1.0 / np.sqrt(SEQ * 64.0))
        for kb2 in range(8):
            pF = [[pst[0], pst[1]], [pst[2], pst[3]]]
            for lc in range(64):
                tcn = tcs()
                tsn = tss()
                kb.dma("sp", tcn[:], c_dftc[lc * 128:(lc + 1) * 128, kb2 * 1024:(kb2 + 1) * 1024], [], [tcn])
                kb.dma("pool", tsn[:], c_dfts[lc * 128:(lc + 1) * 128, kb2 * 1024:(kb2 + 1) * 1024], [], [tsn])
                for fc in range(2):
                    for hf in range(2):
                        kb.mm(pF[fc][hf], pF[fc][hf][:, :], [(AB[:, lc, fc * 128:(fc + 1) * 128], tcn[:, hf * 512:(hf + 1) * 512]),
                                                             (AB[:, lc, 256 + fc * 128:256 + (fc + 1) * 128], tsn[:, hf * 512:(hf + 1) * 512])],
                              [AB, tcn, tsn], start=(lc == 0), stop=(lc == 63))
            for fc in range(2):
                for hf in range(2):
                    kbk = kb2 * 2 + hf
                    fb = fbs()
                    kb.op("act", lambda e: e.activation(fb[:], pF[fc][hf][:, :], AF.Copy, scale=sc_l), [pF[fc][hf]], [fb])
                    kb.dma("sp", fT[fc * 128:(fc + 1) * 128, kbk * 512:(kbk + 1) * 512], fb[:], [fb], [f_t[kbk]])
        if nblocks == 17:
            xc = cb.get([2, CTX])
            ABc = cb.get([2, 512])
            tcc = cb.get([2, CTX])
            tsc = cb.get([2, CTX])
            kb.dma("pool", xc[:], uview(416, 672)[:, :, SEQ:TT], u_tiles(416, 672, 16), [xc])
            kb.dma("sp", tcc[:], c_dftc_c.rearrange("(k p) n -> p k n", p=128), [], [tcc])
            kb.dma("sp", tsc[:], c_dfts_c.rearrange("(k p) n -> p k n", p=128), [], [tsc])
            for tc in range(2):
                p = nps()
                kb.mm(p, p[:, :], [(xc[:, k, tc * 128:(tc + 1) * 128], wcs[:, k, :]) for k in range(2)], [xc, wcs])
                evac(ABc[:, tc, :], p[:, :], [p], [ABc])
            sc_c = float(1.0 / np.sqrt(CTX * 64.0))
            for fc in range(2):
                p = nps()
                prs = []
                for lc in range(2):
                    prs.append((ABc[:, lc, fc * 128:(fc + 1) * 128], tcc[:, lc, :]))
                    prs.append((ABc[:, lc, 256 + fc * 128:256 + (fc + 1) * 128], tsc[:, lc, :]))
                kb.mm(p, p[:, 0:CTX], prs, [ABc, tcc, tsc])
                fb = fbs()
                kb.op("act", lambda e, fb=fb, p=p: e.activation(fb[:, 0:CTX], p[:, 0:CTX], AF.Copy, scale=sc_c), [p], [fb])
                kb.dma("sp", fT[fc * 128:(fc + 1) * 128, SEQ:TT], fb[:, 0:CTX], [fb], [f_t[16]])
        wf = cb.get([2, 256])
        fls = RR([cb.get([2, 512]) for _ in range(2)])
        kb.dma("pool", wf[:], w_fo[l].rearrange("(k p) n -> p k n", p=128), [], [wf])
        for bi in range(nblocks):
            c0, w, isc = BLOCKS[bi]
            fl = fls()
            kb.dma("sp", fl[:, :, :w], fT.rearrange("(k p) t -> p k t", p=128)[:, :, c0:c0 + w], [f_t[bi]], [fl])
            for oc in range(2):
                p = nps()
                kb.mm(p, p[:, :w], [(wf[:, k, oc * 128:(oc + 1) * 128], fl[:, k, :w]) for k in range(2)], [wf, fl])
                fb = fbs()
                evac(fb[:, :w], p[:, :w], [p], [fb])
                kb.dma("sp", oT[256 + oc * 128:256 + (oc + 1) * 128, c0:c0 + w], fb[:, :w], [fb], [o_t[2 + oc][bi]])
        kb.barrier()

    def pool(l, nblocks):
        cb.reset()
        cf.reset()
        wpl = cb.get([2, 128])
        pds = RR([cb.get([2, 512]) for _ in range(2)])
        pos = RR([cb.get([512]) for _ in range(2)])
        xps = RR([cf.get([2, 528]) for _ in range(2)])
        s2 = cf.get([2, 528]); s4 = cf.get([2, 528]); s8 = cf.get([2, 528]); s16 = cf.get([2, 528])
        ivs = RR([cf.get([2, 512]) for _ in range(2)])
        tmp = cf.get([2, 512])
        kb.dma("pool", wpl[:], w_pl[l].rearrange("c p n -> p c n"), [], [wpl])
        uv = uview(672, 928)
        for bi in range(nblocks):
            c0, w, isc = BLOCKS[bi]
            xp = xps()
            lo = c0 - 8 if (not isc and bi > 0) else c0
            hi = c0 + w + 8 if (not isc and bi < 15) else c0 + w
            kb.op("pool", lambda e, xp=xp: e.memset(xp[:], 0.0), [], [xp])
            deps = []
            for b in range(max(0, bi - 1), min(NBK, bi + 2)):
                deps += u_tiles(672, 928, b)
            kb.dma("sp", xp[:, :, 8 + lo - c0:8 + hi - c0], uv[:, :, lo:hi], deps, [xp])
            n = w + 16
            kb.op("dve", lambda e, xp=xp: e.tensor_tensor(s2[:, :, 1:n], xp[:, :, 0:n - 1], xp[:, :, 1:n], ALU.add), [xp], [s2])
            kb.op("dve", lambda e: e.tensor_tensor(s4[:, :, 2:n - 1], s2[:, :, 1:n - 2], s2[:, :, 3:n], ALU.add), [s2], [s4])
            kb.op("dve", lambda e: e.tensor_tensor(s8[:, :, 4:n - 3], s4[:, :, 2:n - 5], s4[:, :, 6:n - 1], ALU.add), [s4], [s8])
            kb.op("dve", lambda e: e.tensor_tensor(s16[:, :, 8:n - 7], s8[:, :, 4:n - 11], s8[:, :, 12:n - 3], ALU.add), [s8], [s16])
            iv = ivs()
            if isc:
                kb.dma("sp", iv[:, :, :w], c_invc_c[:, :, :], [], [iv])
            else:
                kb.dma("sp", iv[:, :, :w], c_invc[:, :, c0:c0 + w], [], [iv])
            pd = pds()
            for g, sg in enumerate((s2, s4, s8, s16)):
                ch, po = g // 2, (g % 2) * 64
                kb.op("dve", lambda e, sg=sg, ch=ch, po=po, iv=iv: e.tensor_tensor(tmp[po:po + 64, ch, :w], sg[po:po + 64, ch, 8:8 + w],
                                                                                   iv[po:po + 64, ch, :w], ALU.mult), [sg, iv], [tmp])
                kb.op("dve", lambda e, ch=ch, po=po, xp=xp, pd=pd: e.tensor_tensor(pd[po:po + 64, ch, :w], tmp[po:po + 64, ch, :w],
                                                                                   xp[po:po + 64, ch, 8:8 + w], ALU.subtract),
                      [tmp, xp], [pd])
            for ch in range(2):
                p = nps()
                kb.mm(p, p[:, :w], [(wpl[:, ch, :], pd[:, ch, :w])], [wpl, pd])
                po_ = pos()
                kb.op("act", lambda e, p=p, po_=po_, ch=ch: e.activation(po_[:, :w], p[:, :w], AF.Copy, scale=pk[:, 69 + ch:70 + ch]),
                      [p, pk], [po_])
                kb.dma("sp", oT[512 + ch * 128:512 + (ch + 1) * 128, c0:c0 + w], po_[:, :w], [po_], [o_t[4 + ch][bi]])
        kb.barrier()

    def outproj(l, xi, nblocks, split=False):
        cb.reset()
        cf.reset()
        wo = cb.get([8, D])
        obs_ = RR([cb.get([8, 512]) for _ in range(2)])
        xbs = RR([cf.get([8, 512]) for _ in range(2)])
        kb.dma("pool", wo[:], w_out[l].rearrange("(k p) n -> p k n", p=128), [], [wo])
        if split:
            ob2s = RR([cb.get([6, 512]) for _ in range(2)])
            xb2s = RR([cf.get([8, 512]) for _ in range(2)])
        oTv = oT.rearrange("(k p) t -> p k t", p=128)
        for bi in range(8 if split else nblocks):
            c0, w, isc = BLOCKS[bi]
            j = 1 if isc else 0
            ob = obs_()
            kb.dma("sp", ob[:, :, :w], oTv[:, :, c0:c0 + w], [o_t[r][bi] for r in range(8)], [ob])
            xb = xbs()
            kb.dma("sp", xb[:, :, :w], xs[xi].rearrange("(k p) t -> p k t", p=128)[:, :, c0:c0 + w], [xs_t[xi][bi]], [xb])
            if split:
                ob2 = ob2s()
                kb.dma("pool", ob2[:, :, :], oTv[:, 2:8, HALF + c0:HALF + c0 + w], [o_t[r][bi + 8] for r in range(2, 8)], [ob2])
                kb.op("dve", lambda e: e.tensor_scalar(ob[:, 2:8, :], ob[:, 2:8, :], flg[:, 0:1], None, ALU.mult), [ob, flg], [ob])
                kb.op("dve", lambda e: e.scalar_tensor_tensor(ob[:, 2:8, :], ob2[:, :, :], flg[:, 1:2], ob[:, 2:8, :], ALU.mult, ALU.add),
                      [ob2, ob, flg], [ob])
                xb2 = xb2s()
                kb.dma("sp", xb2[:, :, :], xs[xi].rearrange("(k p) t -> p k t", p=128)[:, :, HALF + c0:HALF + c0 + w],
                       [xs_t[xi][bi + 8]], [xb2])
                kb.op("dve", lambda e: e.tensor_scalar(xb[:, :, :], xb[:, :, :], flg[:, 0:1], None, ALU.mult), [xb, flg], [xb])
                kb.op("dve", lambda e: e.scalar_tensor_tensor(xb[:, :, :], xb2[:, :, :], flg[:, 1:2], xb[:, :, :], ALU.mult, ALU.add),
                      [xb2, xb, flg], [xb])
            for oc in range(8):
                p = nps()
                kb.mm(p, p[:, :w], [(wo[:, k, oc * 128:(oc + 1) * 128], ob[:, k, :w]) for k in range(8)], [wo, ob])
                kb.op("dve", lambda e, p=p, xb=xb, oc=oc, j=j: e.scalar_tensor_tensor(xb[:, oc, :w], p[:, :w], mod[:, 16 + oc, j:j + 1],
                                                                                       xb[:, oc, :w], ALU.mult, ALU.add),
                      [p, mod, xb], [xb])
            kb.dma("sp", xs[xi + 1].rearrange("(k p) t -> p k t", p=128)[:, :, c0:c0 + w], xb[:, :, :w], [xb], [xs_t[xi + 1][bi]])
        kb.barrier()

    def ffn(l, xi, nblocks, moe, last):
        F = D_FFE if moe else D_FF
        NF = F // 128
        NEX = NE if moe else 1
        groups = [list(range(i, min(i + 7, NF))) for i in range(0, NF, 7)]
        SBW = 2048
        cA.reset()
        R0 = cA.get([8, SBW], F32)
        r0f = R0.ap.rearrange("p k t -> p (k t)")
        xb = r0f[:, 0:4096].rearrange("p (k t) -> p k t", k=8)
        sq = r0f[:, 4096:8192].rearrange("p (k t) -> p k t", k=8)
        yv = R0.ap
        h2 = cA.get([8, SBW], BF16)
        actq = cA.get([7, SBW], BF16)
        WB = RR([(cA.get([4096], BF16), cA.get([4096], BF16)) for _ in range(3)])
        g1s = RR([cA.get([512], BF16) for _ in range(3)])
        gbc = cA.get([SBW], BF16)
        gT = cA.get([SBW], F32)
        rs = cA.get([512], F32)
        xsl = RR([cA.get([512], F32) for _ in range(4 if last else 2)])
        lg = cA.get([8], F32); eq = cA.get([8], F32); l2 = cA.get([8], F32); ex = cA.get([8], F32)
        msk = cA.get([8], F32); gt = cA.get([8], F32); sm = cA.get([8], F32)
        wr = cA.get([8, 8], F32)
        if moe:
            kb.dma("sp", wr[:], w_rt[0].rearrange("(k p) e -> p k e", p=128), [], [wr])
        split = False
        ntok_override = HALF if last else SEQ
        ntok = ntok_override
        xBv = r0f[:, 8192:12288].rearrange("p (k t) -> p k t", k=8)
        sbs = [(i * SBW, [(q * 512, 512) for q in range(SBW // 512)], False) for i in range(ntok // SBW)]
        if nblocks == 17:
            sbs.append((SEQ, [(0, 256)], True))
        xin = xs[xi].rearrange("(k p) t -> p k t", p=128)
        for (s0, subs, isc) in sbs:
            j = 1 if isc else 0
            for (so, w) in subs:
                c0 = s0 + so
                bi = c0 // 512
                kb.dma("sp", xb[:, :, :w], xin[:, :, c0:c0 + w], [xs_t[xi][bi]], [R0])
                if split:
                    kb.dma("sp", xBv[:, :, :w], xin[:, :, HALF + c0:HALF + c0 + w], [xs_t[xi][bi + 8]], [R0])
                    kb.op("dve", lambda e: e.tensor_scalar(r0f[:, 0:4096], r0f[:, 0:4096], flg[:, 0:1], None, ALU.mult), [R0, flg], [R0])
                    kb.op("dve", lambda e: e.scalar_tensor_tensor(r0f[:, 0:4096], r0f[:, 8192:12288], flg[:, 1:2], r0f[:, 0:4096],
                                                                  ALU.mult, ALU.add), [R0, flg], [R0])
                for k in range(8):
                    kb.op("act", lambda e: e.activation(sq[:, k, :w], xb[:, k, :w], AF.Square), [R0], [R0])
                p = nps()
                kb.mm(p, p[:, :w], [(onesD[:, :], sq[:, k, :w]) for k in range(8)], [onesD, R0])
                kb.op("act", lambda e: e.activation(rs[:, :w], p[:, :w], AF.Ln, bias=EPS, scale=1.0), [p], [rs])
                kb.op("act", lambda e: e.activation(rs[:, :w], rs[:, :w], AF.Exp, scale=-0.5), [rs], [rs])
                for k in range(8):
                    kb.op("dve", lambda e: e.tensor_tensor(sq[:, k, :w], xb[:, k, :w], rs[:, :w], ALU.mult), [R0, rs], [R0])
                    kb.op("act", lambda e: e.activation(sq[:, k, :w], sq[:, k, :w], AF.Identity,
                                                        bias=mod[:, 24 + k, j:j + 1], scale=gm2[:, k, j:j + 1]),
                          [R0, mod, gm2], [R0])
                    kb.op("dve", lambda e: e.tensor_copy(h2[:, k, so:so + w], sq[:, k, :w]), [R0], [h2])
                if moe:
                    for t4 in range(w // 128):
                        p = psM()
                        kb.mm(p, p[:, 0:8], [(sq[:, k, t4 * 128:(t4 + 1) * 128], wr[:, k, :]) for k in range(8)], [R0, wr])
                        kb.op("act", lambda e: e.copy(lg[:, :], p[:, 0:8]), [p], [lg])
                        kb.op("dve", lambda e: e.tensor_reduce(sm[:, 0:1], lg[:, :], mybir.AxisListType.X, ALU.max), [lg], [sm])
                        kb.op("dve", lambda e: e.tensor_scalar(eq[:, :], lg[:, :], sm[:, 0:1], None, ALU.is_equal), [lg, sm], [eq])
                        kb.op("dve", lambda e: e.scalar_tensor_tensor(l2[:, :], eq[:, :], -1e30, lg[:, :], ALU.mult, ALU.add), [eq, lg], [l2])
                        kb.op("dve", lambda e: e.tensor_reduce(sm[:, 1:2], l2[:, :], mybir.AxisListType.X, ALU.max), [l2], [sm])
                        kb.op("dve", lambda e: e.tensor_scalar(msk[:, :], lg[:, :], sm[:, 1:2], None, ALU.is_ge), [lg, sm], [msk])
                        kb.op("dve", lambda e: e.tensor_scalar(sm[:, 2:3], sm[:, 0:1], -1.0, None, ALU.mult), [sm], [sm])
                        kb.op("act", lambda e: e.activation(ex[:, :], lg[:, :], AF.Exp, bias=sm[:, 2:3], scale=1.0), [lg, sm], [ex])
                        kb.op("act", lambda e: e.activation(sm[:, 3:4], sm[:, 1:2], AF.Exp, bias=sm[:, 2:3], scale=1.0), [sm], [sm])
                        kb.op("dve", lambda e: e.tensor_scalar(sm[:, 4:5], sm[:, 3:4], 1.0, None, ALU.add), [sm], [sm])
                        kb.op("dve", lambda e: e.reciprocal(sm[:, 5:6], sm[:, 4:5]), [sm], [sm])
                        kb.op("dve", lambda e: e.scalar_tensor_tensor(gt[:, :], ex[:, :], sm[:, 5:6], msk[:, :], ALU.mult, ALU.mult),
                              [ex, sm, msk], [gt])
                        p2 = psM()
                        kb.mm(p2, p2[:8, 0:128], [(gt[:, :], ident[:, :])], [gt, ident])
                        o = so + t4 * 128
                        kb.op("act", lambda e: e.copy(gT[:8, o:o + 128], p2[:8, 0:128]), [p2], [gT])
            kb.barrier()
            first_y = True
            for ex_i in range(NEX):
                if moe:
                    w1v = w1m[0, ex_i].rearrange("(k p) f -> p k f", p=128)
                    w3v = w3m[0, ex_i].rearrange("(k p) f -> p k f", p=128)
                    w2v = w2m[0, ex_i].rearrange("(c p) n -> p c n", p=128)
                    for (so, w) in subs:
                        p = psM()
                        kb.mm(p, p[:, :w], [(sel[:, ex_i * 128:(ex_i + 1) * 128], gT[:8, so:so + w])], [sel, gT])
                        kb.op("act", lambda e: e.copy(gbc[:, so:so + w], p[:, :w]), [p], [gbc])
                else:
                    w1v = w1d[0].rearrange("(k p) f -> p k f", p=128)
                    w3v = w3d[0].rearrange("(k p) f -> p k f", p=128)
                    w2v = w2d[0].rearrange("(c p) n -> p c n", p=128)
                for grp in groups:
                    f0 = grp[0] * 128
                    ncols = len(grp) * 128
                    for cc0 in range(0, ncols, 512):
                        ncc = min(512, ncols - cc0)
                        b1, b3 = WB()
                        w1t = b1.ap.rearrange("p (k f) -> p k f", k=8)
                        w3t = b3.ap.rearrange("p (k f) -> p k f", k=8)
                        kb.dma("pool", w1t[:, :, :ncc], w1v[:, :, f0 + cc0:f0 + cc0 + ncc], [], [b1])
                        kb.dma("pool", w3t[:, :, :ncc], w3v[:, :, f0 + cc0:f0 + cc0 + ncc], [], [b3])
                        for fl in range(ncc // 128):
                            fcl = cc0 // 128 + fl
                            for (so, w) in subs:
                                pa = nps()
                                kb.mm(pa, pa[:, :w], [(w1t[:, k, fl * 128:(fl + 1) * 128], h2[:, k, so:so + w]) for k in range(8)], [b1, h2])
                                pb_ = nps()
                                kb.mm(pb_, pb_[:, :w], [(w3t[:, k, fl * 128:(fl + 1) * 128], h2[:, k, so:so + w]) for k in range(8)], [b3, h2])
                                g1 = g1s()
                                kb.op("act", lambda e: e.activation(g1[:, :w], pa[:, :w], AF.Silu), [pa], [g1])
                                if moe:
                                    kb.op("dve", lambda e: e.tensor_tensor(g1[:, :w], g1[:, :w], gbc[:, so:so + w], ALU.mult), [g1, gbc], [g1])
                                kb.op("dve", lambda e: e.tensor_tensor(actq[:, fcl, so:so + w], g1[:, :w], pb_[:, :w], ALU.mult),
                                      [g1, pb_], [actq])
                    b1, b3 = WB()
                    w2t = b1.ap[:, 0:4096]
                    ng = len(grp)
                    w2pair = T(None)
                    w2a = b1.ap.rearrange("p (c n) -> p c n", n=1024)
                    w2b = b3.ap.rearrange("p (c n) -> p c n", n=1024)
                    na = min(ng, 4)
                    kb.dma("pool", w2a[:, 0:na, :], w2v[:, grp[0]:grp[0] + na, :], [], [b1])
                    if ng > 4:
                        kb.dma("pool", w2b[:, 0:ng - 4, :], w2v[:, grp[0] + 4:grp[0] + ng, :], [], [b3])
                    for oc in range(8):
                        for (so, w) in subs:
                            py = nps()
                            prs = []
                            for c in range(ng):
                                src = w2a if c < 4 else w2b
                                prs.append((src[:, c % 4, oc * 128:(oc + 1) * 128], actq[:, c, so:so + w]))
                            kb.mm(py, py[:, :w], prs, [b1, b3, actq])
                            if first_y:
                                kb.op("act", lambda e: e.copy(yv[:, oc, so:so + w], py[:, :w]), [py], [R0])
                            else:
                                kb.op("dve", lambda e: e.tensor_tensor(yv[:, oc, so:so + w], yv[:, oc, so:so + w], py[:, :w], ALU.add),
                                      [py, R0], [R0])
                    first_y = False
            for (so, w) in subs:
                c0 = s0 + so
                bi = c0 // 512
                for oc in range(8):
                    xl = xsl()
                    kb.dma("sp", xl[:, :w], xs[xi][oc * 128:(oc + 1) * 128, c0:c0 + w], [xs_t[xi][bi]], [xl])
                    if split:
                        xl2 = xsl()
                        kb.dma("sp", xl2[:, :w], xs[xi][oc * 128:(oc + 1) * 128, HALF + c0:HALF + c0 + w], [xs_t[xi][bi + 8]], [xl2])
                        kb.op("dve", lambda e: e.tensor_scalar(xl[:, :w], xl[:, :w], flg[:, 0:1], None, ALU.mult), [xl, flg], [xl])
                        kb.op("dve", lambda e: e.scalar_tensor_tensor(xl[:, :w], xl2[:, :w], flg[:, 1:2], xl[:, :w], ALU.mult, ALU.add),
                              [xl2, xl, flg], [xl])
                    kb.op("dve", lambda e: e.scalar_tensor_tensor(xl[:, :w], yv[:, oc, so:so + w], mod[:, 40 + oc, j:j + 1], xl[:, :w],
                                                                  ALU.mult, ALU.add), [R0, mod, xl], [xl])
                    if last:
                        kb.dma("sp", out_d[oc * 128:(oc + 1) * 128, c0:c0 + w], xl[:, :w], [xl], [out_t[bi]])
                    else:
                        kb.dma("sp", xs[xi + 1][oc * 128:(oc + 1) * 128, c0:c0 + w], xl[:, :w], [xl], [xs_t[xi + 1][bi]])
            kb.barrier()

    def finish():
        kb.barrier()
        kb.emit()
        return nc

    for l in range(DEPTH):
        last = l == DEPTH - 1
        nblocks = 16 if last else 17
        xi = 2 * l
        adaln(l)
        kb.op("pool", lambda e: e.memset(vna[:], 1.0), [], [vna])
        kb.op("pool", lambda e: e.memset(vml[:], 1.0), [], [vml])
        inproj(l, xi, vna, 17)
        if stop_after == f"inproj{l}":
            return finish()
        qkprep(l, 17, split=last)
        if stop_after == f"qkprep{l}":
            return finish()
        mla_attn(l, nblocks, split=last)
        if stop_after == f"mla{l}":
            return finish()
        na_attn(l, nblocks)
        if stop_after == f"na{l}":
            return finish()
        fourier(l, nblocks)
        pool(l, nblocks)
        if stop_after == f"mix{l}":
            return finish()
        outproj(l, xi, nblocks, split=last)
        if stop_after == f"outproj{l}":
            return finish()
        ffn(l, xi + 1, nblocks, moe=(l % 2 == 1), last=last)
        if stop_after == f"ffn{l}":
            return finish()
    return finish()


_CONSTS = None


def _dft_tables():
    c = {}
    k = np.arange(SEQ, dtype=np.int64)
    dc = np.empty((SEQ, SEQ), ml_dtypes.bfloat16)
    ds = np.empty((SEQ, SEQ), ml_dtypes.bfloat16)
    for r0 in range(0, SEQ, 1024):
        kl = (np.outer(k[r0:r0 + 1024], k) % SEQ).astype(np.float32) * np.float32(2 * np.pi / SEQ)
        dc[r0:r0 + 1024] = np.cos(kl).astype(ml_dtypes.bfloat16)
        ds[r0:r0 + 1024] = (-np.sin(kl)).astype(ml_dtypes.bfloat16)
    c["dftc"] = dc
    c["dftsn"] = ds
    return c


def prep_inputs(inp, cores):
    global _CONSTS
    if _CONSTS is None:
        _CONSTS = _const_tables()
    inp = {k: np.asarray(v) for k, v in inp.items()}
    shared = dict(_CONSTS)
    shared["sel"] = shared["sel"].reshape(8, 8 * 128)
    shared["pk"] = np.stack([_pack_params(inp, l) for l in range(DEPTH)], 0)
    for name in ("w_ada", "w_in", "w_out", "w_uq", "w_ukv", "w_fourier", "w1_dense", "w3_dense", "w2_dense",
                 "w_router", "w1_moe", "w3_moe", "w2_moe"):
        shared[name] = np.ascontiguousarray(inp[name], dtype=np.float32)
    bd = np.zeros((DEPTH, 2, 128, 128), np.float32)
    for l in range(DEPTH):
        for g in range(4):
            o = (g % 2) * 64
            bd[l, g // 2, o:o + 64, o:o + 64] = inp["w_pool"][l, g]
    shared["w_poolbd"] = bd
    shared["na_bias"] = np.stack([_na_bias_tiles(inp["na_rpb"][l]).reshape(128, -1) for l in range(DEPTH)], 0)
    maps = []
    for b in cores:
        m = dict(shared)
        m["xT"] = np.ascontiguousarray(np.concatenate([inp["x"][b].T, inp["ctx"][b].T], axis=1), dtype=np.float32)
        cc = np.zeros((128, 16), np.float32)
        cc[:, 0::2] = inp["c"][b].reshape(8, 128).T
        cc[:, 1::2] = inp["c_ctx"].reshape(8, 128).T
        m["cc"] = cc
        maps.append(m)
    return maps


def prep_inputs8(inp):
    base = prep_inputs(inp, list(range(4)))
    maps = []
    for i in range(NCORES):
        m = dict(base[i // 2])
        f = np.zeros((128, 2), np.float32)
        f[:, i % 2] = 1.0
        m["flg"] = f
        o = (i % 2) * HALF
        m["cosq"] = np.ascontiguousarray(base[i // 2]["cos96"][:, o:o + HALF])
        m["sinq"] = np.ascontiguousarray(base[i // 2]["sin96"][:, o:o + HALF])
        maps.append(m)
    return maps


_NC = None


def kernel(**inputs):
    global _NC
    if _NC is None:
        _NC = build_program()
    maps = prep_inputs8(inputs)
    res = run_bass_kernel_spmd(_NC, maps, core_ids=list(range(NCORES)))
    halves = [np.ascontiguousarray(r["outT"].T) for r in res.results]
    out = np.stack([np.concatenate([halves[2 * b], halves[2 * b + 1]], axis=0) for b in range(4)], 0)
    return out.astype(np.float32)
```
